# Optimizing a Trainium2 kernel written in Bass

```python
import jax, jax.numpy as jnp
from jax import lax
import numpy as np

D_MODEL = 1024
BATCH = 8
SEQ = 4096
DEPTH = 4

N_MIXERS = 3
N_RET_LAYERS = (DEPTH + 2) // 3
N_CONV_LAYERS = (DEPTH + 1) // 3
N_ATTN_LAYERS = DEPTH // 3

RET_QK_DIM = 256
RET_HEADS = D_MODEL // RET_QK_DIM
RET_V_DIM = 2 * RET_QK_DIM
RET_CHUNK = 128
RET_ROPE_BASE = 10000.0
RET_QK_WIDTH = RET_HEADS * RET_QK_DIM
RET_V_WIDTH = RET_HEADS * RET_V_DIM
RET_IN_WIDTH = 2 * RET_QK_WIDTH + 2 * RET_V_WIDTH

CONV_WIDTH = 31
CONV_PAD = (CONV_WIDTH - 1) // 2

ATTN_HEAD_DIM = 64
ATTN_HEADS = D_MODEL // ATTN_HEAD_DIM
DILATION_GROUPS = ((128, 1), (512, 4), (2048, 16))
N_ATTN_GROUPS = len(DILATION_GROUPS)
ATTN_IN_WIDTH = N_ATTN_GROUPS * 3 * ATTN_HEADS * ATTN_HEAD_DIM
ROPE_THETA = 500000.0
ROT_DIM = ATTN_HEAD_DIM // 4

FFN_HIDDEN = -(-8 * D_MODEL // (3 * 256)) * 256

RMS_EPS = 1e-6
LN_EPS = 1e-5
MASK_VALUE = -1e30

kernel_name = "hybrid_retention_conformer_dilated_encoder"


def _rmsnorm(x, g):
    xf = x.astype(jnp.float32)
    y = xf * lax.rsqrt(jnp.mean(xf * xf, axis=-1, keepdims=True) + RMS_EPS)
    return (y * g.astype(jnp.float32)).astype(x.dtype)


def _layernorm(x, g, b):
    xf = x.astype(jnp.float32)
    mu = jnp.mean(xf, axis=-1, keepdims=True)
    var = jnp.mean(jnp.square(xf - mu), axis=-1, keepdims=True)
    y = (xf - mu) * lax.rsqrt(var + LN_EPS)
    return (y * g.astype(jnp.float32) + b.astype(jnp.float32)).astype(x.dtype)


def _rotate(x, cos, sin):
    half = x.shape[-1] // 2
    x1, x2 = x[..., :half], x[..., half:]
    return jnp.concatenate([x1 * cos - x2 * sin, x2 * cos + x1 * sin], axis=-1).astype(x.dtype)


def _retention_dir(q, k, v, log1m_gamma, strict):
    B, H, S, dk = q.shape
    dv = v.shape[-1]
    C = RET_CHUNK
    n = S // C
    dt = v.dtype
    gamma = 1.0 - jnp.exp(log1m_gamma.astype(jnp.float32))
    lg = jnp.log(gamma)
    idx = jnp.arange(C)
    diff = idx[:, None] - idx[None, :]
    live = (diff > 0) if strict else (diff >= 0)
    dmat = jnp.where(live, jnp.exp(lg[:, None, None] * jnp.where(live, diff, 0)), 0.0)
    xi = jnp.exp(lg[:, None] * (idx + 1)[None, :]).astype(dt)
    zeta = jnp.exp(lg[:, None] * (C - 1 - idx)[None, :]).astype(dt)
    chunk_decay = jnp.exp(lg * C).astype(dt)

    qc = q.reshape(B, H, n, C, dk)
    kc = k.reshape(B, H, n, C, dk)
    vc = v.reshape(B, H, n, C, dv)
    scores = jnp.einsum('bhnid,bhnjd->bhnij', qc, kc) * dmat[None, :, None].astype(dt)
    intra = jnp.einsum('bhnij,bhnje->bhnie', scores, vc)

    def step(state, inp):
        qb, kb, vb = inp
        cross = jnp.einsum('bhcd,bhde->bhce', qb * xi[None, :, :, None], state)
        state = state * chunk_decay[None, :, None, None] + jnp.einsum(
            'bhcd,bhce->bhde', kb * zeta[None, :, :, None], vb)
        return state, cross

    state0 = jnp.zeros((B, H, dk, dv), dt)
    _, cross = lax.scan(step, state0, (jnp.moveaxis(qc, 2, 0), jnp.moveaxis(kc, 2, 0), jnp.moveaxis(vc, 2, 0)))
    out = intra + jnp.moveaxis(cross, 0, 2)
    return out.reshape(B, H, S, dv)


def _retention_mixer(h, w_in, log1m_decay, w_out):
    B, S, _ = h.shape
    proj = h @ w_in
    q = proj[..., :RET_QK_WIDTH]
    k = proj[..., RET_QK_WIDTH:2 * RET_QK_WIDTH]
    v = proj[..., 2 * RET_QK_WIDTH:2 * RET_QK_WIDTH + RET_V_WIDTH]
    g = proj[..., 2 * RET_QK_WIDTH + RET_V_WIDTH:]
    q = q.reshape(B, S, RET_HEADS, RET_QK_DIM).transpose(0, 2, 1, 3)
    k = k.reshape(B, S, RET_HEADS, RET_QK_DIM).transpose(0, 2, 1, 3)
    v = v.reshape(B, S, RET_HEADS, RET_V_DIM).transpose(0, 2, 1, 3)
    inv = 1.0 / (RET_ROPE_BASE ** jnp.linspace(0.0, 1.0, RET_QK_DIM // 2, dtype=jnp.float32))
    ang = jnp.arange(S, dtype=jnp.float32)[:, None] * inv[None, :]
    cos, sin = jnp.cos(ang), jnp.sin(ang)
    q = _rotate(q, cos, sin)
    k = (_rotate(k, cos, sin) * (RET_QK_DIM ** -0.5)).astype(h.dtype)
    fwd = _retention_dir(q, k, v, log1m_decay[0], strict=False)
    bwd = _retention_dir(q[:, :, ::-1], k[:, :, ::-1], v[:, :, ::-1], log1m_decay[1], strict=True)[:, :, ::-1]
    o = (fwd + bwd).astype(jnp.float32)
    o = o * lax.rsqrt(jnp.mean(o * o, axis=-1, keepdims=True) + RMS_EPS)
    o = o.transpose(0, 2, 1, 3).reshape(B, S, RET_V_WIDTH).astype(h.dtype)
    return (jax.nn.silu(g) * o) @ w_out


def _conv_mixer(h, w_in, b_in, w_dw, b_dw, ln_g, ln_b, w_out, b_out):
    a = h @ w_in + b_in
    u = a[..., :D_MODEL] * jax.nn.sigmoid(a[..., D_MODEL:])
    u = lax.conv_general_dilated(
        u, w_dw[:, None, :].astype(u.dtype), window_strides=(1,), padding=[(CONV_PAD, CONV_PAD)],
        dimension_numbers=('NWC', 'WIO', 'NWC'), feature_group_count=D_MODEL) + b_dw
    u = jax.nn.silu(_layernorm(u, ln_g, ln_b))
    return u @ w_out + b_out


def _dilated_band_attention(q, k, v, dil, radius):
    B, S, H, Dh = q.shape
    L = S // dil
    W = radius
    nb = -(-L // W)
    Lp = nb * W

    def strided(t):
        return t.reshape(B, L, dil, H, Dh).transpose(0, 2, 3, 1, 4)

    qs, ks, vs = strided(q), strided(k), strided(v)
    qs = jnp.pad(qs, ((0, 0), (0, 0), (0, 0), (0, Lp - L), (0, 0)))
    kvpad = ((0, 0), (0, 0), (0, 0), (W, Lp - L + W), (0, 0))
    kb = jnp.pad(ks, kvpad).reshape(B, dil, H, nb + 2, W, Dh)
    vb = jnp.pad(vs, kvpad).reshape(B, dil, H, nb + 2, W, Dh)
    qb = qs.reshape(B, dil, H, nb, W, Dh)
    kwin = jnp.concatenate([kb[:, :, :, 0:nb], kb[:, :, :, 1:nb + 1], kb[:, :, :, 2:nb + 2]], axis=4)
    vwin = jnp.concatenate([vb[:, :, :, 0:nb], vb[:, :, :, 1:nb + 1], vb[:, :, :, 2:nb + 2]], axis=4)
    blk = jnp.arange(nb)[:, None]
    qpos = blk * W + jnp.arange(W)[None, :]
    kpos = (blk - 1) * W + jnp.arange(3 * W)[None, :]
    mask = ((jnp.abs(qpos[:, :, None] - kpos[:, None, :]) <= radius)
            & (kpos[:, None, :] >= 0) & (kpos[:, None, :] < L))
    s = jnp.einsum('bghnqd,bghnkd->bghnqk', qb, kwin).astype(jnp.float32) * (Dh ** -0.5)
    s = jnp.where(mask, s, MASK_VALUE)
    mx = jnp.max(s, axis=-1, keepdims=True)
    p = jnp.exp(s - mx)
    den = jnp.sum(p, axis=-1)
    o = jnp.einsum('bghnqk,bghnkd->bghnqd', p.astype(v.dtype), vwin).astype(jnp.float32) / den[..., None]
    lse = mx[..., 0] + jnp.log(den)
    o = o.reshape(B, dil, H, Lp, Dh)[:, :, :, :L].transpose(0, 3, 1, 2, 4).reshape(B, S, H, Dh)
    lse = lse.reshape(B, dil, H, Lp)[:, :, :, :L].transpose(0, 3, 1, 2).reshape(B, S, H)
    return o, lse


def _dilated_attention_mixer(h, w_in, w_out):
    B, S, _ = h.shape
    proj = (h @ w_in).reshape(B, S, N_ATTN_GROUPS, 3, ATTN_HEADS, ATTN_HEAD_DIM)
    inv = ROPE_THETA ** (-jnp.arange(0, ROT_DIM, 2, dtype=jnp.float32) / ROT_DIM)
    ang = jnp.arange(S, dtype=jnp.float32)[:, None] * inv[None, :]
    cos, sin = jnp.cos(ang)[:, None, :], jnp.sin(ang)[:, None, :]

    def prope(t):
        return jnp.concatenate([_rotate(t[..., :ROT_DIM], cos, sin), t[..., ROT_DIM:]], axis=-1)

    outs, lses = [], []
    for g, (window, dil) in enumerate(DILATION_GROUPS):
        q = prope(proj[:, :, g, 0])
        k = prope(proj[:, :, g, 1])
        v = proj[:, :, g, 2]
        o, l = _dilated_band_attention(q, k, v, dil, window // (2 * dil))
        outs.append(o)
        lses.append(l)
    wts = jax.nn.softmax(jnp.stack(lses, 0), axis=0)
    o = jnp.sum(wts[..., None] * jnp.stack(outs, 0), axis=0).astype(h.dtype)
    return o.reshape(B, S, ATTN_HEADS * ATTN_HEAD_DIM) @ w_out


def _swiglu(h, w_in, w_out):
    a = h @ w_in
    return (jax.nn.silu(a[..., :FFN_HIDDEN]) * a[..., FFN_HIDDEN:]) @ w_out


def setup_inputs(seed: int = 0) -> dict:
    key = jax.random.key(seed)
    ks = jax.random.split(key, 20)
    f32 = jnp.float32
    nrm = lambda k, shape, scale: jax.random.normal(k, shape, f32) * scale
    base_decay = -(5.0 + jnp.arange(RET_HEADS, dtype=f32)) * np.float32(np.log(2.0))
    return {
        "x": nrm(ks[0], (BATCH, SEQ, D_MODEL), 1.0),
        "norm_w": 1.0 + nrm(ks[1], (DEPTH, 4, D_MODEL), 0.05),
        "ffn_w_in": nrm(ks[2], (DEPTH, D_MODEL, 2 * FFN_HIDDEN), D_MODEL ** -0.5),
        "ffn_w_out": nrm(ks[3], (DEPTH, FFN_HIDDEN, D_MODEL), FFN_HIDDEN ** -0.5),
        "ret_w_in": nrm(ks[4], (N_RET_LAYERS, D_MODEL, RET_IN_WIDTH), D_MODEL ** -0.5),
        "ret_log1m_decay": base_decay[None, None, :] + nrm(ks[5], (N_RET_LAYERS, 2, RET_HEADS), 0.1),
        "ret_w_out": nrm(ks[6], (N_RET_LAYERS, RET_V_WIDTH, D_MODEL), RET_V_WIDTH ** -0.5),
        "conv_w_in": nrm(ks[7], (N_CONV_LAYERS, D_MODEL, 2 * D_MODEL), D_MODEL ** -0.5),
        "conv_b_in": nrm(ks[8], (N_CONV_LAYERS, 2 * D_MODEL), 0.02),
        "conv_w_dw": nrm(ks[9], (N_CONV_LAYERS, CONV_WIDTH, D_MODEL), CONV_WIDTH ** -0.5),
        "conv_b_dw": nrm(ks[10], (N_CONV_LAYERS, D_MODEL), 0.02),
        "conv_ln_g": 1.0 + nrm(ks[11], (N_CONV_LAYERS, D_MODEL), 0.05),
        "conv_ln_b": nrm(ks[12], (N_CONV_LAYERS, D_MODEL), 0.02),
        "conv_w_out": nrm(ks[13], (N_CONV_LAYERS, D_MODEL, D_MODEL), D_MODEL ** -0.5),
        "conv_b_out": nrm(ks[14], (N_CONV_LAYERS, D_MODEL), 0.02),
        "attn_w_in": nrm(ks[15], (N_ATTN_LAYERS, D_MODEL, ATTN_IN_WIDTH), D_MODEL ** -0.5),
        "attn_w_out": nrm(ks[16], (N_ATTN_LAYERS, ATTN_HEADS * ATTN_HEAD_DIM, D_MODEL),
                          (ATTN_HEADS * ATTN_HEAD_DIM) ** -0.5),
    }


def reference(x, norm_w, ffn_w_in, ffn_w_out, ret_w_in, ret_log1m_decay, ret_w_out,
              conv_w_in, conv_b_in, conv_w_dw, conv_b_dw, conv_ln_g, conv_ln_b,
              conv_w_out, conv_b_out, attn_w_in, attn_w_out):
    for i in range(DEPTH):
        kind, j = i % N_MIXERS, i // N_MIXERS
        hn = _rmsnorm(x, norm_w[i, 0])
        if kind == 0:
            m = _retention_mixer(hn, ret_w_in[j], ret_log1m_decay[j], ret_w_out[j])
        elif kind == 1:
            m = _conv_mixer(hn, conv_w_in[j], conv_b_in[j], conv_w_dw[j], conv_b_dw[j],
                            conv_ln_g[j], conv_ln_b[j], conv_w_out[j], conv_b_out[j])
        else:
            m = _dilated_attention_mixer(hn, attn_w_in[j], attn_w_out[j])
        x = x + _rmsnorm(m, norm_w[i, 1])
        f = _swiglu(_rmsnorm(x, norm_w[i, 2]), ffn_w_in[i], ffn_w_out[i])
        x = x + _rmsnorm(f, norm_w[i, 3])
    return x
```

```python
import math
import numpy as np
import concourse.bass as bass
import concourse.mybir as mybir
from concourse.bass_utils import run_bass_kernel_spmd

F32 = mybir.dt.float32
BF16 = mybir.dt.bfloat16
ALU = mybir.AluOpType
AF = mybir.ActivationFunctionType
AX = mybir.AxisListType

D = 1024
NCH = D // 128
FFN_H = 2816
RMS_EPS = 1e-6
LN_EPS = 1e-5

PE, ACT, DVE, POOL, SP = "pe", "act", "dve", "pool", "sp"
ENGS = (PE, ACT, DVE, POOL, SP)


class Sched:
    def __init__(self, nc):
        self.nc = nc
        self.ops = {e: [] for e in ENGS}
        self.cnt = {e: 0 for e in ENGS}
        self.sems = {e: nc.alloc_semaphore(f"s_{e}") for e in ENGS}
        self.dsems = {}
        self.seen = {}
        self.res = {}
        self.floor = {}
        self.n_ops = 0

    def _res(self, r):
        st = self.res.get(r)
        if st is None:
            arena = r[0] if isinstance(r, tuple) else None
            st = {"w": None, "r": dict(self.floor.get(arena, {}))}
            self.res[r] = st
        return st

    def end_phase(self, arena):
        fl = dict(self.floor.get(arena, {}))
        dead = []
        for r, st in self.res.items():
            if isinstance(r, tuple) and r[0] == arena:
                if st["w"] is not None:
                    k, v = st["w"]
                    fl[k] = max(fl.get(k, 0), v)
                for k, v in st["r"].items():
                    fl[k] = max(fl.get(k, 0), v)
                dead.append(r)
        for r in dead:
            del self.res[r]
        self.floor[arena] = fl

    def _semh(self, key):
        if key in self.sems:
            return self.sems[key]
        return self.dsems[key][0]

    def _collect(self, eng, reads, writes, is_dma):
        deps = {}

        def add(tok, same_ok):
            if tok is None:
                return
            k, v = tok
            if same_ok and (not is_dma) and k == eng:
                return
            if v > deps.get(k, 0):
                deps[k] = v

        for r in reads:
            st = self._res(r)
            add(st["w"], False)
            if isinstance(r, tuple) and r[0] == "ps":
                for k, v in st["r"].items():
                    add((k, v), True)
        for w in writes:
            st = self._res(w)
            add(st["w"], True)
            for k, v in st["r"].items():
                add((k, v), True)
        waits = []
        for k, v in deps.items():
            if self.seen.get((eng, k), 0) < v:
                self.seen[(eng, k)] = v
                waits.append((k, v))
        return waits

    def _commit(self, tok, reads, writes):
        k, v = tok
        for r in reads:
            st = self._res(r)
            if v > st["r"].get(k, 0):
                st["r"][k] = v
        for w in writes:
            st = self._res(w)
            st["w"] = tok
            st["r"] = {}

    def op(self, eng, fn, reads=(), writes=(), inc=True):
        waits = self._collect(eng, reads, writes, False)
        if inc:
            self.cnt[eng] += 1
            tok = (eng, self.cnt[eng])
        else:
            tok = (eng, self.cnt[eng] + 1)
        self.ops[eng].append((waits, fn, (eng, 1) if inc else None))
        self._commit(tok, reads, writes)
        self.n_ops += 1
        return tok

    def dma(self, q, key, fn, reads=(), writes=()):
        if key not in self.dsems:
            self.dsems[key] = [self.nc.alloc_semaphore(f"d_{len(self.dsems)}"), 0]
        ds = self.dsems[key]
        waits = self._collect(q, reads, writes, True)
        if ds[1] > 0 and self.seen.get((q, key), 0) < ds[1]:
            self.seen[(q, key)] = ds[1]
            waits.append((key, ds[1]))
        ds[1] += 16
        tok = (key, ds[1])
        self.ops[q].append((waits, fn, (key, 16)))
        self._commit(tok, reads, writes)
        self.n_ops += 1
        return tok

    def final_wait(self, eng, toks):
        waits = []
        for k, v in toks:
            if self.seen.get((eng, k), 0) < v:
                self.seen[(eng, k)] = v
                waits.append((k, v))
        self.ops[eng].append((waits, None, None))

    def emit(self):
        nc = self.nc
        with nc.Block() as block:
            def run(e, name):
                for waits, fn, inc in self.ops[name]:
                    if fn is None:
                        for k, v in waits:
                            e.wait_ge(self._semh(k), v)
                        continue
                    for k, v in waits[:-1]:
                        e.wait_ge(self._semh(k), v)
                    ins = fn(e)
                    if waits:
                        ins._wait_ge(self._semh(waits[-1][0]), waits[-1][1])
                    if inc is not None:
                        ins.then_inc(self._semh(inc[0]), inc[1])

            @block.tensor
            def _(e):
                run(e, PE)

            @block.scalar
            def _(e):
                run(e, ACT)

            @block.vector
            def _(e):
                run(e, DVE)

            @block.gpsimd
            def _(e):
                run(e, POOL)

            @block.sync
            def _(e):
                run(e, SP)


class Arena:
    def __init__(self, nc, name, base, size):
        self.nc, self.name, self.base, self.size = nc, name, base, size
        self.off = 0
        self.uid = 0

    def reset(self):
        self.off = 0

    def tile(self, shape, dtype, tag):
        nbytes = int(np.prod(shape[1:])) * (2 if dtype == BF16 else 4)
        nbytes = (nbytes + 63) // 64 * 64
        assert self.off + nbytes <= self.size, (self.name, tag, self.off, nbytes, self.size)
        self.uid += 1
        t = self.nc.alloc_sbuf_tensor_at(f"{self.name}_{tag}_{self.uid}", list(shape), dtype,
                                         offset=self.base + self.off)
        self.off += nbytes
        return t


SB_BASE = 16512
C_SIZE = 12 * 1024
M_SIZE = 229344 - SB_BASE - C_SIZE


class Prog:
    def __init__(self, T):
        self.T = T
        nc = bass.Bass("TRN2", target_bir_lowering=False)
        self.nc = nc
        self.S = Sched(nc)
        self.C = Arena(nc, "C", SB_BASE, C_SIZE)
        self.W = self.A = Arena(nc, "M", SB_BASE + C_SIZE, M_SIZE)
        self.ps = [nc.alloc_psum_tensor(f"ps{i}", [128, 512], F32) for i in range(8)]
        self.psb = [t.bitcast(BF16) for t in self.ps]
        self.ps_rr = 0
        self.dram = {}
        self.out_toks = []

    def din(self, name, shape, dtype=F32):
        t = self.nc.dram_tensor(name, list(shape), dtype, kind="ExternalInput")
        self.dram[name] = t
        return t

    def dout(self, name, shape, dtype=F32):
        t = self.nc.dram_tensor(name, list(shape), dtype, kind="ExternalOutput")
        self.dram[name] = t
        return t

    def dscr(self, name, shape, dtype):
        t = self.nc.dram_tensor(name, list(shape), dtype)
        self.dram[name] = t
        return t

    def bank(self):
        i = self.ps_rr
        self.ps_rr = (self.ps_rr + 1) % 8
        return i

    def new_phase(self):
        S = self.S
        S.end_phase("W")
        S.end_phase("A")
        fl = dict(S.floor.get("W", {}))
        for k, v in S.floor.get("A", {}).items():
            fl[k] = max(fl.get(k, 0), v)
        S.floor["W"] = dict(fl)
        S.floor["A"] = dict(fl)
        self.A.reset()

    def consts(self, gT, cst):
        S, C = self.S, self.C
        identf = C.tile([128, 128], F32, "identf")
        S.dma(SP, "c_id", lambda e: e.dma_start(out=identf[:], in_=cst[:, 0:128]), writes=["c_identf"])
        self.ones = C.tile([128, 128], BF16, "ones")
        self.ident = C.tile([128, 128], BF16, "ident")
        self.gs = C.tile([128, 16 * NCH], F32, "gs")
        onesf = C.tile([128, 128], F32, "onesf")
        self.epsD = C.tile([128, 2], F32, "epsD")
        S.op(DVE, lambda e: e.memset(self.epsD[:, 0:1], float(D * RMS_EPS)), writes=["c_eps"])
        S.op(DVE, lambda e: e.memset(onesf[:], 1.0), writes=["c_onesf"])
        S.op(DVE, lambda e: e.tensor_copy(out=self.ones[:], in_=onesf[:]), reads=["c_onesf"], writes=["c_ones"])
        S.op(DVE, lambda e: e.tensor_copy(out=self.ident[:], in_=identf[:]), reads=["c_identf"], writes=["c_ident"])
        S.dma(SP, "c_g", lambda e: e.dma_start(out=self.gs[:], in_=gT[:, :]), writes=["c_g0"])
        S.op(DVE, lambda e: e.tensor_scalar(out=self.gs[:], in0=self.gs[:], scalar1=float(math.sqrt(D)),
                                            scalar2=None, op0=ALU.mult), reads=["c_g0"], writes=["c_gs"])

    def load_w(self, dst, w_ap, res, key):
        KC = dst.shape[1]
        for c in range(KC):
            self.S.dma(POOL, (key, c % 4), lambda e, c=c: e.dma_start(out=dst[:, c, :], in_=w_ap[c * 128:(c + 1) * 128, :]),
                       writes=[res + (c,)])

    def rstd_of(self, src, srcres, sq, sqres, r, rres, TT):
        S = self.S
        S.op(ACT, lambda e: e.activation(out=sq[:, :, :TT], in_=src[:, :, :TT], func=AF.Square),
             reads=srcres, writes=[sqres])
        b = self.bank()
        for c in range(NCH):
            S.op(PE, lambda e, c=c: e.matmul(self.ps[b][:, :TT], lhsT=self.ones[:], rhs=sq[:, c, :TT],
                                             start=(c == 0), stop=(c == NCH - 1)),
                 reads=["c_ones", sqres], writes=[("ps", b)], inc=(c == NCH - 1))
        S.op(ACT, lambda e: e.activation(out=r[:, :TT], in_=self.ps[b][:, :TT], func=AF.Sqrt, bias=self.epsD[:, 0:1]),
             reads=[("ps", b), "c_eps"], writes=[rres])
        S.op(DVE, lambda e: e.reciprocal(out=r[:, :TT], in_=r[:, :TT]), reads=[rres], writes=[rres])

    def load_x(self, x_src, t0, TT, x, xres, key):
        xv = x_src.rearrange("(c p) t -> p c t", p=128)
        self.S.dma(SP, key, lambda e: e.dma_start(out=x[:, :, :TT], in_=xv[:, :, t0:t0 + TT]),
                   reads=[("X", x_src.name, t0, c) for c in range(NCH)], writes=[xres])

    def prenorm(self, x, xres, h, hres, sq, sqres, r, rres, gi, TT):
        S = self.S
        self.rstd_of(x, [xres], sq, sqres, r, rres, TT)
        for c in range(NCH):
            g = self.gs[:, gi * NCH + c: gi * NCH + c + 1]
            S.op(DVE, lambda e, c=c, g=g: e.scalar_tensor_tensor(
                out=h[:, c, :TT], in0=x[:, c, :TT], scalar=g, in1=r[:, :TT], op0=ALU.mult, op1=ALU.mult),
                reads=[xres, rres, "c_gs"], writes=[hres + (c,)])

    def postnorm_store(self, ft, fkey, x, xres, sq, sqres, r2, r2res, tmp, tkey, gi, x_dst, t0, TT, final):
        S = self.S
        fres = [fkey + (c,) for c in range(NCH)]
        self.rstd_of(ft, fres, sq, sqres, r2, r2res, TT)
        for c in range(NCH):
            g = self.gs[:, gi * NCH + c: gi * NCH + c + 1]
            tm = tmp[c % len(tmp)]
            tres = tkey + (c % len(tmp),)
            S.op(POOL, lambda e, c=c, tm=tm: e.tensor_tensor(out=tm[:, :TT], in0=ft[:, c, :TT], in1=r2[:, :TT], op=ALU.mult),
                 reads=[fres[c], r2res], writes=[tres])
            S.op(DVE, lambda e, c=c, g=g, tm=tm: e.scalar_tensor_tensor(
                out=ft[:, c, :TT], in0=tm[:, :TT], scalar=g, in1=x[:, c, :TT], op0=ALU.mult, op1=ALU.add),
                reads=[tres, xres, "c_gs"], writes=[fres[c]])
            tk = S.dma(SP, ("xs", c), lambda e, c=c: e.dma_start(
                out=x_dst[c * 128:(c + 1) * 128, t0:t0 + TT], in_=ft[:, c, :TT]),
                reads=[fres[c]], writes=[("X", x_dst.name, t0, c)])
            if final:
                self.out_toks.append(tk)

    def ffn_layer(self, x_src, x_dst, w_in, w_out, gi_pre, gi_post, final=False):
        S, T = self.S, self.T
        TT = 256
        NJ = FFN_H // 128
        self.new_phase()
        win = self.W.tile([128, NCH, 2 * FFN_H], BF16, "win")
        wout = self.W.tile([128, NJ, D], BF16, "wout")
        self.load_w(win, w_in, ("W", "win"), "wl")
        self.load_w(wout, w_out, ("W", "wout"), "wl")
        A = self.A
        xt = [A.tile([128, NCH, TT], F32, f"x{i}") for i in range(2)]
        ht = [A.tile([128, NCH, TT], BF16, f"h{i}") for i in range(2)]
        sq = A.tile([128, NCH, TT], BF16, "sq")
        rt = [A.tile([128, TT], F32, f"r{i}") for i in range(2)]
        st = [A.tile([128, TT], F32, f"s{i}") for i in range(4)]
        ut = A.tile([128, NJ, TT], BF16, "u")
        ft = A.tile([128, NCH, TT], F32, "f")
        tmp = [A.tile([128, TT], F32, f"t{i}") for i in range(2)]
        win_res = [("W", "win", c) for c in range(NCH)]
        wout_res = [("W", "wout", j) for j in range(NJ)]
        for ti in range(T // TT):
            t0 = ti * TT
            sl = ti % 2
            x, h, r = xt[sl], ht[sl], rt[sl]
            xres, hres, rres = ("A", "x", sl), ("A", "h", sl), ("A", "r", sl)
            self.load_x(x_src, t0, TT, x, xres, ("xl", sl))
            self.prenorm(x, xres, h, hres, sq, ("A", "sq"), r, rres, gi_pre, TT)
            for j in range(NJ):
                b = self.bank()
                for half in range(2):
                    col = half * FFN_H + j * 128
                    for c in range(NCH):
                        S.op(PE, lambda e, b=b, half=half, col=col, c=c, h=h: e.matmul(
                            self.ps[b][:, half * TT:(half + 1) * TT], lhsT=win[:, c, col:col + 128], rhs=h[:, c, :],
                            start=(c == 0), stop=(c == NCH - 1)),
                            reads=[win_res[c], hres + (c,)], writes=[("ps", b)], inc=(half == 1 and c == NCH - 1))
                s = st[j % 4]
                sres = ("A", "s", j % 4)
                S.op(ACT, lambda e, b=b, s=s: e.activation(out=s[:], in_=self.ps[b][:, 0:TT], func=AF.Silu),
                     reads=[("ps", b)], writes=[sres])
                S.op(DVE, lambda e, b=b, s=s, j=j: e.tensor_tensor(out=ut[:, j, :], in0=self.ps[b][:, TT:2 * TT], in1=s[:],
                                                                   op=ALU.mult),
                     reads=[("ps", b), sres], writes=[("A", "u", j)])
            for c in range(NCH):
                b = self.bank()
                for j in range(NJ):
                    S.op(PE, lambda e, b=b, c=c, j=j: e.matmul(
                        self.ps[b][:, :TT], lhsT=wout[:, j, c * 128:(c + 1) * 128], rhs=ut[:, j, :],
                        start=(j == 0), stop=(j == NJ - 1)),
                        reads=[wout_res[j], ("A", "u", j)], writes=[("ps", b)], inc=(j == NJ - 1))
                S.op(ACT, lambda e, b=b, c=c: e.activation(out=ft[:, c, :], in_=self.ps[b][:, :TT], func=AF.Copy),
                     reads=[("ps", b)], writes=[("A", "f", c)])
            self.postnorm_store(ft, ("A", "f"), x, xres, sq, ("A", "sq"), rt[1 - sl], ("A", "r", 1 - sl),
                                tmp, ("A", "t"), gi_post, x_dst, t0, TT, final)

    def conv_layer(self, x_src, x_dst, w_in, w_out, cvp, gi_pre, gi_post, dbg=0):
        U = self.dscr("conv_u", [D, self.T + 32], BF16)
        self._conv_p1(x_src, w_in, cvp, gi_pre, U, dbg)
        if dbg == 1:
            return
        self._conv_p2(x_src, x_dst, w_out, cvp, gi_post, U, dbg)

    def _conv_p1(self, x_src, w_in, cvp, gi_pre, U, dbg):
        S, T = self.S, self.T
        TT = 256
        KW, PAD = 31, 15
        import os
        sk = os.environ.get("SKIP", "")
        self.new_phase()
        A = self.A
        win = self.W.tile([128, NCH, 2 * D], BF16, "win")
        if "M" in sk:
            for c in range(NCH):
                S.op(DVE, lambda e, c=c: e.memset(win[:, c, :], 1e30), writes=[("W", "win", c)])
        self.load_w(win, w_in, ("W", "win"), "wl")
        cp = A.tile([128, 296], F32, "cp")
        S.dma(SP, "cpl", lambda e: e.dma_start(out=cp[:], in_=cvp[:, :]), writes=[("A", "cp")])
        zt = A.tile([128, 16], BF16, "z")
        S.op(DVE, lambda e: e.memset(zt[:], 0.0), writes=[("A", "z")])
        for c in range(NCH if "P" not in sk else 0):
            for side in range(2):
                col = 0 if side == 0 else T + 16
                S.dma(SP, ("uz", side), lambda e, c=c, col=col: e.dma_start(
                    out=U[c * 128:(c + 1) * 128, col:col + 16], in_=zt[:]), reads=[("A", "z")], writes=[("U", "pad", side, c)])
        xt = [A.tile([128, NCH, TT], F32, f"x{i}") for i in range(2)]
        ht = [A.tile([128, NCH, TT], BF16, f"h{i}") for i in range(2)]
        sq = A.tile([128, NCH, TT], BF16, "sq")
        rt = [A.tile([128, TT], F32, f"r{i}") for i in range(2)]
        st = [A.tile([128, TT], F32, f"s{i}") for i in range(4)]
        ut = [A.tile([128, NCH, TT], BF16, f"u{i}") for i in range(2)]
        win_res = [("W", "win", c) for c in range(NCH)]
        nt = T // TT
        for ti in range(nt):
            t0 = ti * TT
            sl = ti % 2
            x, h, r, u = xt[sl], ht[sl], rt[sl], ut[sl]
            xres, hres, rres = ("A", "x", sl), ("A", "h", sl), ("A", "r", sl)
            self.load_x(x_src, t0, TT, x, xres, ("xl", sl))
            self.prenorm(x, xres, h, hres, sq, ("A", "sq"), r, rres, gi_pre, TT)
            for j in range(NCH):
                b = self.bank()
                for half in range(2):
                    col = half * D + j * 128
                    for c in range(NCH):
                        S.op(PE, lambda e, b=b, half=half, col=col, c=c, h=h: e.matmul(
                            self.ps[b][:, half * TT:(half + 1) * TT], lhsT=win[:, c, col:col + 128], rhs=h[:, c, :],
                            start=(c == 0), stop=(c == NCH - 1)),
                            reads=[win_res[c], hres + (c,)], writes=[("ps", b)], inc=(half == 1 and c == NCH - 1))
                sg = st[j % 4]
                sres = ("A", "s", j % 4)
                S.op(ACT, lambda e, b=b, sg=sg, j=j: e.activation(out=sg[:], in_=self.ps[b][:, TT:2 * TT], func=AF.Sigmoid,
                                                                 bias=cp[:, 8 + j:9 + j]),
                     reads=[("ps", b), ("A", "cp")], writes=[sres])
                S.op(DVE, lambda e, b=b, sg=sg, j=j, u=u: e.scalar_tensor_tensor(
                    out=u[:, j, :], in0=self.ps[b][:, 0:TT], scalar=cp[:, j:j + 1], in1=sg[:], op0=ALU.add, op1=ALU.mult),
                    reads=[("ps", b), sres, ("A", "cp")], writes=[("A", "u", sl, j)])
                if dbg == 2:
                    S.op(DVE, lambda e: e.memset(zt[:, 0:1], 0.0), writes=[("A", "u", sl, j)])
                S.dma(SP, "us", lambda e, j=j, u=u, t0=t0: e.dma_start(
                    out=U[j * 128:(j + 1) * 128, 16 + t0:16 + t0 + TT], in_=u[:, j, :]),
                    reads=[("A", "u", sl, j)] + ([("A", "s", (j + 1) % 4)] if "x" in sk else []), writes=[("U", ti, j)])
                if dbg == 11:
                    dt_ = st[(j + 2) % 4]
                    S.op(POOL, lambda e, j=j, u=u, dt_=dt_: e.tensor_copy(out=dt_[:], in_=u[:, j, :]), reads=[("A", "u", sl, j)], writes=[("A", "s", (j + 2) % 4)])
                    S.dma(SP, ("xs", j), lambda e, j=j, dt_=dt_, t0=t0: e.dma_start(out=x_dst[j * 128:(j + 1) * 128, t0:t0 + TT], in_=dt_[:]),
                          reads=[("A", "s", (j + 2) % 4)], writes=[("X", t0, j)])
    def _conv_p2(self, x_src, x_dst, w_out, cvp, gi_post, U, dbg):
        S, T = self.S, self.T
        TT = 256
        KW, PAD = 31, 15
        nt = T // TT
        import os
        sk = os.environ.get("SKIP", "")
        self.new_phase()
        A = self.A
        uh = [A.tile([128, NCH, TT + 32], BF16, f"uh{i}") for i in range(2)]
        dbt = A.tile([128, NCH, TT], F32, "dbt") if "L" in sk else None
        cp = A.tile([128, 296], F32, "cp")
        S.dma(SP, "cpl", lambda e: e.dma_start(out=cp[:], in_=cvp[:, :]), writes=[("A", "cp")])
        wout = self.W.tile([128, NCH, D], BF16, "wout")
        import os
        sk = os.environ.get("SKIP", "")
        if "w" not in sk:
            self.load_w(wout, w_out, ("W", "wout"), "wl")
        dg = self.W.tile([128, KW * NCH, 128], BF16, "dg")
        for k in range(KW * NCH if "d" not in sk else 0):
            S.op(DVE, lambda e, k=k: e.tensor_scalar(out=dg[:, k, :], in0=self.ident[:], scalar1=cp[:, 48 + k:49 + k],
                                                     scalar2=None, op0=ALU.mult),
                 reads=["c_ident", ("A", "cp")], writes=[("W", "dg", k)])
        epsl = A.tile([128, 1], F32, "epsl")
        S.op(DVE, lambda e: e.memset(epsl[:], float(LN_EPS)), writes=[("A", "epsl")])
        xt = [A.tile([128, NCH, TT], F32, f"x{i}") for i in range(2)]
        vt = A.tile([128, NCH, TT], F32, "v")
        vb = A.tile([128, NCH, TT], BF16, "vb")
        sq = A.tile([128, NCH, TT], BF16, "sq")
        mt = A.tile([128, TT], F32, "m")
        msq = A.tile([128, TT], F32, "msq")
        var = A.tile([128, TT], F32, "var")
        rs = A.tile([128, TT], F32, "rs")
        tmp = [A.tile([128, TT], F32, f"t{i}") for i in range(2)]
        zb = A.tile([128, NCH, TT], BF16, "zb")
        ft = A.tile([128, NCH, TT], F32, "f")
        r2 = A.tile([128, TT], F32, "r2")
        sq2 = A.tile([128, NCH, TT], BF16, "sq2")
        wout_res = [("W", "wout", c) for c in range(NCH)]
        for ti in range(nt):
            t0 = ti * TT
            sl = ti % 2
            x, xres = xt[sl], ("A", "x", sl)
            u, ures = uh[sl], ("A", "uh", sl)
            if "X" not in sk:
                self.load_x(x_src, t0, TT, x, xres, ("xl", sl))
            deps = [("U", tj, c) for tj in (ti - 1, ti, ti + 1) if 0 <= tj < nt for c in range(NCH)]
            if "P" not in sk:
                deps += [("U", "pad", sd, c) for sd in range(2) for c in range(NCH)]
            Uv = U.rearrange("(c p) t -> p c t", p=128)
            if "3" in sk:
                S.dma(SP, ("uhl", sl), lambda e, u=u, t0=t0: e.dma_start(out=u[:], in_=Uv[:, :, t0:t0 + TT + 32]),
                      reads=deps, writes=[ures])
            else:
                for c in range(NCH):
                    S.dma(SP, ("uhl", sl, c % 2), lambda e, u=u, t0=t0, c=c: e.dma_start(
                        out=u[:, c, :], in_=U[c * 128:(c + 1) * 128, t0:t0 + TT + 32]), reads=deps, writes=[ures])
            if dbg == 2:
                for c in range(NCH):
                    fx = dbt if dbt is not None else ft
                    S.op(DVE, lambda e, c=c, u=u, fx=fx: e.tensor_copy(out=fx[:, c, :], in_=u[:, c, 16:16 + TT]), reads=[ures], writes=[("A", "f", c)])
                    S.dma(SP, ("xs", c), lambda e, c=c, fx=fx: e.dma_start(out=x_dst[c * 128:(c + 1) * 128, t0:t0 + TT], in_=fx[:, c, :]),
                          reads=[("A", "f", c)], writes=[("X", t0, c)])
                return
            for c in range(NCH):
                b = self.bank()
                for k in range(KW):
                    S.op(PE, lambda e, b=b, c=c, k=k, u=u: e.matmul(
                        self.ps[b][:, :TT], lhsT=dg[:, k * NCH + c, :], rhs=u[:, c, k + 1:k + 1 + TT],
                        start=(k == 0), stop=(k == KW - 1)),
                        reads=[("W", "dg", k * NCH + c), ures], writes=[("ps", b)], inc=(k == KW - 1))
                S.op(ACT, lambda e, b=b, c=c: e.activation(out=vt[:, c, :], in_=self.ps[b][:, :TT], func=AF.Identity,
                                                           bias=cp[:, 16 + c:17 + c]),
                     reads=[("ps", b), ("A", "cp")], writes=[("A", "v", c)])
                S.op(POOL, lambda e, c=c: e.tensor_copy(out=vb[:, c, :], in_=vt[:, c, :]),
                     reads=[("A", "v", c)], writes=[("A", "vb", c)])
                S.op(ACT, lambda e, c=c: e.activation(out=sq[:, c, :], in_=vt[:, c, :], func=AF.Square),
                     reads=[("A", "v", c)], writes=[("A", "sq", c)])
            if dbg == 3:
                for c in range(NCH):
                    S.dma(SP, ("xs", c), lambda e, c=c: e.dma_start(out=x_dst[c * 128:(c + 1) * 128, t0:t0 + TT], in_=vt[:, c, :]),
                          reads=[("A", "v", c)], writes=[("X", t0, c)])
                return
            b = self.bank()
            for half, (src, key) in enumerate(((vb, "vb"), (sq, "sq"))):
                for c in range(NCH):
                    S.op(PE, lambda e, b=b, half=half, src=src, c=c: e.matmul(
                        self.ps[b][:, half * TT:(half + 1) * TT], lhsT=self.ones[:], rhs=src[:, c, :],
                        start=(c == 0), stop=(c == NCH - 1)),
                        reads=["c_ones", ("A", key, c)], writes=[("ps", b)], inc=(half == 1 and c == NCH - 1))
            S.op(DVE, lambda e, b=b: e.tensor_scalar(out=mt[:], in0=self.ps[b][:, 0:TT], scalar1=1.0 / D, scalar2=None,
                                                     op0=ALU.mult), reads=[("ps", b)], writes=[("A", "m")])
            S.op(POOL, lambda e: e.tensor_tensor(out=msq[:], in0=mt[:], in1=mt[:], op=ALU.mult),
                 reads=[("A", "m")], writes=[("A", "msq")])
            S.op(DVE, lambda e, b=b: e.scalar_tensor_tensor(out=var[:], in0=self.ps[b][:, TT:2 * TT], scalar=1.0 / D, in1=msq[:],
                                                            op0=ALU.mult, op1=ALU.subtract),
                 reads=[("ps", b), ("A", "msq")], writes=[("A", "var")])
            S.op(ACT, lambda e: e.activation(out=rs[:], in_=var[:], func=AF.Sqrt, bias=epsl[:, 0:1]),
                 reads=[("A", "var"), ("A", "epsl")], writes=[("A", "rs")])
            S.op(DVE, lambda e: e.reciprocal(out=rs[:], in_=rs[:]), reads=[("A", "rs")], writes=[("A", "rs")])
            for c in range(NCH):
                tm = tmp[c % 2]
                tres = ("A", "t", c % 2)
                S.op(POOL, lambda e, c=c, tm=tm: e.tensor_tensor(out=tm[:], in0=vt[:, c, :], in1=mt[:], op=ALU.subtract),
                     reads=[("A", "v", c), ("A", "m")], writes=[tres])
                S.op(DVE, lambda e, tm=tm: e.tensor_tensor(out=tm[:], in0=tm[:], in1=rs[:], op=ALU.mult),
                     reads=[tres, ("A", "rs")], writes=[tres])
                S.op(ACT, lambda e, c=c, tm=tm: e.activation(out=zb[:, c, :], in_=tm[:], func=AF.Silu,
                                                             scale=cp[:, 24 + c:25 + c], bias=cp[:, 32 + c:33 + c]),
                     reads=[tres, ("A", "cp")], writes=[("A", "zb", c)])
            if dbg == 4:
                for c in range(NCH):
                    S.op(DVE, lambda e, c=c: e.tensor_copy(out=ft[:, c, :], in_=zb[:, c, :]), reads=[("A", "zb", c)], writes=[("A", "f", c)])
                    S.dma(SP, ("xs", c), lambda e, c=c: e.dma_start(out=x_dst[c * 128:(c + 1) * 128, t0:t0 + TT], in_=ft[:, c, :]),
                          reads=[("A", "f", c)], writes=[("X", t0, c)])
                return
            for co in range(NCH):
                b = self.bank()
                for c in range(NCH):
                    S.op(PE, lambda e, b=b, co=co, c=c: e.matmul(
                        self.ps[b][:, :TT], lhsT=wout[:, c, co * 128:(co + 1) * 128], rhs=zb[:, c, :],
                        start=(c == 0), stop=(c == NCH - 1)),
                        reads=[wout_res[c], ("A", "zb", c)], writes=[("ps", b)], inc=(c == NCH - 1))
                S.op(ACT, lambda e, b=b, co=co: e.activation(out=ft[:, co, :], in_=self.ps[b][:, :TT], func=AF.Identity,
                                                             bias=cp[:, 40 + co:41 + co]),
                     reads=[("ps", b), ("A", "cp")], writes=[("A", "f", co)])
            self.postnorm_store(ft, ("A", "f"), x, xres, sq2, ("A", "sq2"), r2, ("A", "r2"),
                                tmp, ("A", "t"), gi_post, x_dst, t0, TT, False)

    def proj_post(self, src, K, w, x_src, x_dst, gi_post, final=False):
        S, T = self.S, self.T
        TT = 256
        KC = K // 128
        self.new_phase()
        A = self.A
        wt = self.W.tile([128, KC, D], BF16, "pw")
        self.load_w(wt, w, ("W", "pw"), "wl")
        yt = [A.tile([128, KC, TT], BF16, f"py{i}") for i in range(2)]
        xt = [A.tile([128, NCH, TT], F32, f"px{i}") for i in range(2)]
        ft = A.tile([128, NCH, TT], F32, "pf")
        sq = A.tile([128, NCH, TT], BF16, "psq")
        r2 = A.tile([128, TT], F32, "pr2")
        tmp = [A.tile([128, TT], F32, f"pt{i}") for i in range(2)]
        sv = src.rearrange("(c p) t -> p c t", p=128)
        for ti in range(T // TT):
            t0 = ti * TT
            sl = ti % 2
            x, xres = xt[sl], ("A", "px", sl)
            y, yres = yt[sl], ("A", "py", sl)
            self.load_x(x_src, t0, TT, x, xres, ("xl", sl))
            S.dma(SP, ("pyl", sl), lambda e, y=y, t0=t0: e.dma_start(out=y[:], in_=sv[:, :, t0:t0 + TT]),
                  reads=[("SRC", src.name, t0 // 128), ("SRC", src.name, t0 // 128 + 1)], writes=[yres])
            for co in range(NCH):
                b = self.bank()
                for kc in range(KC):
                    S.op(PE, lambda e, b=b, co=co, kc=kc, y=y: e.matmul(
                        self.ps[b][:, :TT], lhsT=wt[:, kc, co * 128:(co + 1) * 128], rhs=y[:, kc, :],
                        start=(kc == 0), stop=(kc == KC - 1)),
                        reads=[("W", "pw", kc), yres], writes=[("ps", b)], inc=(kc == KC - 1))
                S.op(ACT, lambda e, b=b, co=co: e.activation(out=ft[:, co, :], in_=self.ps[b][:, :TT], func=AF.Copy),
                     reads=[("ps", b)], writes=[("A", "pf", co)])
            self.postnorm_store(ft, ("A", "pf"), x, xres, sq, ("A", "psq"), r2, ("A", "pr2"),
                                tmp, ("A", "pt"), gi_post, x_dst, t0, TT, final)

    def ret_layer(self, x_src, x_dst, w_in, w_out, dec, rope, rcst, gi_pre, gi_post, li):
        T = self.T
        QT = self.dscr(f"r{li}_qT", [D, T], BF16)
        KT = self.dscr(f"r{li}_kT", [D, T], BF16)
        V = self.dscr(f"r{li}_v", [T, 2048], BF16)
        G = self.dscr(f"r{li}_g", [T, 2048], BF16)
        CB = self.dscr(f"r{li}_cb", [T, 2048], F32)
        YT = self.dscr(f"r{li}_yT", [2048, T], BF16)
        self._ret_inproj(x_src, w_in, rope, gi_pre, QT, KT, V, G)
        self._ret_scan(dec, rcst, QT, KT, V, G, CB, YT, backward=True)
        self._ret_scan(dec, rcst, QT, KT, V, G, CB, YT, backward=False)
        self.proj_post(YT, 2048, w_out, x_src, x_dst, gi_post)

    def _ret_inproj(self, x_src, w_in, rope, gi_pre, QT, KT, V, G):
        S, T = self.S, self.T
        TT = 256
        self.new_phase()
        A = self.A
        win = self.W.tile([128, NCH, 6144], BF16, "rwin")
        self.load_w(win, w_in, ("W", "rwin"), "wl")
        xt = [A.tile([128, NCH, TT], F32, f"x{i}") for i in range(2)]
        ht = [A.tile([128, NCH, TT], BF16, f"h{i}") for i in range(2)]
        sq = A.tile([128, NCH, TT], BF16, "sq")
        rt = [A.tile([128, TT], F32, f"r{i}") for i in range(2)]
        rp = [A.tile([128, 4, TT], F32, f"rp{i}") for i in range(2)]
        t4 = [A.tile([128, TT], F32, f"t4{i}") for i in range(4)]
        ob = [A.tile([128, 2, TT], BF16, f"ob{i}") for i in range(2)]
        vb = [A.tile([128, 512], BF16, f"vb{i}") for i in range(3)]
        win_res = [("W", "rwin", c) for c in range(NCH)]
        n_ob = 0
        n_vb = 0
        for ti in range(T // TT):
            t0 = ti * TT
            sl = ti % 2
            x, h, r, rpt = xt[sl], ht[sl], rt[sl], rp[sl]
            xres, hres, rres, rpres = ("A", "x", sl), ("A", "h", sl), ("A", "r", sl), ("A", "rp", sl)
            self.load_x(x_src, t0, TT, x, xres, ("xl", sl))
            S.dma(SP, ("rpl", sl), lambda e, rpt=rpt, t0=t0: e.dma_start(out=rpt[:], in_=rope[:, :, t0:t0 + TT]), writes=[rpres])
            self.prenorm(x, xres, h, hres, sq, ("A", "sq"), r, rres, gi_pre, TT)
            for qk in range(2):
                dst = QT if qk == 0 else KT
                for hd in range(4):
                    b = self.bank()
                    for half in range(2):
                        col = qk * 1024 + hd * 256 + half * 128
                        for c in range(NCH):
                            S.op(PE, lambda e, b=b, half=half, col=col, c=c, h=h: e.matmul(
                                self.ps[b][:, half * TT:(half + 1) * TT], lhsT=win[:, c, col:col + 128], rhs=h[:, c, :],
                                start=(c == 0), stop=(c == NCH - 1)),
                                reads=[win_res[c], hres + (c,)], writes=[("ps", b)], inc=(half == 1 and c == NCH - 1))
                    o = ob[n_ob % 2]
                    ores = ("A", "ob", n_ob % 2)
                    n_ob += 1
                    cs, sn = rpt[:, 2 * qk, :], rpt[:, 2 * qk + 1, :]
                    p1, p2 = self.ps[b][:, 0:TT], self.ps[b][:, TT:2 * TT]
                    for k4, (pa, tb) in enumerate(((p1, cs), (p2, sn), (p2, cs), (p1, sn))):
                        S.op(DVE, lambda e, k4=k4, pa=pa, tb=tb: e.tensor_tensor(out=t4[k4][:], in0=pa, in1=tb, op=ALU.mult),
                             reads=[("ps", b), rpres], writes=[("A", "t4", k4)])
                    S.op(POOL, lambda e, o=o: e.tensor_tensor(out=o[:, 0, :], in0=t4[0][:], in1=t4[1][:], op=ALU.subtract),
                         reads=[("A", "t4", 0), ("A", "t4", 1)], writes=[ores])
                    S.op(POOL, lambda e, o=o: e.tensor_tensor(out=o[:, 1, :], in0=t4[2][:], in1=t4[3][:], op=ALU.add),
                         reads=[("A", "t4", 2), ("A", "t4", 3)], writes=[ores])
                    for half in range(2):
                        row = hd * 256 + half * 128
                        S.dma(SP, ("qks", half), lambda e, o=o, half=half, row=row, dst=dst, t0=t0: e.dma_start(
                            out=dst[row:row + 128, t0:t0 + TT], in_=o[:, half, :]),
                            reads=[ores], writes=[("SRC", dst.name, t0 // 128), ("SRC", dst.name, t0 // 128 + 1)])
            for blk in range(TT // 128):
                for vg in range(2):
                    dst = V if vg == 0 else G
                    for grp in range(4):
                        b = self.bank()
                        col = 2048 + vg * 2048 + grp * 512
                        for c in range(NCH):
                            S.op(PE, lambda e, b=b, col=col, c=c, h=h, blk=blk: e.matmul(
                                self.ps[b][:, :], lhsT=h[:, c, blk * 128:(blk + 1) * 128], rhs=win[:, c, col:col + 512],
                                start=(c == 0), stop=(c == NCH - 1)),
                                reads=[win_res[c], hres + (c,)], writes=[("ps", b)], inc=(c == NCH - 1))
                        v = vb[n_vb % 3]
                        vres = ("A", "vb", n_vb % 3)
                        n_vb += 1
                        S.op(ACT, lambda e, b=b, v=v, vg=vg: e.activation(out=v[:], in_=self.ps[b][:, :],
                                                                         func=(AF.Copy if vg == 0 else AF.Silu)),
                             reads=[("ps", b)], writes=[vres])
                        S.dma(SP, ("vgs", n_vb % 3), lambda e, v=v, dst=dst, grp=grp, t0=t0, blk=blk: e.dma_start(
                            out=dst[t0 + blk * 128:t0 + (blk + 1) * 128, grp * 512:(grp + 1) * 512], in_=v[:]),
                            reads=[vres], writes=[("SRC", dst.name, t0 // 128 + blk, grp)])

    def _ret_scan(self, dec, rcst, QT, KT, V, G, CB, YT, backward):
        S, T = self.S, self.T
        n = T // 128
        self.new_phase()
        A = self.A
        dct = A.tile([128, 8], F32, "dct")
        lg = A.tile([128, 8], F32, "lg")
        rc = A.tile([128, 772], F32, "rc")
        S.dma(SP, "dcl", lambda e: e.dma_start(out=dct[:], in_=dec[:, :]), writes=[("A", "dct")])
        S.dma(SP, "rcl", lambda e: e.dma_start(out=rc[:], in_=rcst[:, :]), writes=[("A", "rc")])
        S.op(ACT, lambda e: e.activation(out=lg[:], in_=dct[:], func=AF.Exp), reads=[("A", "dct")], writes=[("A", "lg")])
        S.op(DVE, lambda e: e.tensor_scalar(out=lg[:], in0=lg[:], scalar1=-1.0, scalar2=1.0, op0=ALU.mult, op1=ALU.add),
             reads=[("A", "lg")], writes=[("A", "lg")])
        S.op(ACT, lambda e: e.activation(out=lg[:], in_=lg[:], func=AF.Ln), reads=[("A", "lg")], writes=[("A", "lg")])
        d0 = 4 if backward else 0
        xi8 = A.tile([128, 8, 128], F32, "xi8")
        zeta = A.tile([128, 4], F32, "zeta")
        cdk = A.tile([128, 4], F32, "cdk")
        eps1 = A.tile([128, 1], F32, "eps1")
        S.op(DVE, lambda e: e.memset(eps1[:], float(RMS_EPS)), writes=[("A", "eps1")])
        rowc = 640 if backward else 512
        colc = 769 if backward else 768
        for hd in range(4):
            sc = lg[:, d0 + hd:d0 + hd + 1]
            for half in range(2):
                S.op(ACT, lambda e, hd=hd, half=half, sc=sc: e.activation(out=xi8[:, hd * 2 + half, :], in_=rc[:, rowc:rowc + 128],
                                                                          func=AF.Exp, scale=sc),
                     reads=[("A", "rc"), ("A", "lg")], writes=[("A", "xi8")])
            S.op(ACT, lambda e, hd=hd, sc=sc: e.activation(out=zeta[:, hd:hd + 1], in_=rc[:, colc:colc + 1], func=AF.Exp, scale=sc),
                 reads=[("A", "rc"), ("A", "lg")], writes=[("A", "zeta")])
            S.op(ACT, lambda e, hd=hd, sc=sc: e.activation(out=cdk[:, hd:hd + 1], in_=rc[:, 770:771], func=AF.Exp, scale=sc),
                 reads=[("A", "rc"), ("A", "lg")], writes=[("A", "cdk")])
        DT = A.tile([128, 4, 128], F32, "DT")
        dtmp = A.tile([128, 128], F32, "dtmp")
        if not backward:
            for hd in range(4):
                S.op(ACT, lambda e, hd=hd: e.activation(out=DT[:, hd, :], in_=rc[:, 0:128], func=AF.Exp, scale=lg[:, hd:hd + 1]),
                     reads=[("A", "rc"), ("A", "lg")], writes=[("A", "DT", hd)])
                S.op(DVE, lambda e, hd=hd: e.tensor_tensor(out=DT[:, hd, :], in0=DT[:, hd, :], in1=rc[:, 128:256], op=ALU.mult),
                     reads=[("A", "DT", hd), ("A", "rc")], writes=[("A", "DT", hd)])
                S.op(ACT, lambda e, hd=hd: e.activation(out=dtmp[:], in_=rc[:, 256:384], func=AF.Exp, scale=lg[:, 4 + hd:5 + hd]),
                     reads=[("A", "rc"), ("A", "lg")], writes=[("A", "dtmp")])
                S.op(DVE, lambda e: e.tensor_tensor(out=dtmp[:], in0=dtmp[:], in1=rc[:, 384:512], op=ALU.mult),
                     reads=[("A", "dtmp"), ("A", "rc")], writes=[("A", "dtmp")])
                S.op(DVE, lambda e, hd=hd: e.tensor_tensor(out=DT[:, hd, :], in0=DT[:, hd, :], in1=dtmp[:], op=ALU.add),
                     reads=[("A", "DT", hd), ("A", "dtmp")], writes=[("A", "DT", hd)])
        DTres = [("A", "DT", hd) for hd in range(4)]
        Sf = A.tile([128, 4, 2, 512], F32, "Sf")
        Sb = A.tile([128, 4, 2, 512], BF16, "Sb")
        for hd in range(4):
            S.op(POOL, lambda e, hd=hd: e.memset(Sf[:, hd, :, :], 0.0), writes=[("A", "Sf", hd, 0), ("A", "Sf", hd, 1)])
            S.op(POOL, lambda e, hd=hd: e.memset(Sb[:, hd, :, :], 0.0), writes=[("A", "Sb", hd, 0), ("A", "Sb", hd, 1)])
        qt = [A.tile([128, 8, 128], BF16, f"qt{i}") for i in range(2)]
        kt = [A.tile([128, 8, 128], BF16, f"kt{i}") for i in range(2)]
        vt = [A.tile([128, 2048], BF16, f"vt{i}") for i in range(2)]
        qx = A.tile([128, 8, 128], BF16, "qx")
        kz = A.tile([128, 4, 256], BF16, "kz")
        if backward:
            cbo = [A.tile([128, 512], F32, f"cbo{i}") for i in range(2)]
        else:
            gt = [A.tile([128, 2048], BF16, f"gt{i}") for i in range(2)]
            cbt = [A.tile([128, 2048], F32, f"cbt{i}") for i in range(2)]
            Pm = A.tile([128, 512], BF16, "Pm")
            ot = A.tile([128, 4, 512], F32, "ot")
            junk = A.tile([128, 512], BF16, "junk")
            ss = A.tile([128, 4], F32, "ss")
            yb = A.tile([128, 2048], BF16, "yb")
            yT = A.tile([128, 16, 128], BF16, "yT")
        QTv = QT.rearrange("(c p) t -> p c t", p=128)
        KTv = KT.rearrange("(c p) t -> p c t", p=128)
        order = list(range(n - 1, -1, -1)) if backward else list(range(n))
        ncb = 0
        for it, jc in enumerate(order):
            sl = it % 2
            c0 = jc * 128
            q, k, v = qt[sl], kt[sl], vt[sl]
            qres, kres, vres = ("A", "qt", sl), ("A", "kt", sl), ("A", "vt", sl)
            S.dma(SP, ("ql", sl), lambda e, q=q, c0=c0: e.dma_start(out=q[:], in_=QTv[:, :, c0:c0 + 128]),
                  reads=[("SRC", QT.name, jc)], writes=[qres])
            S.dma(SP, ("kl", sl), lambda e, k=k, c0=c0: e.dma_start(out=k[:], in_=KTv[:, :, c0:c0 + 128]),
                  reads=[("SRC", KT.name, jc)], writes=[kres])
            S.dma(SP, ("vl", sl), lambda e, v=v, c0=c0: e.dma_start(out=v[:], in_=V[c0:c0 + 128, :]),
                  reads=[("SRC", V.name, jc, g4) for g4 in range(4)], writes=[vres])
            if not backward:
                g, cb = gt[sl], cbt[sl]
                gres, cbres = ("A", "gt", sl), ("A", "cbt", sl)
                S.dma(SP, ("gl", sl), lambda e, g=g, c0=c0: e.dma_start(out=g[:], in_=G[c0:c0 + 128, :]),
                      reads=[("SRC", G.name, jc, g4) for g4 in range(4)], writes=[gres])
                S.dma(SP, ("cbl", sl), lambda e, cb=cb, c0=c0: e.dma_start(out=cb[:], in_=CB[c0:c0 + 128, :]),
                      reads=[("SRC", CB.name, jc, g4) for g4 in range(4)], writes=[cbres])
            S.op(DVE, lambda e, q=q: e.tensor_tensor(out=qx[:], in0=q[:], in1=xi8[:], op=ALU.mult),
                 reads=[qres, ("A", "xi8")], writes=[("A", "qx")])
            bt = self.bank()
            pbt = self.psb[bt]
            for c in range(8):
                S.op(PE, lambda e, c=c, k=k, pbt=pbt: e.transpose(out=pbt[:, c * 128:(c + 1) * 128], in_=k[:, c, :], identity=self.ident[:]),
                     reads=[kres, "c_ident"], writes=[("ps", bt)], inc=(c == 7))
            for hd in range(4):
                S.op(ACT, lambda e, hd=hd, pbt=pbt: e.activation(out=kz[:, hd, :], in_=pbt[:, hd * 256:(hd + 1) * 256], func=AF.Copy,
                                                                 scale=zeta[:, hd:hd + 1]),
                     reads=[("ps", bt), ("A", "zeta")], writes=[("A", "kz", hd)])
            if not backward:
                bs = self.bank()
                for hd in range(4):
                    for half in range(2):
                        S.op(PE, lambda e, hd=hd, half=half, k=k, q=q, bs=bs: e.matmul(
                            self.ps[bs][:, hd * 128:(hd + 1) * 128], lhsT=k[:, hd * 2 + half, :], rhs=q[:, hd * 2 + half, :],
                            start=(half == 0), stop=(half == 1)),
                            reads=[kres, qres], writes=[("ps", bs)], inc=(hd == 3 and half == 1))
                S.op(DVE, lambda e, bs=bs: e.tensor_tensor(out=Pm[:], in0=self.ps[bs][:], in1=DT[:].rearrange("p h i -> p (h i)"),
                                                           op=ALU.mult),
                     reads=[("ps", bs)] + DTres, writes=[("A", "Pm")])
            for hd in range(4):
                bo = self.bank()
                if not backward:
                    S.op(PE, lambda e, hd=hd, bo=bo, v=v: e.matmul(
                        self.ps[bo][:], lhsT=Pm[:, hd * 128:(hd + 1) * 128], rhs=v[:, hd * 512:(hd + 1) * 512], start=True, stop=False),
                        reads=[("A", "Pm"), vres], writes=[("ps", bo)], inc=False)
                for half in range(2):
                    S.op(PE, lambda e, hd=hd, half=half, bo=bo: e.matmul(
                        self.ps[bo][:], lhsT=qx[:, hd * 2 + half, :], rhs=Sb[:, hd, half, :],
                        start=(backward and half == 0), stop=(half == 1)),
                        reads=[("A", "qx"), ("A", "Sb", hd, half)], writes=[("ps", bo)], inc=(half == 1))
                if backward:
                    co = cbo[ncb % 2]
                    cores = ("A", "cbo", ncb % 2)
                    ncb += 1
                    S.op(ACT, lambda e, bo=bo, co=co: e.activation(out=co[:], in_=self.ps[bo][:], func=AF.Copy),
                         reads=[("ps", bo)], writes=[cores])
                    S.dma(SP, ("cbs", ncb % 2), lambda e, co=co, c0=c0, hd=hd: e.dma_start(
                        out=CB[c0:c0 + 128, hd * 512:(hd + 1) * 512], in_=co[:]),
                        reads=[cores], writes=[("SRC", CB.name, jc, hd)])
                else:
                    S.op(DVE, lambda e, bo=bo, hd=hd, cb=cb: e.tensor_tensor(out=ot[:, hd, :], in0=self.ps[bo][:],
                                                                             in1=cb[:, hd * 512:(hd + 1) * 512], op=ALU.add),
                         reads=[("ps", bo), cbres], writes=[("A", "ot", hd)])
                    S.op(ACT, lambda e, hd=hd: e.activation(out=junk[:], in_=ot[:, hd, :], func=AF.Square, accum_out=ss[:, hd:hd + 1]),
                         reads=[("A", "ot", hd)], writes=[("A", "junk"), ("A", "ss", hd)])
                    S.op(ACT, lambda e, hd=hd: e.activation(out=ss[:, hd:hd + 1], in_=ss[:, hd:hd + 1], func=AF.Sqrt,
                                                            scale=1.0 / 512.0, bias=eps1[:, 0:1]),
                         reads=[("A", "ss", hd), ("A", "eps1")], writes=[("A", "ss", hd)])
                    S.op(DVE, lambda e, hd=hd: e.reciprocal(out=ss[:, hd:hd + 1], in_=ss[:, hd:hd + 1]),
                         reads=[("A", "ss", hd)], writes=[("A", "ss", hd)])
                    S.op(DVE, lambda e, hd=hd, g=g: e.scalar_tensor_tensor(
                        out=yb[:, hd * 512:(hd + 1) * 512], in0=ot[:, hd, :], scalar=ss[:, hd:hd + 1],
                        in1=g[:, hd * 512:(hd + 1) * 512], op0=ALU.mult, op1=ALU.mult),
                        reads=[("A", "ot", hd), ("A", "ss", hd), gres], writes=[("A", "yb", hd)])
            if not backward:
                for half8 in range(2):
                    by = self.bank()
                    pby = self.psb[by]
                    for bb in range(8):
                        blk = half8 * 8 + bb
                        S.op(PE, lambda e, blk=blk, bb=bb, pby=pby: e.transpose(
                            out=pby[:, bb * 128:(bb + 1) * 128], in_=yb[:, blk * 128:(blk + 1) * 128], identity=self.ident[:]),
                            reads=[("A", "yb", blk // 4), "c_ident"], writes=[("ps", by)], inc=(bb == 7))
                    S.op(ACT, lambda e, half8=half8, pby=pby: e.activation(
                        out=yT[:, half8 * 8:(half8 + 1) * 8, :].rearrange("p a b -> p (a b)"), in_=pby[:, :], func=AF.Copy),
                        reads=[("ps", by)], writes=[("A", "yT", half8)])
                for blk in range(16):
                    S.dma(SP, ("yts", blk % 4), lambda e, blk=blk, c0=c0: e.dma_start(
                        out=YT[blk * 128:(blk + 1) * 128, c0:c0 + 128], in_=yT[:, blk, :]),
                        reads=[("A", "yT", blk // 8)], writes=[("SRC", YT.name, jc)])
            for hd in range(4):
                for half in range(2):
                    b = self.bank()
                    S.op(PE, lambda e, hd=hd, half=half, b=b, v=v: e.matmul(
                        self.ps[b][:], lhsT=kz[:, hd, half * 128:(half + 1) * 128], rhs=v[:, hd * 512:(hd + 1) * 512],
                        start=True, stop=True),
                        reads=[("A", "kz", hd), vres], writes=[("ps", b)])
                    S.op(DVE, lambda e, hd=hd, half=half, b=b: e.scalar_tensor_tensor(
                        out=Sf[:, hd, half, :], in0=Sf[:, hd, half, :], scalar=cdk[:, hd:hd + 1], in1=self.ps[b][:],
                        op0=ALU.mult, op1=ALU.add),
                        reads=[("A", "Sf", hd, half), ("A", "cdk"), ("ps", b)], writes=[("A", "Sf", hd, half)])
                    S.op(POOL, lambda e, hd=hd, half=half: e.tensor_copy(out=Sb[:, hd, half, :], in_=Sf[:, hd, half, :]),
                         reads=[("A", "Sf", hd, half)], writes=[("A", "Sb", hd, half)])

    DILS = (1, 4, 16)

    def attn_layer(self, x_src, x_dst, w_in, w_out, ropeA, acst, gi_pre, gi_post, dbg=9):
        T = self.T
        HD = [self.dscr(f"a_h{g}", [D, T], BF16) for g in range(3)]
        QT = [self.dscr(f"a_q{g}", [D, T], BF16) for g in range(3)]
        KT = [self.dscr(f"a_k{g}", [D, T], BF16) for g in range(3)]
        VE = [self.dscr(f"a_v{g}", [T + 128 * self.DILS[g], D], BF16) for g in range(3)]
        ND = [self.dscr(f"a_n{g}", [2, D, T], F32) for g in range(3)]
        self._attn_prep(x_src, gi_pre, HD)
        if dbg < 1:
            return
        for g in range(3):
            self._attn_inproj(g, HD[g], w_in, ropeA, acst, QT[g], KT[g], VE[g])
        if dbg < 2:
            return
        for g in range(3):
            self._attn_core(g, acst, QT[g], KT[g], VE[g], ND[g])
        if dbg < 3:
            return
        self._attn_out(ND, w_out, x_src, x_dst, gi_post)

    def _attn_prep(self, x_src, gi_pre, HD):
        S, T = self.S, self.T
        TT = 256
        self.new_phase()
        A = self.A
        Hn = A.tile([128, NCH, T], BF16, "Hn")
        Hp = A.tile([128, NCH, T], BF16, "Hp")
        xt = [A.tile([128, NCH, TT], F32, f"x{i}") for i in range(2)]
        sq = A.tile([128, NCH, TT], BF16, "sq")
        rt = [A.tile([128, TT], F32, f"r{i}") for i in range(2)]
        for ti in range(T // TT):
            t0 = ti * TT
            sl = ti % 2
            x, r = xt[sl], rt[sl]
            xres, rres = ("A", "x", sl), ("A", "r", sl)
            self.load_x(x_src, t0, TT, x, xres, ("xl", sl))
            self.rstd_of(x, [xres], sq, ("A", "sq"), r, rres, TT)
            for c in range(NCH):
                gsc = self.gs[:, gi_pre * NCH + c: gi_pre * NCH + c + 1]
                S.op(DVE, lambda e, c=c, gsc=gsc, x=x, r=r, t0=t0: e.scalar_tensor_tensor(
                    out=Hn[:, c, t0:t0 + TT], in0=x[:, c, :], scalar=gsc, in1=r[:], op0=ALU.mult, op1=ALU.mult),
                    reads=[xres, rres, "c_gs"], writes=[("A", "Hn", c)])
        for c in range(NCH):
            S.dma(SP, ("hs", c % 4), lambda e, c=c: e.dma_start(out=HD[0][c * 128:(c + 1) * 128, :], in_=Hn[:, c, :]),
                  reads=[("A", "Hn", c)], writes=[("SRC", HD[0].name, c)])
        for gi, dil in ((1, 4), (2, 16)):
            hp = Hp
            for c in range(NCH):
                eng = POOL if c % 2 == 0 else DVE
                S.op(eng, lambda e, c=c, hp=hp, dil=dil: e.tensor_copy(
                    out=hp[:, c, :].rearrange("p (r l) -> p r l", r=dil),
                    in_=Hn[:, c, :].rearrange("p (l r) -> p r l", r=dil)),
                    reads=[("A", "Hn", c)], writes=[("A", "Hp", c)])
                S.dma(SP, ("hs", c % 4), lambda e, c=c, hp=hp, gi=gi: e.dma_start(out=HD[gi][c * 128:(c + 1) * 128, :], in_=hp[:, c, :]),
                      reads=[("A", "Hp", c)], writes=[("SRC", HD[gi].name, c)])

    def _attn_inproj(self, g, Hg, w_in, ropeA, acst, QTg, KTg, VEg):
        S, T = self.S, self.T
        TT = 256
        dil = self.DILS[g]
        L = T // dil
        Lp = L + 128
        self.new_phase()
        A = self.A
        win = self.W.tile([128, NCH, 3072], BF16, "awin")
        self.load_w(win, w_in[:, g * 3072:(g + 1) * 3072], ("W", "awin"), "wl")
        win_res = [("W", "awin", c) for c in range(NCH)]
        pmf = A.tile([128, 128], F32, "pmf")
        pm = A.tile([128, 128], BF16, "pm")
        S.dma(SP, "pml", lambda e: e.dma_start(out=pmf[:], in_=acst[:, 0:128]), writes=[("A", "pmf")])
        S.op(DVE, lambda e: e.tensor_copy(out=pm[:], in_=pmf[:]), reads=[("A", "pmf")], writes=[("A", "pm")])
        zt = A.tile([64, D], BF16, "zt")
        S.op(POOL, lambda e: e.memset(zt[:], 0.0), writes=[("A", "zt")])
        for r in range(dil):
            for side in range(2):
                row = r * Lp + (0 if side == 0 else 64 + L)
                S.dma(SP, ("vz", side), lambda e, row=row: e.dma_start(out=VEg[row:row + 64, :], in_=zt[:]),
                      reads=[("A", "zt")], writes=[("SRC", VEg.name, "pad", r, side)])
        stop = getattr(self, "att_stop", 0)
        if stop == 1:
            return
        ht = [A.tile([128, NCH, TT], BF16, f"h{i}") for i in range(2)]
        rp = [A.tile([128, 2, TT], F32, f"rp{i}") for i in range(2)]
        raw = [A.tile([128, TT], BF16, f"raw{i}") for i in range(2)]
        t1 = [A.tile([128, TT], F32, f"t1{i}") for i in range(2)]
        t2 = [A.tile([128, TT], F32, f"t2{i}") for i in range(2)]
        ob = [A.tile([128, TT], BF16, f"ob{i}") for i in range(3)]
        vb = [A.tile([128, 512], BF16, f"vb{i}") for i in range(3)]
        Hv = Hg.rearrange("(c p) t -> p c t", p=128)
        n_o = 0
        n_v = 0
        for ti in range(T // TT):
            t0 = ti * TT
            sl = ti % 2
            h, rpt = ht[sl], rp[sl]
            hres, rpres = ("A", "h", sl), ("A", "rp", sl)
            S.dma(SP, ("hl", sl), lambda e, h=h, t0=t0: e.dma_start(out=h[:], in_=Hv[:, :, t0:t0 + TT]),
                  reads=[("SRC", Hg.name, c) for c in range(NCH)], writes=[hres])
            S.dma(SP, ("rpl", sl), lambda e, rpt=rpt, t0=t0: e.dma_start(out=rpt[:], in_=ropeA[g, :, :, t0:t0 + TT]), writes=[rpres])
            if stop == 2:
                return
            for qk in range(2):
                dst = QTg if qk == 0 else KTg
                for cc in range(NCH):
                    ba = self.bank()
                    col = qk * 1024 + cc * 128
                    for c in range(NCH):
                        S.op(PE, lambda e, ba=ba, col=col, c=c, h=h: e.matmul(
                            self.ps[ba][:, :TT], lhsT=win[:, c, col:col + 128], rhs=h[:, c, :],
                            start=(c == 0), stop=(c == NCH - 1)),
                            reads=[win_res[c], hres], writes=[("ps", ba)], inc=(c == NCH - 1))
                    k2 = n_o % 2
                    rw, a1, a2 = raw[k2], t1[k2], t2[k2]
                    o = ob[n_o % 3]
                    ores = ("A", "ob", n_o % 3)
                    n_o += 1
                    S.op(ACT, lambda e, ba=ba, rw=rw: e.activation(out=rw[:], in_=self.ps[ba][:, :TT], func=AF.Copy),
                         reads=[("ps", ba)], writes=[("A", "raw", k2)])
                    bb = self.bank()
                    S.op(PE, lambda e, bb=bb, rw=rw: e.matmul(self.ps[bb][:, :TT], lhsT=pm[:], rhs=rw[:], start=True, stop=True),
                         reads=[("A", "pm"), ("A", "raw", k2)], writes=[("ps", bb)])
                    S.op(DVE, lambda e, ba=ba, a1=a1, rpt=rpt: e.tensor_tensor(out=a1[:], in0=self.ps[ba][:, :TT], in1=rpt[:, 0, :], op=ALU.mult),
                         reads=[("ps", ba), rpres, ("A", "raw", k2)], writes=[("A", "t1", k2)])
                    S.op(DVE, lambda e, bb=bb, a2=a2, rpt=rpt: e.tensor_tensor(out=a2[:], in0=self.ps[bb][:, :TT], in1=rpt[:, 1, :], op=ALU.mult),
                         reads=[("ps", bb), rpres], writes=[("A", "t2", k2)])
                    S.op(POOL, lambda e, o=o, a1=a1, a2=a2: e.tensor_tensor(out=o[:], in0=a1[:], in1=a2[:], op=ALU.add),
                         reads=[("A", "t1", k2), ("A", "t2", k2)], writes=[ores])
                    S.dma(SP, ("qks", n_o % 3), lambda e, o=o, cc=cc, dst=dst, t0=t0: e.dma_start(
                        out=dst[cc * 128:(cc + 1) * 128, t0:t0 + TT], in_=o[:]),
                        reads=[ores], writes=[("SRC", dst.name, cc, ti)])
            if stop == 3:
                return
            for blk in range(TT // 128):
                pos = t0 + blk * 128
                r, l0 = pos // L, pos % L
                row = r * Lp + 64 + l0
                for grp in range(2):
                    b = self.bank()
                    col = 2048 + grp * 512
                    for c in range(NCH):
                        S.op(PE, lambda e, b=b, col=col, c=c, h=h, blk=blk: e.matmul(
                            self.ps[b][:, :], lhsT=h[:, c, blk * 128:(blk + 1) * 128], rhs=win[:, c, col:col + 512],
                            start=(c == 0), stop=(c == NCH - 1)),
                            reads=[win_res[c], hres], writes=[("ps", b)], inc=(c == NCH - 1))
                    v = vb[n_v % 3]
                    vres = ("A", "vb", n_v % 3)
                    n_v += 1
                    S.op(ACT, lambda e, b=b, v=v: e.activation(out=v[:], in_=self.ps[b][:, :], func=AF.Copy),
                         reads=[("ps", b)], writes=[vres])
                    S.dma(SP, ("vgs", n_v % 3), lambda e, v=v, row=row, grp=grp: e.dma_start(
                        out=VEg[row:row + 128, grp * 512:(grp + 1) * 512], in_=v[:]),
                        reads=[vres], writes=[("SRC", VEg.name, pos // 128, grp)])

    def _attn_core(self, g, acst, QTg, KTg, VEg, NDg):
        S, T = self.S, self.T
        dil = self.DILS[g]
        L = T // dil
        Lp = L + 128
        nqb = L // 128
        nkb = nqb + 1
        self.new_phase()
        A = self.A
        mbf = A.tile([128, 1024], F32, "mbf")
        mb = A.tile([128, 1024], BF16, "mb")
        S.dma(SP, "mbl", lambda e: e.dma_start(out=mbf[:], in_=acst[:, 128:1152]), writes=[("A", "mbf")])
        S.op(DVE, lambda e: e.tensor_copy(out=mb[:], in_=mbf[:]), reads=[("A", "mbf")], writes=[("A", "mb")])
        kt = [A.tile([64, 2, dil * Lp], BF16, f"kt{i}") for i in range(2)]
        qt = [A.tile([64, 2, T], BF16, f"qt{i}") for i in range(2)]
        ve = [A.tile([128, dil * nkb, 128], BF16, f"ve{i}") for i in range(2)]
        for i in range(2):
            S.op(POOL, lambda e, i=i: e.memset(kt[i][:], 0.0), writes=[("A", "kt", i)])
        pmt = [A.tile([128, 512], BF16, f"pm{i}") for i in range(2)]
        span = 128 * dil if dil > 1 else 512
        stg = [A.tile([64, 2, 2, span], F32, f"stg{i}") for i in range(2)]
        VEv = VEg.rearrange("(n p) f -> p n f", p=128)
        n_p = 0
        n_s = 0
        for cc in range(NCH):
            sl = cc % 2
            k, q, v = kt[sl], qt[sl], ve[sl]
            kres, qres, vres = ("A", "kt", sl), ("A", "qt", sl), ("A", "ve", sl)
            kdeps = [("SRC", KTg.name, cc, ti) for ti in range(T // 256)]
            qdeps = [("SRC", QTg.name, cc, ti) for ti in range(T // 256)]
            for hh in range(2):
                r0 = cc * 128 + hh * 64
                S.dma(SP, ("kl", sl), lambda e, k=k, hh=hh, r0=r0: e.dma_start(
                    out=k[:, hh, :].rearrange("p (r l) -> p r l", r=dil)[:, :, 64:64 + L],
                    in_=KTg[r0:r0 + 64, :].rearrange("p (r l) -> p r l", r=dil)), reads=kdeps, writes=[kres])
                S.dma(SP, ("ql", sl), lambda e, q=q, hh=hh, r0=r0: e.dma_start(out=q[:, hh, :], in_=QTg[r0:r0 + 64, :]),
                      reads=qdeps, writes=[qres])
            vdeps = [("SRC", VEg.name, pb, grp) for pb in range(T // 128) for grp in range(2)]
            vdeps += [("SRC", VEg.name, "pad", r, sd) for r in range(dil) for sd in range(2)]
            S.dma(SP, ("vl", sl), lambda e, v=v, cc=cc: e.dma_start(out=v[:], in_=VEv[:, :, cc * 128:(cc + 1) * 128]),
                  reads=vdeps, writes=[vres])
            nsp = T // span
            for sp_i in range(nsp):
                st = stg[n_s % 2]
                stres = ("A", "stg", n_s % 2)
                n_s += 1
                if dil > 1:
                    blocks = [(r, sp_i) for r in range(dil)]
                else:
                    blocks = [(0, sp_i * 4 + b4) for b4 in range(4)]
                for bi, (r, i) in enumerate(blocks):
                    var = 0
                    if i == 0:
                        var = 1
                    if i == nqb - 1:
                        var = 2 if var == 0 else 3
                    bs = self.bank()
                    qc = r * L + i * 128
                    for hh in range(2):
                        S.op(PE, lambda e, bs=bs, hh=hh, var=var: e.matmul(
                            self.ps[bs][:, hh * 256:(hh + 1) * 256], lhsT=self.ident[:], rhs=mb[:, var * 256:(var + 1) * 256],
                            start=True, stop=False, skip_group_check=True),
                            reads=["c_ident", ("A", "mb")], writes=[("ps", bs)], inc=False)
                        for kb in range(2):
                            kc = r * Lp + (i + kb) * 128
                            S.op(PE, lambda e, bs=bs, hh=hh, kb=kb, kc=kc, qc=qc, k=k, q=q: e.matmul(
                                self.ps[bs][:, hh * 256 + kb * 128:hh * 256 + (kb + 1) * 128],
                                lhsT=k[:, hh, kc:kc + 128], rhs=q[:, hh, qc:qc + 128],
                                start=False, stop=(kb == 1), skip_group_check=True),
                                reads=[kres, qres], writes=[("ps", bs)], inc=(hh == 1 and kb == 1))
                    pmx = pmt[n_p % 2]
                    pres = ("A", "pm", n_p % 2)
                    n_p += 1
                    S.op(ACT, lambda e, bs=bs, pmx=pmx: e.activation(out=pmx[:], in_=self.ps[bs][:], func=AF.Exp, scale=0.125),
                         reads=[("ps", bs)], writes=[pres])
                    bo = self.bank()
                    for hh in range(2):
                        for nd in range(2):
                            for kb in range(2):
                                kbi = r * nkb + i + kb
                                lhs = v[:, kbi, hh * 64:(hh + 1) * 64] if nd == 0 else self.ones[:, 0:64]
                                S.op(PE, lambda e, bo=bo, hh=hh, nd=nd, kb=kb, lhs=lhs, pmx=pmx: e.matmul(
                                    self.ps[bo][0:64, (hh * 2 + nd) * 128:(hh * 2 + nd + 1) * 128], lhsT=lhs,
                                    rhs=pmx[:, hh * 256 + kb * 128:hh * 256 + (kb + 1) * 128],
                                    start=(kb == 0), stop=(kb == 1)),
                                    reads=[vres, "c_ones", pres], writes=[("ps", bo)],
                                    inc=(hh == 1 and nd == 1 and kb == 1))
                    if dil > 1:
                        dstv = st[:].rearrange("p h n (l r) -> p (h n) l r", r=dil)[:, :, :, r]
                    else:
                        dstv = st[:].rearrange("p h n t -> p (h n) t")[:, :, bi * 128:(bi + 1) * 128]
                    S.op(DVE, lambda e, bo=bo, dstv=dstv: e.tensor_copy(
                        out=dstv, in_=self.ps[bo][0:64, :].rearrange("p (a q) -> p a q", a=4)),
                        reads=[("ps", bo)], writes=[stres])
                for hh in range(2):
                    for nd in range(2):
                        r0 = cc * 128 + hh * 64
                        S.dma(SP, ("nds", hh * 2 + nd), lambda e, st=st, hh=hh, nd=nd, r0=r0, sp_i=sp_i: e.dma_start(
                            out=NDg[nd, r0:r0 + 64, sp_i * span:(sp_i + 1) * span], in_=st[:, hh, nd, :]),
                            reads=[stres], writes=[("SRC", NDg.name, cc, sp_i, hh, nd)])

    def _attn_out(self, ND, w_out, x_src, x_dst, gi_post):
        S, T = self.S, self.T
        TT = 256
        self.new_phase()
        A = self.A
        wt = self.W.tile([128, NCH, D], BF16, "aw")
        self.load_w(wt, w_out, ("W", "aw"), "wl")
        nt = [[A.tile([128, NCH, TT], F32, f"n{g}{i}") for i in range(2)] for g in range(3)]
        dn = [[A.tile([128, NCH, TT], F32, f"d{g}{i}") for i in range(2)] for g in range(3)]
        ob = A.tile([128, NCH, TT], BF16, "ob")
        xt = [A.tile([128, NCH, TT], F32, f"x{i}") for i in range(2)]
        ft = A.tile([128, NCH, TT], F32, "f")
        sq = A.tile([128, NCH, TT], BF16, "sq")
        r2 = A.tile([128, TT], F32, "r2")
        tmp = [A.tile([128, TT], F32, f"t{i}") for i in range(2)]
        for ti in range(T // TT):
            t0 = ti * TT
            sl = ti % 2
            x, xres = xt[sl], ("A", "x", sl)
            self.load_x(x_src, t0, TT, x, xres, ("xl", sl))
            for g in range(3):
                dil = self.DILS[g]
                span = 128 * dil if dil > 1 else 512
                deps = [("SRC", ND[g].name, cc, t0 // span, hh, nd) for cc in range(NCH) for hh in range(2) for nd in range(2)]
                for nd, tl in ((0, nt[g][sl]), (1, dn[g][sl])):
                    src = ND[g][nd].rearrange("(c p) t -> p c t", p=128)
                    S.dma(SP, ("ndl", g, nd, sl), lambda e, tl=tl, src=src, t0=t0: e.dma_start(out=tl[:], in_=src[:, :, t0:t0 + TT]),
                          reads=deps, writes=[("A", "nd", g, nd, sl)])
            n0, n1, n2 = nt[0][sl], nt[1][sl], nt[2][sl]
            d0, d1, d2 = dn[0][sl], dn[1][sl], dn[2][sl]
            S.op(POOL, lambda e, n0=n0, n1=n1: e.tensor_tensor(out=n0[:], in0=n0[:], in1=n1[:], op=ALU.add),
                 reads=[("A", "nd", 0, 0, sl), ("A", "nd", 1, 0, sl)], writes=[("A", "nd", 0, 0, sl)])
            S.op(POOL, lambda e, n0=n0, n2=n2: e.tensor_tensor(out=n0[:], in0=n0[:], in1=n2[:], op=ALU.add),
                 reads=[("A", "nd", 0, 0, sl), ("A", "nd", 2, 0, sl)], writes=[("A", "nd", 0, 0, sl)])
            S.op(DVE, lambda e, d0=d0, d1=d1: e.tensor_tensor(out=d0[:], in0=d0[:], in1=d1[:], op=ALU.add),
                 reads=[("A", "nd", 0, 1, sl), ("A", "nd", 1, 1, sl)], writes=[("A", "nd", 0, 1, sl)])
            S.op(DVE, lambda e, d0=d0, d2=d2: e.tensor_tensor(out=d0[:], in0=d0[:], in1=d2[:], op=ALU.add),
                 reads=[("A", "nd", 0, 1, sl), ("A", "nd", 2, 1, sl)], writes=[("A", "nd", 0, 1, sl)])
            S.op(DVE, lambda e, d0=d0: e.reciprocal(out=d0[:], in_=d0[:]), reads=[("A", "nd", 0, 1, sl)], writes=[("A", "nd", 0, 1, sl)])
            S.op(DVE, lambda e, n0=n0, d0=d0: e.tensor_tensor(out=ob[:], in0=n0[:], in1=d0[:], op=ALU.mult),
                 reads=[("A", "nd", 0, 0, sl), ("A", "nd", 0, 1, sl)], writes=[("A", "ob")])
            for co in range(NCH):
                b = self.bank()
                for kc in range(NCH):
                    S.op(PE, lambda e, b=b, co=co, kc=kc: e.matmul(
                        self.ps[b][:, :TT], lhsT=wt[:, kc, co * 128:(co + 1) * 128], rhs=ob[:, kc, :],
                        start=(kc == 0), stop=(kc == NCH - 1)),
                        reads=[("W", "aw", kc), ("A", "ob")], writes=[("ps", b)], inc=(kc == NCH - 1))
                S.op(ACT, lambda e, b=b, co=co: e.activation(out=ft[:, co, :], in_=self.ps[b][:, :TT], func=AF.Copy),
                     reads=[("ps", b)], writes=[("A", "f", co)])
            self.postnorm_store(ft, ("A", "f"), x, xres, sq, ("A", "sq"), r2, ("A", "r2"),
                                tmp, ("A", "t"), gi_post, x_dst, t0, TT, False)

    def finish(self):
        self.S.final_wait(SP, self.out_toks)
        self.S.emit()
        return self.nc


SEQ = 4096
N_CORES = 8
DEPTH = 4


def _ret_tables(T):
    inv = (1.0 / (10000.0 ** np.linspace(0.0, 1.0, 128, dtype=np.float32))).astype(np.float32)
    ang = (np.arange(T, dtype=np.float32)[None, :] * inv[:, None]).astype(np.float32)
    cos, sin = np.cos(ang).astype(np.float32), np.sin(ang).astype(np.float32)
    rope = np.stack([cos, sin, cos / 16.0, sin / 16.0], axis=1).astype(np.float32)
    m = np.arange(128)[:, None]
    i = np.arange(128)[None, :]
    rc = np.zeros((128, 772), np.float32)
    rc[:, 0:128] = np.maximum(i - m, 0)
    rc[:, 128:256] = (i >= m)
    rc[:, 256:384] = np.maximum(m - i, 0)
    rc[:, 384:512] = (m > i)
    rc[:, 512:640] = i + 1
    rc[:, 640:768] = 128 - i
    rc[:, 768] = 127 - np.arange(128)
    rc[:, 769] = np.arange(128)
    rc[:, 770] = 128.0
    return np.ascontiguousarray(rope), rc


def _attn_tables(T):
    inv = (500000.0 ** (-np.arange(0, 16, 2, dtype=np.float32) / 16.0)).astype(np.float32)
    ropeA = np.zeros((3, 128, 2, T), np.float32)
    d = np.arange(128) % 64
    rot = d < 16
    for g, dil in enumerate(Prog.DILS):
        L = T // dil
        pos = np.arange(T)
        tok = (pos % L) * dil + pos // L
        ang = (tok.astype(np.float32)[None, :] * inv[:, None]).astype(np.float32)
        cos, sin = np.cos(ang).astype(np.float32), np.sin(ang).astype(np.float32)
        ropeA[g, :, 0, :] = 1.0
        ropeA[g, rot, 0, :] = cos[d[rot] % 8]
        ropeA[g, rot, 1, :] = sin[d[rot] % 8]
    ac = np.zeros((128, 1152), np.float32)
    for m in range(128):
        dd = m % 64
        if dd < 8:
            ac[m + 8, m] = -1.0
        elif dd < 16:
            ac[m - 8, m] = 1.0
    kk = np.arange(128)[:, None]
    qq = np.arange(128)[None, :]
    for var in range(4):
        v0 = (kk >= qq)
        v1 = (kk <= qq)
        if var in (1, 3):
            v0 = v0 & (kk >= 64)
        if var in (2, 3):
            v1 = v1 & (kk < 64)
        ac[:, 128 + var * 256:128 + var * 256 + 128] = np.where(v0, 0.0, -30000.0)
        ac[:, 128 + var * 256 + 128:128 + (var + 1) * 256] = np.where(v1, 0.0, -30000.0)
    return ropeA, ac


def build_program(T=SEQ, depth=DEPTH):
    P = Prog(T)
    xT = P.din("xT", [D, T])
    gT = P.din("gT", [128, 16 * NCH])
    cst = P.din("cst", [128, 128])
    ffn_w_in = P.din("ffn_w_in", [DEPTH, D, 2 * FFN_H])
    ffn_w_out = P.din("ffn_w_out", [DEPTH, FFN_H, D])
    ret_w_in = P.din("ret_w_in", [2, D, 6144])
    ret_w_out = P.din("ret_w_out", [2, 2048, D])
    dec = P.din("dec", [2, 128, 8])
    rope = P.din("rope", [128, 4, T])
    rcst = P.din("rcst", [128, 772])
    conv_w_in = P.din("conv_w_in", [D, 2 * D])
    conv_w_out = P.din("conv_w_out", [D, D])
    cvp = P.din("cvp", [128, 296])
    attn_w_in = P.din("attn_w_in", [D, 9216])
    attn_w_out = P.din("attn_w_out", [D, D])
    ropeA = P.din("ropeA", [3, 128, 2, T])
    acst = P.din("acst", [128, 1152])
    outT = P.dout("outT", [D, T])
    X = P.dscr("xres", [D, T], F32)
    P.consts(gT, cst)
    cur = xT
    for i in range(depth):
        kind, j = i % 3, i // 3
        if kind == 0:
            P.ret_layer(cur, X, ret_w_in[j], ret_w_out[j], dec[j], rope, rcst, 4 * i + 0, 4 * i + 1, i)
        elif kind == 1:
            P.conv_layer(cur, X, conv_w_in, conv_w_out, cvp, 4 * i + 0, 4 * i + 1)
        else:
            P.attn_layer(cur, X, attn_w_in, attn_w_out, ropeA, acst, 4 * i + 0, 4 * i + 1)
        last = (i == depth - 1)
        P.ffn_layer(X, outT if last else X, ffn_w_in[i], ffn_w_out[i], 4 * i + 2, 4 * i + 3, final=last)
        cur = X
    return P.finish()


def _pl(v):
    return np.asarray(v, np.float32).reshape(-1, 128).T


def kernel(x, norm_w, ffn_w_in, ffn_w_out, ret_w_in, ret_log1m_decay, ret_w_out,
           conv_w_in, conv_b_in, conv_w_dw, conv_b_dw, conv_ln_g, conv_ln_b,
           conv_w_out, conv_b_out, attn_w_in, attn_w_out):
    f = lambda a: np.ascontiguousarray(np.asarray(a, dtype=np.float32))
    x = f(x)
    B, T, _ = x.shape
    nc = build_program(T)
    rope, rcst = _ret_tables(T)
    ropeA, acst = _attn_tables(T)
    norm_w = f(norm_w)
    gT = f(norm_w.reshape(16, NCH, 128).transpose(2, 0, 1).reshape(128, 16 * NCH))
    dec = f(np.broadcast_to(f(ret_log1m_decay).reshape(2, 1, 8), (2, 128, 8)))
    cvp = f(np.concatenate([_pl(f(conv_b_in)[0]), _pl(f(conv_b_dw)[0]), _pl(f(conv_ln_g)[0]), _pl(f(conv_ln_b)[0]),
                            _pl(f(conv_b_out)[0]),
                            f(conv_w_dw)[0].reshape(31, NCH, 128).transpose(2, 0, 1).reshape(128, 248)], axis=1))
    shared = {
        "gT": gT, "cst": np.eye(128, dtype=np.float32),
        "ffn_w_in": f(ffn_w_in), "ffn_w_out": f(ffn_w_out),
        "ret_w_in": f(ret_w_in), "ret_w_out": f(ret_w_out), "dec": dec, "rope": rope, "rcst": rcst,
        "conv_w_in": f(conv_w_in)[0], "conv_w_out": f(conv_w_out)[0], "cvp": cvp,
        "attn_w_in": f(attn_w_in)[0], "attn_w_out": f(attn_w_out)[0], "ropeA": ropeA, "acst": acst,
    }
    in_maps = []
    for b in range(B):
        m = dict(shared)
        m["xT"] = np.ascontiguousarray(x[b].T)
        in_maps.append(m)
    res = run_bass_kernel_spmd(nc, in_maps, core_ids=list(range(B)))
    out = np.stack([np.ascontiguousarray(r["outT"].T) for r in res.results], axis=0)
    return out.astype(np.float32)
```

```python
import math
import numpy as np
import concourse.bass as bass
import concourse.mybir as mybir
from concourse.bass_utils import run_bass_kernel_spmd

F32 = mybir.dt.float32
BF16 = mybir.dt.bfloat16
ALU = mybir.AluOpType
AF = mybir.ActivationFunctionType
AX = mybir.AxisListType

D = 1024
NCH = D // 128
FFN_H = 2816
RMS_EPS = 1e-6
LN_EPS = 1e-5

PE, ACT, DVE, POOL, SP = "pe", "act", "dve", "pool", "sp"
ENGS = (PE, ACT, DVE, POOL, SP)


class Sched:
    def __init__(self, nc):
        self.nc = nc
        self.ops = {e: [] for e in ENGS}
        self.cnt = {e: 0 for e in ENGS}
        self.sems = {e: nc.alloc_semaphore(f"s_{e}") for e in ENGS}
        self.dsems = {}
        self.seen = {}
        self.res = {}
        self.floor = {}
        self.n_ops = 0

    def _res(self, r):
        st = self.res.get(r)
        if st is None:
            arena = r[0] if isinstance(r, tuple) else None
            st = {"w": None, "r": dict(self.floor.get(arena, {}))}
            self.res[r] = st
        return st

    def end_phase(self, arena):
        fl = dict(self.floor.get(arena, {}))
        dead = []
        for r, st in self.res.items():
            if isinstance(r, tuple) and r[0] == arena:
                if st["w"] is not None:
                    k, v = st["w"]
                    fl[k] = max(fl.get(k, 0), v)
                for k, v in st["r"].items():
                    fl[k] = max(fl.get(k, 0), v)
                dead.append(r)
        for r in dead:
            del self.res[r]
        self.floor[arena] = fl

    def _semh(self, key):
        if key in self.sems:
            return self.sems[key]
        return self.dsems[key][0]

    def _collect(self, eng, reads, writes, is_dma):
        deps = {}

        def add(tok, same_ok):
            if tok is None:
                return
            k, v = tok
            if same_ok and (not is_dma) and k == eng:
                return
            if v > deps.get(k, 0):
                deps[k] = v

        for r in reads:
            st = self._res(r)
            add(st["w"], False)
            if isinstance(r, tuple) and r[0] == "ps":
                for k, v in st["r"].items():
                    add((k, v), True)
        for w in writes:
            st = self._res(w)
            add(st["w"], True)
            for k, v in st["r"].items():
                add((k, v), True)
        waits = []
        for k, v in deps.items():
            if self.seen.get((eng, k), 0) < v:
                self.seen[(eng, k)] = v
                waits.append((k, v))
        return waits

    def _commit(self, tok, reads, writes):
        k, v = tok
        for r in reads:
            st = self._res(r)
            if v > st["r"].get(k, 0):
                st["r"][k] = v
        for w in writes:
            st = self._res(w)
            st["w"] = tok
            st["r"] = {}

    def op(self, eng, fn, reads=(), writes=(), inc=True):
        waits = self._collect(eng, reads, writes, False)
        if inc:
            self.cnt[eng] += 1
            tok = (eng, self.cnt[eng])
        else:
            tok = (eng, self.cnt[eng] + 1)
        self.ops[eng].append((waits, fn, (eng, 1) if inc else None))
        self._commit(tok, reads, writes)
        self.n_ops += 1
        return tok

    def dma(self, q, key, fn, reads=(), writes=()):
        if key not in self.dsems:
            self.dsems[key] = [self.nc.alloc_semaphore(f"d_{len(self.dsems)}"), 0]
        ds = self.dsems[key]
        waits = self._collect(q, reads, writes, True)
        if ds[1] > 0 and self.seen.get((q, key), 0) < ds[1]:
            self.seen[(q, key)] = ds[1]
            waits.append((key, ds[1]))
        ds[1] += 16
        tok = (key, ds[1])
        self.ops[q].append((waits, fn, (key, 16)))
        self._commit(tok, reads, writes)
        self.n_ops += 1
        return tok

    def final_wait(self, eng, toks):
        waits = []
        for k, v in toks:
            if self.seen.get((eng, k), 0) < v:
                self.seen[(eng, k)] = v
                waits.append((k, v))
        self.ops[eng].append((waits, None, None))

    def emit(self):
        nc = self.nc
        with nc.Block() as block:
            def run(e, name):
                for waits, fn, inc in self.ops[name]:
                    if fn is None:
                        for k, v in waits:
                            e.wait_ge(self._semh(k), v)
                        continue
                    for k, v in waits[:-1]:
                        e.wait_ge(self._semh(k), v)
                    ins = fn(e)
                    if waits:
                        ins._wait_ge(self._semh(waits[-1][0]), waits[-1][1])
                    if inc is not None:
                        ins.then_inc(self._semh(inc[0]), inc[1])

            @block.tensor
            def _(e):
                run(e, PE)

            @block.scalar
            def _(e):
                run(e, ACT)

            @block.vector
            def _(e):
                run(e, DVE)

            @block.gpsimd
            def _(e):
                run(e, POOL)

            @block.sync
            def _(e):
                run(e, SP)


class Arena:
    def __init__(self, nc, name, base, size):
        self.nc, self.name, self.base, self.size = nc, name, base, size
        self.off = 0
        self.uid = 0

    def reset(self):
        self.off = 0

    def tile(self, shape, dtype, tag):
        nbytes = int(np.prod(shape[1:])) * (2 if dtype == BF16 else 4)
        nbytes = (nbytes + 63) // 64 * 64
        assert self.off + nbytes <= self.size, (self.name, tag, self.off, nbytes, self.size)
        self.uid += 1
        t = self.nc.alloc_sbuf_tensor_at(f"{self.name}_{tag}_{self.uid}", list(shape), dtype,
                                         offset=self.base + self.off)
        self.off += nbytes
        return t


SB_BASE = 16512
C_SIZE = 12 * 1024
M_SIZE = 229344 - SB_BASE - C_SIZE


class Prog:
    def __init__(self, T):
        self.T = T
        nc = bass.Bass("TRN2", target_bir_lowering=False)
        self.nc = nc
        self.S = Sched(nc)
        self.C = Arena(nc, "C", SB_BASE, C_SIZE)
        self.W = self.A = Arena(nc, "M", SB_BASE + C_SIZE, M_SIZE)
        self.ps = [nc.alloc_psum_tensor(f"ps{i}", [128, 512], F32) for i in range(8)]
        self.psb = [t.bitcast(BF16) for t in self.ps]
        self.ps_rr = 0
        self.dram = {}
        self.out_toks = []

    def din(self, name, shape, dtype=F32):
        t = self.nc.dram_tensor(name, list(shape), dtype, kind="ExternalInput")
        self.dram[name] = t
        return t

    def dout(self, name, shape, dtype=F32):
        t = self.nc.dram_tensor(name, list(shape), dtype, kind="ExternalOutput")
        self.dram[name] = t
        return t

    def dscr(self, name, shape, dtype):
        t = self.nc.dram_tensor(name, list(shape), dtype)
        self.dram[name] = t
        return t

    def bank(self):
        i = self.ps_rr
        self.ps_rr = (self.ps_rr + 1) % 8
        return i

    def new_phase(self):
        S = self.S
        S.end_phase("W")
        S.end_phase("A")
        fl = dict(S.floor.get("W", {}))
        for k, v in S.floor.get("A", {}).items():
            fl[k] = max(fl.get(k, 0), v)
        S.floor["W"] = dict(fl)
        S.floor["A"] = dict(fl)
        self.A.reset()

    def consts(self, gT, cst):
        S, C = self.S, self.C
        identf = C.tile([128, 128], F32, "identf")
        S.dma(SP, "c_id", lambda e: e.dma_start(out=identf[:], in_=cst[:, 0:128]), writes=["c_identf"])
        self.ones = C.tile([128, 128], BF16, "ones")
        self.ident = C.tile([128, 128], BF16, "ident")
        self.gs = C.tile([128, 16 * NCH], F32, "gs")
        onesf = C.tile([128, 128], F32, "onesf")
        self.epsD = C.tile([128, 2], F32, "epsD")
        S.op(DVE, lambda e: e.memset(self.epsD[:, 0:1], float(D * RMS_EPS)), writes=["c_eps"])
        S.op(DVE, lambda e: e.memset(onesf[:], 1.0), writes=["c_onesf"])
        S.op(DVE, lambda e: e.tensor_copy(out=self.ones[:], in_=onesf[:]), reads=["c_onesf"], writes=["c_ones"])
        S.op(DVE, lambda e: e.tensor_copy(out=self.ident[:], in_=identf[:]), reads=["c_identf"], writes=["c_ident"])
        S.dma(SP, "c_g", lambda e: e.dma_start(out=self.gs[:], in_=gT[:, :]), writes=["c_g0"])
        S.op(DVE, lambda e: e.tensor_scalar(out=self.gs[:], in0=self.gs[:], scalar1=float(math.sqrt(D)),
                                            scalar2=None, op0=ALU.mult), reads=["c_g0"], writes=["c_gs"])

    def load_w(self, dst, w_ap, res, key):
        KC = dst.shape[1]
        for c in range(KC):
            self.S.dma(POOL, (key, c % 4), lambda e, c=c: e.dma_start(out=dst[:, c, :], in_=w_ap[c * 128:(c + 1) * 128, :]),
                       writes=[res + (c,)])

    def rstd_of(self, src, srcres, sq, sqres, r, rres, TT):
        S = self.S
        S.op(ACT, lambda e: e.activation(out=sq[:, :, :TT], in_=src[:, :, :TT], func=AF.Square),
             reads=srcres, writes=[sqres])
        b = self.bank()
        for c in range(NCH):
            S.op(PE, lambda e, c=c: e.matmul(self.ps[b][:, :TT], lhsT=self.ones[:], rhs=sq[:, c, :TT],
                                             start=(c == 0), stop=(c == NCH - 1)),
                 reads=["c_ones", sqres], writes=[("ps", b)], inc=(c == NCH - 1))
        S.op(ACT, lambda e: e.activation(out=r[:, :TT], in_=self.ps[b][:, :TT], func=AF.Sqrt, bias=self.epsD[:, 0:1]),
             reads=[("ps", b), "c_eps"], writes=[rres])
        S.op(DVE, lambda e: e.reciprocal(out=r[:, :TT], in_=r[:, :TT]), reads=[rres], writes=[rres])

    def load_x(self, x_src, t0, TT, x, xres, key):
        xv = x_src.rearrange("(c p) t -> p c t", p=128)
        self.S.dma(SP, key, lambda e: e.dma_start(out=x[:, :, :TT], in_=xv[:, :, t0:t0 + TT]),
                   reads=[("X", x_src.name, t0, c) for c in range(NCH)], writes=[xres])

    def prenorm(self, x, xres, h, hres, sq, sqres, r, rres, gi, TT):
        S = self.S
        self.rstd_of(x, [xres], sq, sqres, r, rres, TT)
        for c in range(NCH):
            g = self.gs[:, gi * NCH + c: gi * NCH + c + 1]
            S.op(DVE, lambda e, c=c, g=g: e.scalar_tensor_tensor(
                out=h[:, c, :TT], in0=x[:, c, :TT], scalar=g, in1=r[:, :TT], op0=ALU.mult, op1=ALU.mult),
                reads=[xres, rres, "c_gs"], writes=[hres + (c,)])

    def postnorm_store(self, ft, fkey, x, xres, sq, sqres, r2, r2res, tmp, tkey, gi, x_dst, t0, TT, final):
        S = self.S
        fres = [fkey + (c,) for c in range(NCH)]
        self.rstd_of(ft, fres, sq, sqres, r2, r2res, TT)
        for c in range(NCH):
            g = self.gs[:, gi * NCH + c: gi * NCH + c + 1]
            tm = tmp[c % len(tmp)]
            tres = tkey + (c % len(tmp),)
            S.op(POOL, lambda e, c=c, tm=tm: e.tensor_tensor(out=tm[:, :TT], in0=ft[:, c, :TT], in1=r2[:, :TT], op=ALU.mult),
                 reads=[fres[c], r2res], writes=[tres])
            S.op(DVE, lambda e, c=c, g=g, tm=tm: e.scalar_tensor_tensor(
                out=ft[:, c, :TT], in0=tm[:, :TT], scalar=g, in1=x[:, c, :TT], op0=ALU.mult, op1=ALU.add),
                reads=[tres, xres, "c_gs"], writes=[fres[c]])
            tk = S.dma(SP, ("xs", c), lambda e, c=c: e.dma_start(
                out=x_dst[c * 128:(c + 1) * 128, t0:t0 + TT], in_=ft[:, c, :TT]),
                reads=[fres[c]], writes=[("X", x_dst.name, t0, c)])
            if final:
                self.out_toks.append(tk)

    def ffn_layer(self, x_src, x_dst, w_in, w_out, gi_pre, gi_post, final=False):
        S, T = self.S, self.T
        TT = 256
        NJ = FFN_H // 128
        self.new_phase()
        win = self.W.tile([128, NCH, 2 * FFN_H], BF16, "win")
        wout = self.W.tile([128, NJ, D], BF16, "wout")
        self.load_w(win, w_in, ("W", "win"), "wl")
        self.load_w(wout, w_out, ("W", "wout"), "wl")
        A = self.A
        xt = [A.tile([128, NCH, TT], F32, f"x{i}") for i in range(2)]
        ht = [A.tile([128, NCH, TT], BF16, f"h{i}") for i in range(2)]
        sq = A.tile([128, NCH, TT], BF16, "sq")
        rt = [A.tile([128, TT], F32, f"r{i}") for i in range(2)]
        st = [A.tile([128, TT], F32, f"s{i}") for i in range(4)]
        ut = A.tile([128, NJ, TT], BF16, "u")
        ft = A.tile([128, NCH, TT], F32, "f")
        tmp = [A.tile([128, TT], F32, f"t{i}") for i in range(2)]
        win_res = [("W", "win", c) for c in range(NCH)]
        wout_res = [("W", "wout", j) for j in range(NJ)]
        for ti in range(T // TT):
            t0 = ti * TT
            sl = ti % 2
            x, h, r = xt[sl], ht[sl], rt[sl]
            xres, hres, rres = ("A", "x", sl), ("A", "h", sl), ("A", "r", sl)
            if ti == 0:
                self.load_x(x_src, 0, TT, xt[0], ("A", "x", 0), ("xl", 0))
            if ti + 1 < T // TT:
                self.load_x(x_src, t0 + TT, TT, xt[1 - sl], ("A", "x", 1 - sl), ("xl", 1 - sl))
            self.prenorm(x, xres, h, hres, sq, ("A", "sq"), r, rres, gi_pre, TT)
            for j in range(NJ):
                b = self.bank()
                for half in range(2):
                    col = half * FFN_H + j * 128
                    for c in range(NCH):
                        S.op(PE, lambda e, b=b, half=half, col=col, c=c, h=h: e.matmul(
                            self.ps[b][:, half * TT:(half + 1) * TT], lhsT=win[:, c, col:col + 128], rhs=h[:, c, :],
                            start=(c == 0), stop=(c == NCH - 1)),
                            reads=[win_res[c], hres + (c,)], writes=[("ps", b)], inc=(half == 1 and c == NCH - 1))
                s = st[j % 4]
                sres = ("A", "s", j % 4)
                S.op(ACT, lambda e, b=b, s=s: e.activation(out=s[:], in_=self.ps[b][:, 0:TT], func=AF.Silu),
                     reads=[("ps", b)], writes=[sres])
                S.op(DVE, lambda e, b=b, s=s, j=j: e.tensor_tensor(out=ut[:, j, :], in0=self.ps[b][:, TT:2 * TT], in1=s[:],
                                                                   op=ALU.mult),
                     reads=[("ps", b), sres], writes=[("A", "u", j)])
            for c in range(NCH):
                b = self.bank()
                for j in range(NJ):
                    S.op(PE, lambda e, b=b, c=c, j=j: e.matmul(
                        self.ps[b][:, :TT], lhsT=wout[:, j, c * 128:(c + 1) * 128], rhs=ut[:, j, :],
                        start=(j == 0), stop=(j == NJ - 1)),
                        reads=[wout_res[j], ("A", "u", j)], writes=[("ps", b)], inc=(j == NJ - 1))
                S.op(ACT, lambda e, b=b, c=c: e.activation(out=ft[:, c, :], in_=self.ps[b][:, :TT], func=AF.Copy),
                     reads=[("ps", b)], writes=[("A", "f", c)])
            self.postnorm_store(ft, ("A", "f"), x, xres, sq, ("A", "sq"), rt[1 - sl], ("A", "r", 1 - sl),
                                tmp, ("A", "t"), gi_post, x_dst, t0, TT, final)

    def conv_layer(self, x_src, x_dst, w_in, w_out, cvp, gi_pre, gi_post, dbg=0):
        U = self.dscr("conv_u", [D, self.T + 32], BF16)
        self._conv_p1(x_src, w_in, cvp, gi_pre, U, dbg)
        if dbg == 1:
            return
        self._conv_p2(x_src, x_dst, w_out, cvp, gi_post, U, dbg)

    def _conv_p1(self, x_src, w_in, cvp, gi_pre, U, dbg):
        S, T = self.S, self.T
        TT = 256
        KW, PAD = 31, 15
        import os
        sk = os.environ.get("SKIP", "")
        self.new_phase()
        A = self.A
        win = self.W.tile([128, NCH, 2 * D], BF16, "win")
        if "M" in sk:
            for c in range(NCH):
                S.op(DVE, lambda e, c=c: e.memset(win[:, c, :], 1e30), writes=[("W", "win", c)])
        self.load_w(win, w_in, ("W", "win"), "wl")
        cp = A.tile([128, 296], F32, "cp")
        S.dma(SP, "cpl", lambda e: e.dma_start(out=cp[:], in_=cvp[:, :]), writes=[("A", "cp")])
        zt = A.tile([128, 16], BF16, "z")
        S.op(DVE, lambda e: e.memset(zt[:], 0.0), writes=[("A", "z")])
        for c in range(NCH if "P" not in sk else 0):
            for side in range(2):
                col = 0 if side == 0 else T + 16
                S.dma(SP, ("uz", side), lambda e, c=c, col=col: e.dma_start(
                    out=U[c * 128:(c + 1) * 128, col:col + 16], in_=zt[:]), reads=[("A", "z")], writes=[("U", "pad", side, c)])
        xt = [A.tile([128, NCH, TT], F32, f"x{i}") for i in range(2)]
        ht = [A.tile([128, NCH, TT], BF16, f"h{i}") for i in range(2)]
        sq = A.tile([128, NCH, TT], BF16, "sq")
        rt = [A.tile([128, TT], F32, f"r{i}") for i in range(2)]
        st = [A.tile([128, TT], F32, f"s{i}") for i in range(4)]
        ut = [A.tile([128, NCH, TT], BF16, f"u{i}") for i in range(2)]
        win_res = [("W", "win", c) for c in range(NCH)]
        nt = T // TT
        for ti in range(nt):
            t0 = ti * TT
            sl = ti % 2
            x, h, r, u = xt[sl], ht[sl], rt[sl], ut[sl]
            xres, hres, rres = ("A", "x", sl), ("A", "h", sl), ("A", "r", sl)
            if ti == 0:
                self.load_x(x_src, 0, TT, xt[0], ("A", "x", 0), ("xl", 0))
            if ti + 1 < nt:
                self.load_x(x_src, t0 + TT, TT, xt[1 - sl], ("A", "x", 1 - sl), ("xl", 1 - sl))
            self.prenorm(x, xres, h, hres, sq, ("A", "sq"), r, rres, gi_pre, TT)
            for j in range(NCH):
                b = self.bank()
                for half in range(2):
                    col = half * D + j * 128
                    for c in range(NCH):
                        S.op(PE, lambda e, b=b, half=half, col=col, c=c, h=h: e.matmul(
                            self.ps[b][:, half * TT:(half + 1) * TT], lhsT=win[:, c, col:col + 128], rhs=h[:, c, :],
                            start=(c == 0), stop=(c == NCH - 1)),
                            reads=[win_res[c], hres + (c,)], writes=[("ps", b)], inc=(half == 1 and c == NCH - 1))
                sg = st[j % 4]
                sres = ("A", "s", j % 4)
                S.op(ACT, lambda e, b=b, sg=sg, j=j: e.activation(out=sg[:], in_=self.ps[b][:, TT:2 * TT], func=AF.Sigmoid,
                                                                 bias=cp[:, 8 + j:9 + j]),
                     reads=[("ps", b), ("A", "cp")], writes=[sres])
                S.op(DVE, lambda e, b=b, sg=sg, j=j, u=u: e.scalar_tensor_tensor(
                    out=u[:, j, :], in0=self.ps[b][:, 0:TT], scalar=cp[:, j:j + 1], in1=sg[:], op0=ALU.add, op1=ALU.mult),
                    reads=[("ps", b), sres, ("A", "cp")], writes=[("A", "u", sl, j)])
                if dbg == 2:
                    S.op(DVE, lambda e: e.memset(zt[:, 0:1], 0.0), writes=[("A", "u", sl, j)])
                S.dma(SP, "us", lambda e, j=j, u=u, t0=t0: e.dma_start(
                    out=U[j * 128:(j + 1) * 128, 16 + t0:16 + t0 + TT], in_=u[:, j, :]),
                    reads=[("A", "u", sl, j)] + ([("A", "s", (j + 1) % 4)] if "x" in sk else []), writes=[("U", ti, j)])
                if dbg == 11:
                    dt_ = st[(j + 2) % 4]
                    S.op(POOL, lambda e, j=j, u=u, dt_=dt_: e.tensor_copy(out=dt_[:], in_=u[:, j, :]), reads=[("A", "u", sl, j)], writes=[("A", "s", (j + 2) % 4)])
                    S.dma(SP, ("xs", j), lambda e, j=j, dt_=dt_, t0=t0: e.dma_start(out=x_dst[j * 128:(j + 1) * 128, t0:t0 + TT], in_=dt_[:]),
                          reads=[("A", "s", (j + 2) % 4)], writes=[("X", t0, j)])
    def _conv_p2(self, x_src, x_dst, w_out, cvp, gi_post, U, dbg):
        S, T = self.S, self.T
        TT = 256
        KW, PAD = 31, 15
        nt = T // TT
        import os
        sk = os.environ.get("SKIP", "")
        self.new_phase()
        A = self.A
        uh = [A.tile([128, NCH, TT + 32], BF16, f"uh{i}") for i in range(2)]
        dbt = A.tile([128, NCH, TT], F32, "dbt") if "L" in sk else None
        cp = A.tile([128, 296], F32, "cp")
        S.dma(SP, "cpl", lambda e: e.dma_start(out=cp[:], in_=cvp[:, :]), writes=[("A", "cp")])
        wout = self.W.tile([128, NCH, D], BF16, "wout")
        import os
        sk = os.environ.get("SKIP", "")
        if "w" not in sk:
            self.load_w(wout, w_out, ("W", "wout"), "wl")
        dg = self.W.tile([128, KW * NCH, 128], BF16, "dg")
        for k in range(KW * NCH if "d" not in sk else 0):
            S.op(DVE, lambda e, k=k: e.tensor_scalar(out=dg[:, k, :], in0=self.ident[:], scalar1=cp[:, 48 + k:49 + k],
                                                     scalar2=None, op0=ALU.mult),
                 reads=["c_ident", ("A", "cp")], writes=[("W", "dg", k)])
        epsl = A.tile([128, 1], F32, "epsl")
        S.op(DVE, lambda e: e.memset(epsl[:], float(LN_EPS)), writes=[("A", "epsl")])
        xt = [A.tile([128, NCH, TT], F32, f"x{i}") for i in range(2)]
        vt = A.tile([128, NCH, TT], F32, "v")
        vb = A.tile([128, NCH, TT], BF16, "vb")
        sq = A.tile([128, NCH, TT], BF16, "sq")
        mt = A.tile([128, TT], F32, "m")
        msq = A.tile([128, TT], F32, "msq")
        var = A.tile([128, TT], F32, "var")
        rs = A.tile([128, TT], F32, "rs")
        tmp = [A.tile([128, TT], F32, f"t{i}") for i in range(2)]
        zb = A.tile([128, NCH, TT], BF16, "zb")
        ft = A.tile([128, NCH, TT], F32, "f")
        r2 = A.tile([128, TT], F32, "r2")
        sq2 = A.tile([128, NCH, TT], BF16, "sq2")
        wout_res = [("W", "wout", c) for c in range(NCH)]
        for ti in range(nt):
            t0 = ti * TT
            sl = ti % 2
            x, xres = xt[sl], ("A", "x", sl)
            u, ures = uh[sl], ("A", "uh", sl)

            def loads(tj):
                s2 = tj % 2
                tb = tj * TT
                self.load_x(x_src, tb, TT, xt[s2], ("A", "x", s2), ("xl", s2))
                deps = [("U", tk, c) for tk in (tj - 1, tj, tj + 1) if 0 <= tk < nt for c in range(NCH)]
                deps += [("U", "pad", sd, c) for sd in range(2) for c in range(NCH)]
                Uv = U.rearrange("(c p) t -> p c t", p=128)
                S.dma(SP, ("uhl", s2), lambda e, u2=uh[s2], tb=tb: e.dma_start(out=u2[:], in_=Uv[:, :, tb:tb + TT + 32]),
                      reads=deps, writes=[("A", "uh", s2)])
            if ti == 0:
                loads(0)
            if ti + 1 < nt:
                loads(ti + 1)
            if dbg == 2:
                for c in range(NCH):
                    fx = dbt if dbt is not None else ft
                    S.op(DVE, lambda e, c=c, u=u, fx=fx: e.tensor_copy(out=fx[:, c, :], in_=u[:, c, 16:16 + TT]), reads=[ures], writes=[("A", "f", c)])
                    S.dma(SP, ("xs", c), lambda e, c=c, fx=fx: e.dma_start(out=x_dst[c * 128:(c + 1) * 128, t0:t0 + TT], in_=fx[:, c, :]),
                          reads=[("A", "f", c)], writes=[("X", t0, c)])
                return
            for c in range(NCH):
                b = self.bank()
                for k in range(KW):
                    S.op(PE, lambda e, b=b, c=c, k=k, u=u: e.matmul(
                        self.ps[b][:, :TT], lhsT=dg[:, k * NCH + c, :], rhs=u[:, c, k + 1:k + 1 + TT],
                        start=(k == 0), stop=(k == KW - 1)),
                        reads=[("W", "dg", k * NCH + c), ures], writes=[("ps", b)], inc=(k == KW - 1))
                S.op(ACT, lambda e, b=b, c=c: e.activation(out=vt[:, c, :], in_=self.ps[b][:, :TT], func=AF.Identity,
                                                           bias=cp[:, 16 + c:17 + c]),
                     reads=[("ps", b), ("A", "cp")], writes=[("A", "v", c)])
                S.op(POOL, lambda e, c=c: e.tensor_copy(out=vb[:, c, :], in_=vt[:, c, :]),
                     reads=[("A", "v", c)], writes=[("A", "vb", c)])
                S.op(ACT, lambda e, c=c: e.activation(out=sq[:, c, :], in_=vt[:, c, :], func=AF.Square),
                     reads=[("A", "v", c)], writes=[("A", "sq", c)])
            if dbg == 3:
                for c in range(NCH):
                    S.dma(SP, ("xs", c), lambda e, c=c: e.dma_start(out=x_dst[c * 128:(c + 1) * 128, t0:t0 + TT], in_=vt[:, c, :]),
                          reads=[("A", "v", c)], writes=[("X", t0, c)])
                return
            b = self.bank()
            for half, (src, key) in enumerate(((vb, "vb"), (sq, "sq"))):
                for c in range(NCH):
                    S.op(PE, lambda e, b=b, half=half, src=src, c=c: e.matmul(
                        self.ps[b][:, half * TT:(half + 1) * TT], lhsT=self.ones[:], rhs=src[:, c, :],
                        start=(c == 0), stop=(c == NCH - 1)),
                        reads=["c_ones", ("A", key, c)], writes=[("ps", b)], inc=(half == 1 and c == NCH - 1))
            S.op(DVE, lambda e, b=b: e.tensor_scalar(out=mt[:], in0=self.ps[b][:, 0:TT], scalar1=1.0 / D, scalar2=None,
                                                     op0=ALU.mult), reads=[("ps", b)], writes=[("A", "m")])
            S.op(POOL, lambda e: e.tensor_tensor(out=msq[:], in0=mt[:], in1=mt[:], op=ALU.mult),
                 reads=[("A", "m")], writes=[("A", "msq")])
            S.op(DVE, lambda e, b=b: e.scalar_tensor_tensor(out=var[:], in0=self.ps[b][:, TT:2 * TT], scalar=1.0 / D, in1=msq[:],
                                                            op0=ALU.mult, op1=ALU.subtract),
                 reads=[("ps", b), ("A", "msq")], writes=[("A", "var")])
            S.op(ACT, lambda e: e.activation(out=rs[:], in_=var[:], func=AF.Sqrt, bias=epsl[:, 0:1]),
                 reads=[("A", "var"), ("A", "epsl")], writes=[("A", "rs")])
            S.op(DVE, lambda e: e.reciprocal(out=rs[:], in_=rs[:]), reads=[("A", "rs")], writes=[("A", "rs")])
            for c in range(NCH):
                tm = tmp[c % 2]
                tres = ("A", "t", c % 2)
                S.op(POOL, lambda e, c=c, tm=tm: e.tensor_tensor(out=tm[:], in0=vt[:, c, :], in1=mt[:], op=ALU.subtract),
                     reads=[("A", "v", c), ("A", "m")], writes=[tres])
                S.op(DVE, lambda e, tm=tm: e.tensor_tensor(out=tm[:], in0=tm[:], in1=rs[:], op=ALU.mult),
                     reads=[tres, ("A", "rs")], writes=[tres])
                S.op(ACT, lambda e, c=c, tm=tm: e.activation(out=zb[:, c, :], in_=tm[:], func=AF.Silu,
                                                             scale=cp[:, 24 + c:25 + c], bias=cp[:, 32 + c:33 + c]),
                     reads=[tres, ("A", "cp")], writes=[("A", "zb", c)])
            if dbg == 4:
                for c in range(NCH):
                    S.op(DVE, lambda e, c=c: e.tensor_copy(out=ft[:, c, :], in_=zb[:, c, :]), reads=[("A", "zb", c)], writes=[("A", "f", c)])
                    S.dma(SP, ("xs", c), lambda e, c=c: e.dma_start(out=x_dst[c * 128:(c + 1) * 128, t0:t0 + TT], in_=ft[:, c, :]),
                          reads=[("A", "f", c)], writes=[("X", t0, c)])
                return
            for co in range(NCH):
                b = self.bank()
                for c in range(NCH):
                    S.op(PE, lambda e, b=b, co=co, c=c: e.matmul(
                        self.ps[b][:, :TT], lhsT=wout[:, c, co * 128:(co + 1) * 128], rhs=zb[:, c, :],
                        start=(c == 0), stop=(c == NCH - 1)),
                        reads=[wout_res[c], ("A", "zb", c)], writes=[("ps", b)], inc=(c == NCH - 1))
                S.op(ACT, lambda e, b=b, co=co: e.activation(out=ft[:, co, :], in_=self.ps[b][:, :TT], func=AF.Identity,
                                                             bias=cp[:, 40 + co:41 + co]),
                     reads=[("ps", b), ("A", "cp")], writes=[("A", "f", co)])
            self.postnorm_store(ft, ("A", "f"), x, xres, sq2, ("A", "sq2"), r2, ("A", "r2"),
                                tmp, ("A", "t"), gi_post, x_dst, t0, TT, False)

    def proj_post(self, src, K, w, x_src, x_dst, gi_post, final=False):
        S, T = self.S, self.T
        TT = 256
        KC = K // 128
        self.new_phase()
        A = self.A
        wt = self.W.tile([128, KC, D], BF16, "pw")
        self.load_w(wt, w, ("W", "pw"), "wl")
        yt = [A.tile([128, KC, TT], BF16, f"py{i}") for i in range(2)]
        xt = [A.tile([128, NCH, TT], F32, f"px{i}") for i in range(2)]
        ft = A.tile([128, NCH, TT], F32, "pf")
        sq = A.tile([128, NCH, TT], BF16, "psq")
        r2 = A.tile([128, TT], F32, "pr2")
        tmp = [A.tile([128, TT], F32, f"pt{i}") for i in range(2)]
        sv = src.rearrange("(c p) t -> p c t", p=128)
        for ti in range(T // TT):
            t0 = ti * TT
            sl = ti % 2
            x, xres = xt[sl], ("A", "px", sl)
            y, yres = yt[sl], ("A", "py", sl)

            def loads(tj):
                s2 = tj % 2
                tb = tj * TT
                self.load_x(x_src, tb, TT, xt[s2], ("A", "px", s2), ("xl", s2))
                S.dma(SP, ("pyl", s2), lambda e, y2=yt[s2], tb=tb: e.dma_start(out=y2[:], in_=sv[:, :, tb:tb + TT]),
                      reads=[("SRC", src.name, tb // 128), ("SRC", src.name, tb // 128 + 1)], writes=[("A", "py", s2)])
            if ti == 0:
                loads(0)
            if ti + 1 < T // TT:
                loads(ti + 1)
            for co in range(NCH):
                b = self.bank()
                for kc in range(KC):
                    S.op(PE, lambda e, b=b, co=co, kc=kc, y=y: e.matmul(
                        self.ps[b][:, :TT], lhsT=wt[:, kc, co * 128:(co + 1) * 128], rhs=y[:, kc, :],
                        start=(kc == 0), stop=(kc == KC - 1)),
                        reads=[("W", "pw", kc), yres], writes=[("ps", b)], inc=(kc == KC - 1))
                S.op(ACT, lambda e, b=b, co=co: e.activation(out=ft[:, co, :], in_=self.ps[b][:, :TT], func=AF.Copy),
                     reads=[("ps", b)], writes=[("A", "pf", co)])
            self.postnorm_store(ft, ("A", "pf"), x, xres, sq, ("A", "psq"), r2, ("A", "pr2"),
                                tmp, ("A", "pt"), gi_post, x_dst, t0, TT, final)

    def ret_layer(self, x_src, x_dst, w_in, w_out, dec, rope, rcst, gi_pre, gi_post, li):
        T = self.T
        QT = self.dscr(f"r{li}_qT", [D, T], BF16)
        KT = self.dscr(f"r{li}_kT", [D, T], BF16)
        V = self.dscr(f"r{li}_v", [T, 2048], BF16)
        G = self.dscr(f"r{li}_g", [T, 2048], BF16)
        CB = self.dscr(f"r{li}_cb", [T, 2048], F32)
        YT = self.dscr(f"r{li}_yT", [2048, T], BF16)
        self._ret_inproj(x_src, w_in, rope, gi_pre, QT, KT, V, G)
        self._ret_scan(dec, rcst, QT, KT, V, G, CB, YT, backward=True)
        self._ret_scan(dec, rcst, QT, KT, V, G, CB, YT, backward=False)
        self.proj_post(YT, 2048, w_out, x_src, x_dst, gi_post)

    def _ret_inproj(self, x_src, w_in, rope, gi_pre, QT, KT, V, G):
        S, T = self.S, self.T
        TT = 256
        self.new_phase()
        A = self.A
        win = self.W.tile([128, NCH, 6144], BF16, "rwin")
        self.load_w(win, w_in, ("W", "rwin"), "wl")
        xt = [A.tile([128, NCH, TT], F32, f"x{i}") for i in range(2)]
        ht = [A.tile([128, NCH, TT], BF16, f"h{i}") for i in range(2)]
        sq = A.tile([128, NCH, TT], BF16, "sq")
        rt = [A.tile([128, TT], F32, f"r{i}") for i in range(2)]
        rp = [A.tile([128, 4, TT], F32, f"rp{i}") for i in range(2)]
        t4 = [A.tile([128, TT], F32, f"t4{i}") for i in range(4)]
        ob = [A.tile([128, 2, TT], BF16, f"ob{i}") for i in range(2)]
        vb = [A.tile([128, 512], BF16, f"vb{i}") for i in range(3)]
        win_res = [("W", "rwin", c) for c in range(NCH)]
        n_ob = 0
        n_vb = 0
        for ti in range(T // TT):
            t0 = ti * TT
            sl = ti % 2
            x, h, r, rpt = xt[sl], ht[sl], rt[sl], rp[sl]
            xres, hres, rres, rpres = ("A", "x", sl), ("A", "h", sl), ("A", "r", sl), ("A", "rp", sl)
            def loads(tj):
                s2 = tj % 2
                tb = tj * TT
                self.load_x(x_src, tb, TT, xt[s2], ("A", "x", s2), ("xl", s2))
                S.dma(SP, ("rpl", s2), lambda e, rp2=rp[s2], tb=tb: e.dma_start(out=rp2[:], in_=rope[:, :, tb:tb + TT]),
                      writes=[("A", "rp", s2)])
            if ti == 0:
                loads(0)
            if ti + 1 < T // TT:
                loads(ti + 1)
            self.prenorm(x, xres, h, hres, sq, ("A", "sq"), r, rres, gi_pre, TT)
            for qk in range(2):
                dst = QT if qk == 0 else KT
                for hd in range(4):
                    b = self.bank()
                    for half in range(2):
                        col = qk * 1024 + hd * 256 + half * 128
                        for c in range(NCH):
                            S.op(PE, lambda e, b=b, half=half, col=col, c=c, h=h: e.matmul(
                                self.ps[b][:, half * TT:(half + 1) * TT], lhsT=win[:, c, col:col + 128], rhs=h[:, c, :],
                                start=(c == 0), stop=(c == NCH - 1)),
                                reads=[win_res[c], hres + (c,)], writes=[("ps", b)], inc=(half == 1 and c == NCH - 1))
                    o = ob[n_ob % 2]
                    ores = ("A", "ob", n_ob % 2)
                    n_ob += 1
                    cs, sn = rpt[:, 2 * qk, :], rpt[:, 2 * qk + 1, :]
                    p1, p2 = self.ps[b][:, 0:TT], self.ps[b][:, TT:2 * TT]
                    for k4, (pa, tb) in enumerate(((p1, cs), (p2, sn), (p2, cs), (p1, sn))):
                        S.op(DVE, lambda e, k4=k4, pa=pa, tb=tb: e.tensor_tensor(out=t4[k4][:], in0=pa, in1=tb, op=ALU.mult),
                             reads=[("ps", b), rpres], writes=[("A", "t4", k4)])
                    S.op(POOL, lambda e, o=o: e.tensor_tensor(out=o[:, 0, :], in0=t4[0][:], in1=t4[1][:], op=ALU.subtract),
                         reads=[("A", "t4", 0), ("A", "t4", 1)], writes=[ores])
                    S.op(POOL, lambda e, o=o: e.tensor_tensor(out=o[:, 1, :], in0=t4[2][:], in1=t4[3][:], op=ALU.add),
                         reads=[("A", "t4", 2), ("A", "t4", 3)], writes=[ores])
                    for half in range(2):
                        row = hd * 256 + half * 128
                        S.dma(SP, ("qks", half), lambda e, o=o, half=half, row=row, dst=dst, t0=t0: e.dma_start(
                            out=dst[row:row + 128, t0:t0 + TT], in_=o[:, half, :]),
                            reads=[ores], writes=[("SRC", dst.name, t0 // 128), ("SRC", dst.name, t0 // 128 + 1)])
            for blk in range(TT // 128):
                for vg in range(2):
                    dst = V if vg == 0 else G
                    for grp in range(4):
                        b = self.bank()
                        col = 2048 + vg * 2048 + grp * 512
                        for c in range(NCH):
                            S.op(PE, lambda e, b=b, col=col, c=c, h=h, blk=blk: e.matmul(
                                self.ps[b][:, :], lhsT=h[:, c, blk * 128:(blk + 1) * 128], rhs=win[:, c, col:col + 512],
                                start=(c == 0), stop=(c == NCH - 1)),
                                reads=[win_res[c], hres + (c,)], writes=[("ps", b)], inc=(c == NCH - 1))
                        v = vb[n_vb % 3]
                        vres = ("A", "vb", n_vb % 3)
                        n_vb += 1
                        S.op(ACT, lambda e, b=b, v=v, vg=vg: e.activation(out=v[:], in_=self.ps[b][:, :],
                                                                         func=(AF.Copy if vg == 0 else AF.Silu)),
                             reads=[("ps", b)], writes=[vres])
                        S.dma(SP, ("vgs", n_vb % 3), lambda e, v=v, dst=dst, grp=grp, t0=t0, blk=blk: e.dma_start(
                            out=dst[t0 + blk * 128:t0 + (blk + 1) * 128, grp * 512:(grp + 1) * 512], in_=v[:]),
                            reads=[vres], writes=[("SRC", dst.name, t0 // 128 + blk, grp)])

    def _ret_scan(self, dec, rcst, QT, KT, V, G, CB, YT, backward):
        S, T = self.S, self.T
        n = T // 128
        self.new_phase()
        A = self.A
        dct = A.tile([128, 8], F32, "dct")
        lg = A.tile([128, 8], F32, "lg")
        rc = A.tile([128, 772], F32, "rc")
        S.dma(SP, "dcl", lambda e: e.dma_start(out=dct[:], in_=dec[:, :]), writes=[("A", "dct")])
        S.dma(SP, "rcl", lambda e: e.dma_start(out=rc[:], in_=rcst[:, :]), writes=[("A", "rc")])
        S.op(ACT, lambda e: e.activation(out=lg[:], in_=dct[:], func=AF.Exp), reads=[("A", "dct")], writes=[("A", "lg")])
        S.op(DVE, lambda e: e.tensor_scalar(out=lg[:], in0=lg[:], scalar1=-1.0, scalar2=1.0, op0=ALU.mult, op1=ALU.add),
             reads=[("A", "lg")], writes=[("A", "lg")])
        S.op(ACT, lambda e: e.activation(out=lg[:], in_=lg[:], func=AF.Ln), reads=[("A", "lg")], writes=[("A", "lg")])
        d0 = 4 if backward else 0
        xi8 = A.tile([128, 8, 128], F32, "xi8")
        zeta = A.tile([128, 4], F32, "zeta")
        cdk = A.tile([128, 4], F32, "cdk")
        eps1 = A.tile([128, 1], F32, "eps1")
        S.op(DVE, lambda e: e.memset(eps1[:], float(RMS_EPS)), writes=[("A", "eps1")])
        rowc = 640 if backward else 512
        colc = 769 if backward else 768
        for hd in range(4):
            sc = lg[:, d0 + hd:d0 + hd + 1]
            for half in range(2):
                S.op(ACT, lambda e, hd=hd, half=half, sc=sc: e.activation(out=xi8[:, hd * 2 + half, :], in_=rc[:, rowc:rowc + 128],
                                                                          func=AF.Exp, scale=sc),
                     reads=[("A", "rc"), ("A", "lg")], writes=[("A", "xi8")])
            S.op(ACT, lambda e, hd=hd, sc=sc: e.activation(out=zeta[:, hd:hd + 1], in_=rc[:, colc:colc + 1], func=AF.Exp, scale=sc),
                 reads=[("A", "rc"), ("A", "lg")], writes=[("A", "zeta")])
            S.op(ACT, lambda e, hd=hd, sc=sc: e.activation(out=cdk[:, hd:hd + 1], in_=rc[:, 770:771], func=AF.Exp, scale=sc),
                 reads=[("A", "rc"), ("A", "lg")], writes=[("A", "cdk")])
        DT = A.tile([128, 4, 128], F32, "DT")
        dtmp = A.tile([128, 128], F32, "dtmp")
        if not backward:
            for hd in range(4):
                S.op(ACT, lambda e, hd=hd: e.activation(out=DT[:, hd, :], in_=rc[:, 0:128], func=AF.Exp, scale=lg[:, hd:hd + 1]),
                     reads=[("A", "rc"), ("A", "lg")], writes=[("A", "DT", hd)])
                S.op(DVE, lambda e, hd=hd: e.tensor_tensor(out=DT[:, hd, :], in0=DT[:, hd, :], in1=rc[:, 128:256], op=ALU.mult),
                     reads=[("A", "DT", hd), ("A", "rc")], writes=[("A", "DT", hd)])
                S.op(ACT, lambda e, hd=hd: e.activation(out=dtmp[:], in_=rc[:, 256:384], func=AF.Exp, scale=lg[:, 4 + hd:5 + hd]),
                     reads=[("A", "rc"), ("A", "lg")], writes=[("A", "dtmp")])
                S.op(DVE, lambda e: e.tensor_tensor(out=dtmp[:], in0=dtmp[:], in1=rc[:, 384:512], op=ALU.mult),
                     reads=[("A", "dtmp"), ("A", "rc")], writes=[("A", "dtmp")])
                S.op(DVE, lambda e, hd=hd: e.tensor_tensor(out=DT[:, hd, :], in0=DT[:, hd, :], in1=dtmp[:], op=ALU.add),
                     reads=[("A", "DT", hd), ("A", "dtmp")], writes=[("A", "DT", hd)])
        DTres = [("A", "DT", hd) for hd in range(4)]
        Sf = A.tile([128, 4, 2, 512], F32, "Sf")
        Sb = A.tile([128, 4, 2, 512], BF16, "Sb")
        for hd in range(4):
            S.op(POOL, lambda e, hd=hd: e.memset(Sf[:, hd, :, :], 0.0), writes=[("A", "Sf", hd, 0), ("A", "Sf", hd, 1)])
            S.op(POOL, lambda e, hd=hd: e.memset(Sb[:, hd, :, :], 0.0), writes=[("A", "Sb", hd, 0), ("A", "Sb", hd, 1)])
        qt = [A.tile([128, 8, 128], BF16, f"qt{i}") for i in range(2)]
        kt = [A.tile([128, 8, 128], BF16, f"kt{i}") for i in range(2)]
        vt = [A.tile([128, 2048], BF16, f"vt{i}") for i in range(2)]
        qx = A.tile([128, 8, 128], BF16, "qx")
        kz = A.tile([128, 4, 256], BF16, "kz")
        if backward:
            cbo = [A.tile([128, 512], F32, f"cbo{i}") for i in range(2)]
        else:
            gt = [A.tile([128, 2048], BF16, f"gt{i}") for i in range(2)]
            cbt = [A.tile([128, 2048], F32, f"cbt{i}") for i in range(2)]
            Pm = A.tile([128, 512], BF16, "Pm")
            ot = A.tile([128, 4, 512], F32, "ot")
            junk = A.tile([128, 512], BF16, "junk")
            ss = A.tile([128, 4], F32, "ss")
            yb = A.tile([128, 2048], BF16, "yb")
            yT = A.tile([128, 16, 128], BF16, "yT")
        QTv = QT.rearrange("(c p) t -> p c t", p=128)
        KTv = KT.rearrange("(c p) t -> p c t", p=128)
        order = list(range(n - 1, -1, -1)) if backward else list(range(n))
        ncb = 0
        for it, jc in enumerate(order):
            sl = it % 2
            c0 = jc * 128
            q, k, v = qt[sl], kt[sl], vt[sl]
            qres, kres, vres = ("A", "qt", sl), ("A", "kt", sl), ("A", "vt", sl)
            def loads(it2):
                s2 = it2 % 2
                j2 = order[it2]
                cb0 = j2 * 128
                S.dma(SP, ("ql", s2), lambda e, q2=qt[s2], cb0=cb0: e.dma_start(out=q2[:], in_=QTv[:, :, cb0:cb0 + 128]),
                      reads=[("SRC", QT.name, j2)], writes=[("A", "qt", s2)])
                S.dma(SP, ("kl", s2), lambda e, k2=kt[s2], cb0=cb0: e.dma_start(out=k2[:], in_=KTv[:, :, cb0:cb0 + 128]),
                      reads=[("SRC", KT.name, j2)], writes=[("A", "kt", s2)])
                S.dma(SP, ("vl", s2), lambda e, v2=vt[s2], cb0=cb0: e.dma_start(out=v2[:], in_=V[cb0:cb0 + 128, :]),
                      reads=[("SRC", V.name, j2, g4) for g4 in range(4)], writes=[("A", "vt", s2)])
                if not backward:
                    S.dma(SP, ("gl", s2), lambda e, g2=gt[s2], cb0=cb0: e.dma_start(out=g2[:], in_=G[cb0:cb0 + 128, :]),
                          reads=[("SRC", G.name, j2, g4) for g4 in range(4)], writes=[("A", "gt", s2)])
                    S.dma(SP, ("cbl", s2), lambda e, c2=cbt[s2], cb0=cb0: e.dma_start(out=c2[:], in_=CB[cb0:cb0 + 128, :]),
                          reads=[("SRC", CB.name, j2, g4) for g4 in range(4)], writes=[("A", "cbt", s2)])
            if it == 0:
                loads(0)
            if it + 1 < n:
                loads(it + 1)
            if not backward:
                g, cb = gt[sl], cbt[sl]
                gres, cbres = ("A", "gt", sl), ("A", "cbt", sl)
            S.op(DVE, lambda e, q=q: e.tensor_tensor(out=qx[:], in0=q[:], in1=xi8[:], op=ALU.mult),
                 reads=[qres, ("A", "xi8")], writes=[("A", "qx")])
            bt = self.bank()
            pbt = self.psb[bt]
            for c in range(8):
                S.op(PE, lambda e, c=c, k=k, pbt=pbt: e.transpose(out=pbt[:, c * 128:(c + 1) * 128], in_=k[:, c, :], identity=self.ident[:]),
                     reads=[kres, "c_ident"], writes=[("ps", bt)], inc=(c == 7))
            for hd in range(4):
                S.op(ACT, lambda e, hd=hd, pbt=pbt: e.activation(out=kz[:, hd, :], in_=pbt[:, hd * 256:(hd + 1) * 256], func=AF.Copy,
                                                                 scale=zeta[:, hd:hd + 1]),
                     reads=[("ps", bt), ("A", "zeta")], writes=[("A", "kz", hd)])
            if not backward:
                bs = self.bank()
                for hd in range(4):
                    for half in range(2):
                        S.op(PE, lambda e, hd=hd, half=half, k=k, q=q, bs=bs: e.matmul(
                            self.ps[bs][:, hd * 128:(hd + 1) * 128], lhsT=k[:, hd * 2 + half, :], rhs=q[:, hd * 2 + half, :],
                            start=(half == 0), stop=(half == 1)),
                            reads=[kres, qres], writes=[("ps", bs)], inc=(hd == 3 and half == 1))
                S.op(DVE, lambda e, bs=bs: e.tensor_tensor(out=Pm[:], in0=self.ps[bs][:], in1=DT[:].rearrange("p h i -> p (h i)"),
                                                           op=ALU.mult),
                     reads=[("ps", bs)] + DTres, writes=[("A", "Pm")])
            for hd in range(4):
                bo = self.bank()
                if not backward:
                    S.op(PE, lambda e, hd=hd, bo=bo, v=v: e.matmul(
                        self.ps[bo][:], lhsT=Pm[:, hd * 128:(hd + 1) * 128], rhs=v[:, hd * 512:(hd + 1) * 512], start=True, stop=False),
                        reads=[("A", "Pm"), vres], writes=[("ps", bo)], inc=False)
                for half in range(2):
                    S.op(PE, lambda e, hd=hd, half=half, bo=bo: e.matmul(
                        self.ps[bo][:], lhsT=qx[:, hd * 2 + half, :], rhs=Sb[:, hd, half, :],
                        start=(backward and half == 0), stop=(half == 1)),
                        reads=[("A", "qx"), ("A", "Sb", hd, half)], writes=[("ps", bo)], inc=(half == 1))
                if backward:
                    co = cbo[ncb % 2]
                    cores = ("A", "cbo", ncb % 2)
                    ncb += 1
                    S.op(ACT, lambda e, bo=bo, co=co: e.activation(out=co[:], in_=self.ps[bo][:], func=AF.Copy),
                         reads=[("ps", bo)], writes=[cores])
                    S.dma(SP, ("cbs", ncb % 2), lambda e, co=co, c0=c0, hd=hd: e.dma_start(
                        out=CB[c0:c0 + 128, hd * 512:(hd + 1) * 512], in_=co[:]),
                        reads=[cores], writes=[("SRC", CB.name, jc, hd)])
                else:
                    S.op(DVE, lambda e, bo=bo, hd=hd, cb=cb: e.tensor_tensor(out=ot[:, hd, :], in0=self.ps[bo][:],
                                                                             in1=cb[:, hd * 512:(hd + 1) * 512], op=ALU.add),
                         reads=[("ps", bo), cbres], writes=[("A", "ot", hd)])
                    S.op(ACT, lambda e, hd=hd: e.activation(out=junk[:], in_=ot[:, hd, :], func=AF.Square, accum_out=ss[:, hd:hd + 1]),
                         reads=[("A", "ot", hd)], writes=[("A", "junk"), ("A", "ss", hd)])
                    S.op(ACT, lambda e, hd=hd: e.activation(out=ss[:, hd:hd + 1], in_=ss[:, hd:hd + 1], func=AF.Sqrt,
                                                            scale=1.0 / 512.0, bias=eps1[:, 0:1]),
                         reads=[("A", "ss", hd), ("A", "eps1")], writes=[("A", "ss", hd)])
                    S.op(DVE, lambda e, hd=hd: e.reciprocal(out=ss[:, hd:hd + 1], in_=ss[:, hd:hd + 1]),
                         reads=[("A", "ss", hd)], writes=[("A", "ss", hd)])
                    S.op(DVE, lambda e, hd=hd, g=g: e.scalar_tensor_tensor(
                        out=yb[:, hd * 512:(hd + 1) * 512], in0=ot[:, hd, :], scalar=ss[:, hd:hd + 1],
                        in1=g[:, hd * 512:(hd + 1) * 512], op0=ALU.mult, op1=ALU.mult),
                        reads=[("A", "ot", hd), ("A", "ss", hd), gres], writes=[("A", "yb", hd)])
            if not backward:
                for half8 in range(2):
                    by = self.bank()
                    pby = self.psb[by]
                    for bb in range(8):
                        blk = half8 * 8 + bb
                        S.op(PE, lambda e, blk=blk, bb=bb, pby=pby: e.transpose(
                            out=pby[:, bb * 128:(bb + 1) * 128], in_=yb[:, blk * 128:(blk + 1) * 128], identity=self.ident[:]),
                            reads=[("A", "yb", blk // 4), "c_ident"], writes=[("ps", by)], inc=(bb == 7))
                    S.op(ACT, lambda e, half8=half8, pby=pby: e.activation(
                        out=yT[:, half8 * 8:(half8 + 1) * 8, :].rearrange("p a b -> p (a b)"), in_=pby[:, :], func=AF.Copy),
                        reads=[("ps", by)], writes=[("A", "yT", half8)])
                for blk in range(16):
                    S.dma(SP, ("yts", blk % 4), lambda e, blk=blk, c0=c0: e.dma_start(
                        out=YT[blk * 128:(blk + 1) * 128, c0:c0 + 128], in_=yT[:, blk, :]),
                        reads=[("A", "yT", blk // 8)], writes=[("SRC", YT.name, jc)])
            for hd in range(4):
                for half in range(2):
                    b = self.bank()
                    S.op(PE, lambda e, hd=hd, half=half, b=b, v=v: e.matmul(
                        self.ps[b][:], lhsT=kz[:, hd, half * 128:(half + 1) * 128], rhs=v[:, hd * 512:(hd + 1) * 512],
                        start=True, stop=True),
                        reads=[("A", "kz", hd), vres], writes=[("ps", b)])
                    S.op(DVE, lambda e, hd=hd, half=half, b=b: e.scalar_tensor_tensor(
                        out=Sf[:, hd, half, :], in0=Sf[:, hd, half, :], scalar=cdk[:, hd:hd + 1], in1=self.ps[b][:],
                        op0=ALU.mult, op1=ALU.add),
                        reads=[("A", "Sf", hd, half), ("A", "cdk"), ("ps", b)], writes=[("A", "Sf", hd, half)])
                    S.op(POOL, lambda e, hd=hd, half=half: e.tensor_copy(out=Sb[:, hd, half, :], in_=Sf[:, hd, half, :]),
                         reads=[("A", "Sf", hd, half)], writes=[("A", "Sb", hd, half)])

    DILS = (1, 4, 16)

    def attn_layer(self, x_src, x_dst, w_in, w_out, ropeA, acst, gi_pre, gi_post, dbg=9):
        T = self.T
        HD = [self.dscr(f"a_h{g}", [D, T], BF16) for g in range(3)]
        QT = [self.dscr(f"a_q{g}", [D, T], BF16) for g in range(3)]
        KT = [self.dscr(f"a_k{g}", [D, T], BF16) for g in range(3)]
        VE = [self.dscr(f"a_v{g}", [T + 128 * self.DILS[g], D], BF16) for g in range(3)]
        ND = [self.dscr(f"a_n{g}", [2, D, T], F32) for g in range(3)]
        self._attn_prep(x_src, gi_pre, HD)
        if dbg < 1:
            return
        for g in range(3):
            self._attn_inproj(g, HD[g], w_in, ropeA, acst, QT[g], KT[g], VE[g])
        if dbg < 2:
            return
        for g in range(3):
            self._attn_core(g, acst, QT[g], KT[g], VE[g], ND[g])
        if dbg < 3:
            return
        self._attn_out(ND, w_out, x_src, x_dst, gi_post)

    def _attn_prep(self, x_src, gi_pre, HD):
        S, T = self.S, self.T
        TT = 256
        self.new_phase()
        A = self.A
        Hn = A.tile([128, NCH, T], BF16, "Hn")
        Hp = A.tile([128, NCH, T], BF16, "Hp")
        xt = [A.tile([128, NCH, TT], F32, f"x{i}") for i in range(2)]
        sq = A.tile([128, NCH, TT], BF16, "sq")
        rt = [A.tile([128, TT], F32, f"r{i}") for i in range(2)]
        for ti in range(T // TT):
            t0 = ti * TT
            sl = ti % 2
            x, r = xt[sl], rt[sl]
            xres, rres = ("A", "x", sl), ("A", "r", sl)
            if ti == 0:
                self.load_x(x_src, 0, TT, xt[0], ("A", "x", 0), ("xl", 0))
            if ti + 1 < T // TT:
                self.load_x(x_src, t0 + TT, TT, xt[1 - sl], ("A", "x", 1 - sl), ("xl", 1 - sl))
            self.rstd_of(x, [xres], sq, ("A", "sq"), r, rres, TT)
            for c in range(NCH):
                gsc = self.gs[:, gi_pre * NCH + c: gi_pre * NCH + c + 1]
                S.op(DVE, lambda e, c=c, gsc=gsc, x=x, r=r, t0=t0: e.scalar_tensor_tensor(
                    out=Hn[:, c, t0:t0 + TT], in0=x[:, c, :], scalar=gsc, in1=r[:], op0=ALU.mult, op1=ALU.mult),
                    reads=[xres, rres, "c_gs"], writes=[("A", "Hn", c)])
        for c in range(NCH):
            S.dma(SP, ("hs", c % 4), lambda e, c=c: e.dma_start(out=HD[0][c * 128:(c + 1) * 128, :], in_=Hn[:, c, :]),
                  reads=[("A", "Hn", c)], writes=[("SRC", HD[0].name, c)])
        for gi, dil in ((1, 4), (2, 16)):
            hp = Hp
            for c in range(NCH):
                eng = POOL if c % 2 == 0 else DVE
                S.op(eng, lambda e, c=c, hp=hp, dil=dil: e.tensor_copy(
                    out=hp[:, c, :].rearrange("p (r l) -> p r l", r=dil),
                    in_=Hn[:, c, :].rearrange("p (l r) -> p r l", r=dil)),
                    reads=[("A", "Hn", c)], writes=[("A", "Hp", c)])
                S.dma(SP, ("hs", c % 4), lambda e, c=c, hp=hp, gi=gi: e.dma_start(out=HD[gi][c * 128:(c + 1) * 128, :], in_=hp[:, c, :]),
                      reads=[("A", "Hp", c)], writes=[("SRC", HD[gi].name, c)])

    def _attn_inproj(self, g, Hg, w_in, ropeA, acst, QTg, KTg, VEg):
        S, T = self.S, self.T
        TT = 256
        dil = self.DILS[g]
        L = T // dil
        Lp = L + 128
        self.new_phase()
        A = self.A
        win = self.W.tile([128, NCH, 3072], BF16, "awin")
        self.load_w(win, w_in[:, g * 3072:(g + 1) * 3072], ("W", "awin"), "wl")
        win_res = [("W", "awin", c) for c in range(NCH)]
        pmf = A.tile([128, 128], F32, "pmf")
        pm = A.tile([128, 128], BF16, "pm")
        S.dma(SP, "pml", lambda e: e.dma_start(out=pmf[:], in_=acst[:, 0:128]), writes=[("A", "pmf")])
        S.op(DVE, lambda e: e.tensor_copy(out=pm[:], in_=pmf[:]), reads=[("A", "pmf")], writes=[("A", "pm")])
        zt = A.tile([64, D], BF16, "zt")
        S.op(POOL, lambda e: e.memset(zt[:], 0.0), writes=[("A", "zt")])
        for r in range(dil):
            for side in range(2):
                row = r * Lp + (0 if side == 0 else 64 + L)
                S.dma(SP, ("vz", side), lambda e, row=row: e.dma_start(out=VEg[row:row + 64, :], in_=zt[:]),
                      reads=[("A", "zt")], writes=[("SRC", VEg.name, "pad", r, side)])
        stop = getattr(self, "att_stop", 0)
        if stop == 1:
            return
        ht = [A.tile([128, NCH, TT], BF16, f"h{i}") for i in range(2)]
        rp = [A.tile([128, 2, TT], F32, f"rp{i}") for i in range(2)]
        raw = [A.tile([128, TT], BF16, f"raw{i}") for i in range(2)]
        t1 = [A.tile([128, TT], F32, f"t1{i}") for i in range(2)]
        t2 = [A.tile([128, TT], F32, f"t2{i}") for i in range(2)]
        ob = [A.tile([128, TT], BF16, f"ob{i}") for i in range(3)]
        vb = [A.tile([128, 512], BF16, f"vb{i}") for i in range(3)]
        Hv = Hg.rearrange("(c p) t -> p c t", p=128)
        n_o = 0
        n_v = 0
        for ti in range(T // TT):
            t0 = ti * TT
            sl = ti % 2
            h, rpt = ht[sl], rp[sl]
            hres, rpres = ("A", "h", sl), ("A", "rp", sl)
            def loads(tj):
                s2 = tj % 2
                tb = tj * TT
                S.dma(SP, ("hl", s2), lambda e, h2=ht[s2], tb=tb: e.dma_start(out=h2[:], in_=Hv[:, :, tb:tb + TT]),
                      reads=[("SRC", Hg.name, c) for c in range(NCH)], writes=[("A", "h", s2)])
                S.dma(SP, ("rpl", s2), lambda e, rp2=rp[s2], tb=tb: e.dma_start(out=rp2[:], in_=ropeA[g, :, :, tb:tb + TT]),
                      writes=[("A", "rp", s2)])
            if ti == 0:
                loads(0)
            if ti + 1 < T // TT:
                loads(ti + 1)
            if stop == 2:
                return
            for qk in range(2):
                dst = QTg if qk == 0 else KTg
                for cc in range(NCH):
                    ba = self.bank()
                    col = qk * 1024 + cc * 128
                    for c in range(NCH):
                        S.op(PE, lambda e, ba=ba, col=col, c=c, h=h: e.matmul(
                            self.ps[ba][:, :TT], lhsT=win[:, c, col:col + 128], rhs=h[:, c, :],
                            start=(c == 0), stop=(c == NCH - 1)),
                            reads=[win_res[c], hres], writes=[("ps", ba)], inc=(c == NCH - 1))
                    k2 = n_o % 2
                    rw, a1, a2 = raw[k2], t1[k2], t2[k2]
                    o = ob[n_o % 3]
                    ores = ("A", "ob", n_o % 3)
                    n_o += 1
                    S.op(ACT, lambda e, ba=ba, rw=rw: e.activation(out=rw[:], in_=self.ps[ba][:, :TT], func=AF.Copy),
                         reads=[("ps", ba)], writes=[("A", "raw", k2)])
                    bb = self.bank()
                    S.op(PE, lambda e, bb=bb, rw=rw: e.matmul(self.ps[bb][:, :TT], lhsT=pm[:], rhs=rw[:], start=True, stop=True),
                         reads=[("A", "pm"), ("A", "raw", k2)], writes=[("ps", bb)])
                    S.op(DVE, lambda e, ba=ba, a1=a1, rpt=rpt: e.tensor_tensor(out=a1[:], in0=self.ps[ba][:, :TT], in1=rpt[:, 0, :], op=ALU.mult),
                         reads=[("ps", ba), rpres, ("A", "raw", k2)], writes=[("A", "t1", k2)])
                    S.op(DVE, lambda e, bb=bb, a2=a2, rpt=rpt: e.tensor_tensor(out=a2[:], in0=self.ps[bb][:, :TT], in1=rpt[:, 1, :], op=ALU.mult),
                         reads=[("ps", bb), rpres], writes=[("A", "t2", k2)])
                    S.op(POOL, lambda e, o=o, a1=a1, a2=a2: e.tensor_tensor(out=o[:], in0=a1[:], in1=a2[:], op=ALU.add),
                         reads=[("A", "t1", k2), ("A", "t2", k2)], writes=[ores])
                    S.dma(SP, ("qks", n_o % 3), lambda e, o=o, cc=cc, dst=dst, t0=t0: e.dma_start(
                        out=dst[cc * 128:(cc + 1) * 128, t0:t0 + TT], in_=o[:]),
                        reads=[ores], writes=[("SRC", dst.name, cc, ti)])
            if stop == 3:
                return
            for blk in range(TT // 128):
                pos = t0 + blk * 128
                r, l0 = pos // L, pos % L
                row = r * Lp + 64 + l0
                for grp in range(2):
                    b = self.bank()
                    col = 2048 + grp * 512
                    for c in range(NCH):
                        S.op(PE, lambda e, b=b, col=col, c=c, h=h, blk=blk: e.matmul(
                            self.ps[b][:, :], lhsT=h[:, c, blk * 128:(blk + 1) * 128], rhs=win[:, c, col:col + 512],
                            start=(c == 0), stop=(c == NCH - 1)),
                            reads=[win_res[c], hres], writes=[("ps", b)], inc=(c == NCH - 1))
                    v = vb[n_v % 3]
                    vres = ("A", "vb", n_v % 3)
                    n_v += 1
                    S.op(ACT, lambda e, b=b, v=v: e.activation(out=v[:], in_=self.ps[b][:, :], func=AF.Copy),
                         reads=[("ps", b)], writes=[vres])
                    S.dma(SP, ("vgs", n_v % 3), lambda e, v=v, row=row, grp=grp: e.dma_start(
                        out=VEg[row:row + 128, grp * 512:(grp + 1) * 512], in_=v[:]),
                        reads=[vres], writes=[("SRC", VEg.name, pos // 128, grp)])

    def _attn_core(self, g, acst, QTg, KTg, VEg, NDg):
        S, T = self.S, self.T
        dil = self.DILS[g]
        L = T // dil
        Lp = L + 128
        nqb = L // 128
        nkb = nqb + 1
        self.new_phase()
        A = self.A
        mbf = A.tile([128, 1024], F32, "mbf")
        mb = A.tile([128, 1024], BF16, "mb")
        S.dma(SP, "mbl", lambda e: e.dma_start(out=mbf[:], in_=acst[:, 128:1152]), writes=[("A", "mbf")])
        S.op(DVE, lambda e: e.tensor_copy(out=mb[:], in_=mbf[:]), reads=[("A", "mbf")], writes=[("A", "mb")])
        kt = [A.tile([64, 2, dil * Lp], BF16, f"kt{i}") for i in range(2)]
        qt = [A.tile([64, 2, T], BF16, f"qt{i}") for i in range(2)]
        ve = [A.tile([128, dil * nkb, 128], BF16, f"ve{i}") for i in range(2)]
        for i in range(2):
            S.op(POOL, lambda e, i=i: e.memset(kt[i][:], 0.0), writes=[("A", "kt", i)])
        pmt = [A.tile([128, 512], BF16, f"pm{i}") for i in range(3)]
        span = 128 * dil if dil > 1 else 512
        stg = [A.tile([64, 2, 2, span], F32, f"stg{i}") for i in range(2)]
        VEv = VEg.rearrange("(n p) f -> p n f", p=128)
        kdeps_all = [[("SRC", KTg.name, cc, ti) for ti in range(T // 256)] for cc in range(NCH)]
        qdeps_all = [[("SRC", QTg.name, cc, ti) for ti in range(T // 256)] for cc in range(NCH)]
        vdeps = [("SRC", VEg.name, pb, grp) for pb in range(T // 128) for grp in range(2)]
        vdeps += [("SRC", VEg.name, "pad", r, sd) for r in range(dil) for sd in range(2)]

        def loads(cc):
            sl = cc % 2
            k, q, v = kt[sl], qt[sl], ve[sl]
            kres, qres, vres = ("A", "kt", sl), ("A", "qt", sl), ("A", "ve", sl)
            for hh in range(2):
                r0 = cc * 128 + hh * 64
                S.dma(SP, ("kl", sl), lambda e, k=k, hh=hh, r0=r0: e.dma_start(
                    out=k[:, hh, :].rearrange("p (r l) -> p r l", r=dil)[:, :, 64:64 + L],
                    in_=KTg[r0:r0 + 64, :].rearrange("p (r l) -> p r l", r=dil)), reads=kdeps_all[cc], writes=[kres])
                S.dma(SP, ("ql", sl), lambda e, q=q, hh=hh, r0=r0: e.dma_start(out=q[:, hh, :], in_=QTg[r0:r0 + 64, :]),
                      reads=qdeps_all[cc], writes=[qres])
            S.dma(SP, ("vl", sl), lambda e, v=v, cc=cc: e.dma_start(out=v[:], in_=VEv[:, :, cc * 128:(cc + 1) * 128]),
                  reads=vdeps, writes=[vres])

        nsp = T // span
        items = []
        for cc in range(NCH):
            for sp_i in range(nsp):
                if dil > 1:
                    blocks = [(r, sp_i) for r in range(dil)]
                else:
                    blocks = [(0, sp_i * 4 + b4) for b4 in range(4)]
                for bi, (r, i) in enumerate(blocks):
                    items.append((cc, sp_i, bi, r, i, bi == len(blocks) - 1))
        state = {"n_p": 0}

        def stage_a(item):
            cc, sp_i, bi, r, i, last = item
            sl = cc % 2
            k, q = kt[sl], qt[sl]
            kres, qres = ("A", "kt", sl), ("A", "qt", sl)
            var = 0
            if i == 0:
                var = 1
            if i == nqb - 1:
                var = 2 if var == 0 else 3
            bs = self.bank()
            qc = r * L + i * 128
            for hh in range(2):
                S.op(PE, lambda e, bs=bs, hh=hh, var=var: e.matmul(
                    self.ps[bs][:, hh * 256:(hh + 1) * 256], lhsT=self.ident[:], rhs=mb[:, var * 256:(var + 1) * 256],
                    start=True, stop=False, skip_group_check=True),
                    reads=["c_ident", ("A", "mb")], writes=[("ps", bs)], inc=False)
                for kb in range(2):
                    kc = r * Lp + (i + kb) * 128
                    S.op(PE, lambda e, bs=bs, hh=hh, kb=kb, kc=kc, qc=qc, k=k, q=q: e.matmul(
                        self.ps[bs][:, hh * 256 + kb * 128:hh * 256 + (kb + 1) * 128],
                        lhsT=k[:, hh, kc:kc + 128], rhs=q[:, hh, qc:qc + 128],
                        start=False, stop=(kb == 1), skip_group_check=True),
                        reads=[kres, qres], writes=[("ps", bs)], inc=(hh == 1 and kb == 1))
            np_ = state["n_p"]
            state["n_p"] += 1
            pmx = pmt[np_ % 3]
            pres = ("A", "pm", np_ % 3)
            S.op(ACT, lambda e, bs=bs, pmx=pmx: e.activation(out=pmx[:], in_=self.ps[bs][:], func=AF.Exp, scale=0.125),
                 reads=[("ps", bs)], writes=[pres])
            return pmx, pres

        def stage_b(item, pmx, pres, st, stres):
            cc, sp_i, bi, r, i, last = item
            sl = cc % 2
            v = ve[sl]
            vres = ("A", "ve", sl)
            bo = self.bank()
            for hh in range(2):
                for nd in range(2):
                    for kb in range(2):
                        kbi = r * nkb + i + kb
                        lhs = v[:, kbi, hh * 64:(hh + 1) * 64] if nd == 0 else self.ones[:, 0:64]
                        S.op(PE, lambda e, bo=bo, hh=hh, nd=nd, kb=kb, lhs=lhs, pmx=pmx: e.matmul(
                            self.ps[bo][0:64, (hh * 2 + nd) * 128:(hh * 2 + nd + 1) * 128], lhsT=lhs,
                            rhs=pmx[:, hh * 256 + kb * 128:hh * 256 + (kb + 1) * 128],
                            start=(kb == 0), stop=(kb == 1)),
                            reads=[vres, "c_ones", pres], writes=[("ps", bo)],
                            inc=(hh == 1 and nd == 1 and kb == 1))
            if dil > 1:
                dstv = st[:].rearrange("p h n (l r) -> p (h n) l r", r=dil)[:, :, :, r]
            else:
                dstv = st[:].rearrange("p h n t -> p (h n) t")[:, :, bi * 128:(bi + 1) * 128]
            S.op(DVE, lambda e, bo=bo, dstv=dstv: e.tensor_copy(
                out=dstv, in_=self.ps[bo][0:64, :].rearrange("p (a q) -> p a q", a=4)),
                reads=[("ps", bo)], writes=[stres])
            if last:
                for hh in range(2):
                    for nd in range(2):
                        r0 = cc * 128 + hh * 64
                        S.dma(SP, ("nds", hh * 2 + nd), lambda e, st=st, hh=hh, nd=nd, r0=r0, sp_i=sp_i: e.dma_start(
                            out=NDg[nd, r0:r0 + 64, sp_i * span:(sp_i + 1) * span], in_=st[:, hh, nd, :]),
                            reads=[stres], writes=[("SRC", NDg.name, cc, sp_i, hh, nd)])

        loads(0)
        pend = None
        n_s = 0
        for idx, item in enumerate(items):
            cc, sp_i, bi, r, i, last = item
            a = stage_a(item)
            if pend is not None:
                stage_b(*pend)
            if bi == 0 and sp_i == 0 and cc + 1 < NCH:
                loads(cc + 1)
            if bi == 0:
                cur_st = stg[n_s % 2]
                cur_stres = ("A", "stg", n_s % 2)
                n_s += 1
            pend = (item, a[0], a[1], cur_st, cur_stres)
        stage_b(*pend)

    def _attn_out(self, ND, w_out, x_src, x_dst, gi_post):
        S, T = self.S, self.T
        TT = 256
        self.new_phase()
        A = self.A
        wt = self.W.tile([128, NCH, D], BF16, "aw")
        self.load_w(wt, w_out, ("W", "aw"), "wl")
        nt = [[A.tile([128, NCH, TT], F32, f"n{g}{i}") for i in range(2)] for g in range(3)]
        dn = [[A.tile([128, NCH, TT], F32, f"d{g}{i}") for i in range(2)] for g in range(3)]
        ob = A.tile([128, NCH, TT], BF16, "ob")
        xt = [A.tile([128, NCH, TT], F32, f"x{i}") for i in range(2)]
        ft = A.tile([128, NCH, TT], F32, "f")
        sq = A.tile([128, NCH, TT], BF16, "sq")
        r2 = A.tile([128, TT], F32, "r2")
        tmp = [A.tile([128, TT], F32, f"t{i}") for i in range(2)]
        for ti in range(T // TT):
            t0 = ti * TT
            sl = ti % 2
            x, xres = xt[sl], ("A", "x", sl)

            def loads(tj):
                s2 = tj % 2
                tb = tj * TT
                self.load_x(x_src, tb, TT, xt[s2], ("A", "x", s2), ("xl", s2))
                for g in range(3):
                    dil = self.DILS[g]
                    span = 128 * dil if dil > 1 else 512
                    deps = [("SRC", ND[g].name, cc, tb // span, hh, nd) for cc in range(NCH) for hh in range(2) for nd in range(2)]
                    for nd, tl in ((0, nt[g][s2]), (1, dn[g][s2])):
                        src = ND[g][nd].rearrange("(c p) t -> p c t", p=128)
                        S.dma(SP, ("ndl", g, nd, s2), lambda e, tl=tl, src=src, tb=tb: e.dma_start(out=tl[:], in_=src[:, :, tb:tb + TT]),
                              reads=deps, writes=[("A", "nd", g, nd, s2)])
            if ti == 0:
                loads(0)
            if ti + 1 < T // TT:
                loads(ti + 1)
            n0, n1, n2 = nt[0][sl], nt[1][sl], nt[2][sl]
            d0, d1, d2 = dn[0][sl], dn[1][sl], dn[2][sl]
            S.op(POOL, lambda e, n0=n0, n1=n1: e.tensor_tensor(out=n0[:], in0=n0[:], in1=n1[:], op=ALU.add),
                 reads=[("A", "nd", 0, 0, sl), ("A", "nd", 1, 0, sl)], writes=[("A", "nd", 0, 0, sl)])
            S.op(POOL, lambda e, n0=n0, n2=n2: e.tensor_tensor(out=n0[:], in0=n0[:], in1=n2[:], op=ALU.add),
                 reads=[("A", "nd", 0, 0, sl), ("A", "nd", 2, 0, sl)], writes=[("A", "nd", 0, 0, sl)])
            S.op(DVE, lambda e, d0=d0, d1=d1: e.tensor_tensor(out=d0[:], in0=d0[:], in1=d1[:], op=ALU.add),
                 reads=[("A", "nd", 0, 1, sl), ("A", "nd", 1, 1, sl)], writes=[("A", "nd", 0, 1, sl)])
            S.op(DVE, lambda e, d0=d0, d2=d2: e.tensor_tensor(out=d0[:], in0=d0[:], in1=d2[:], op=ALU.add),
                 reads=[("A", "nd", 0, 1, sl), ("A", "nd", 2, 1, sl)], writes=[("A", "nd", 0, 1, sl)])
            S.op(DVE, lambda e, d0=d0: e.reciprocal(out=d0[:], in_=d0[:]), reads=[("A", "nd", 0, 1, sl)], writes=[("A", "nd", 0, 1, sl)])
            S.op(DVE, lambda e, n0=n0, d0=d0: e.tensor_tensor(out=ob[:], in0=n0[:], in1=d0[:], op=ALU.mult),
                 reads=[("A", "nd", 0, 0, sl), ("A", "nd", 0, 1, sl)], writes=[("A", "ob")])
            for co in range(NCH):
                b = self.bank()
                for kc in range(NCH):
                    S.op(PE, lambda e, b=b, co=co, kc=kc: e.matmul(
                        self.ps[b][:, :TT], lhsT=wt[:, kc, co * 128:(co + 1) * 128], rhs=ob[:, kc, :],
                        start=(kc == 0), stop=(kc == NCH - 1)),
                        reads=[("W", "aw", kc), ("A", "ob")], writes=[("ps", b)], inc=(kc == NCH - 1))
                S.op(ACT, lambda e, b=b, co=co: e.activation(out=ft[:, co, :], in_=self.ps[b][:, :TT], func=AF.Copy),
                     reads=[("ps", b)], writes=[("A", "f", co)])
            self.postnorm_store(ft, ("A", "f"), x, xres, sq, ("A", "sq"), r2, ("A", "r2"),
                                tmp, ("A", "t"), gi_post, x_dst, t0, TT, False)

    def finish(self):
        self.S.final_wait(SP, self.out_toks)
        self.S.emit()
        return self.nc


SEQ = 4096
N_CORES = 8
DEPTH = 4


def _ret_tables(T):
    inv = (1.0 / (10000.0 ** np.linspace(0.0, 1.0, 128, dtype=np.float32))).astype(np.float32)
    ang = (np.arange(T, dtype=np.float32)[None, :] * inv[:, None]).astype(np.float32)
    cos, sin = np.cos(ang).astype(np.float32), np.sin(ang).astype(np.float32)
    rope = np.stack([cos, sin, cos / 16.0, sin / 16.0], axis=1).astype(np.float32)
    m = np.arange(128)[:, None]
    i = np.arange(128)[None, :]
    rc = np.zeros((128, 772), np.float32)
    rc[:, 0:128] = np.maximum(i - m, 0)
    rc[:, 128:256] = (i >= m)
    rc[:, 256:384] = np.maximum(m - i, 0)
    rc[:, 384:512] = (m > i)
    rc[:, 512:640] = i + 1
    rc[:, 640:768] = 128 - i
    rc[:, 768] = 127 - np.arange(128)
    rc[:, 769] = np.arange(128)
    rc[:, 770] = 128.0
    return np.ascontiguousarray(rope), rc


def _attn_tables(T):
    inv = (500000.0 ** (-np.arange(0, 16, 2, dtype=np.float32) / 16.0)).astype(np.float32)
    ropeA = np.zeros((3, 128, 2, T), np.float32)
    d = np.arange(128) % 64
    rot = d < 16
    for g, dil in enumerate(Prog.DILS):
        L = T // dil
        pos = np.arange(T)
        tok = (pos % L) * dil + pos // L
        ang = (tok.astype(np.float32)[None, :] * inv[:, None]).astype(np.float32)
        cos, sin = np.cos(ang).astype(np.float32), np.sin(ang).astype(np.float32)
        ropeA[g, :, 0, :] = 1.0
        ropeA[g, rot, 0, :] = cos[d[rot] % 8]
        ropeA[g, rot, 1, :] = sin[d[rot] % 8]
    ac = np.zeros((128, 1152), np.float32)
    for m in range(128):
        dd = m % 64
        if dd < 8:
            ac[m + 8, m] = -1.0
        elif dd < 16:
            ac[m - 8, m] = 1.0
    kk = np.arange(128)[:, None]
    qq = np.arange(128)[None, :]
    for var in range(4):
        v0 = (kk >= qq)
        v1 = (kk <= qq)
        if var in (1, 3):
            v0 = v0 & (kk >= 64)
        if var in (2, 3):
            v1 = v1 & (kk < 64)
        ac[:, 128 + var * 256:128 + var * 256 + 128] = np.where(v0, 0.0, -30000.0)
        ac[:, 128 + var * 256 + 128:128 + (var + 1) * 256] = np.where(v1, 0.0, -30000.0)
    return ropeA, ac


def build_program(T=SEQ, depth=DEPTH):
    P = Prog(T)
    xT = P.din("xT", [D, T])
    gT = P.din("gT", [128, 16 * NCH])
    cst = P.din("cst", [128, 128])
    ffn_w_in = P.din("ffn_w_in", [DEPTH, D, 2 * FFN_H])
    ffn_w_out = P.din("ffn_w_out", [DEPTH, FFN_H, D])
    ret_w_in = P.din("ret_w_in", [2, D, 6144])
    ret_w_out = P.din("ret_w_out", [2, 2048, D])
    dec = P.din("dec", [2, 128, 8])
    rope = P.din("rope", [128, 4, T])
    rcst = P.din("rcst", [128, 772])
    conv_w_in = P.din("conv_w_in", [D, 2 * D])
    conv_w_out = P.din("conv_w_out", [D, D])
    cvp = P.din("cvp", [128, 296])
    attn_w_in = P.din("attn_w_in", [D, 9216])
    attn_w_out = P.din("attn_w_out", [D, D])
    ropeA = P.din("ropeA", [3, 128, 2, T])
    acst = P.din("acst", [128, 1152])
    outT = P.dout("outT", [D, T])
    X = P.dscr("xres", [D, T], F32)
    P.consts(gT, cst)
    cur = xT
    for i in range(depth):
        kind, j = i % 3, i // 3
        if kind == 0:
            P.ret_layer(cur, X, ret_w_in[j], ret_w_out[j], dec[j], rope, rcst, 4 * i + 0, 4 * i + 1, i)
        elif kind == 1:
            P.conv_layer(cur, X, conv_w_in, conv_w_out, cvp, 4 * i + 0, 4 * i + 1)
        else:
            P.attn_layer(cur, X, attn_w_in, attn_w_out, ropeA, acst, 4 * i + 0, 4 * i + 1)
        last = (i == depth - 1)
        P.ffn_layer(X, outT if last else X, ffn_w_in[i], ffn_w_out[i], 4 * i + 2, 4 * i + 3, final=last)
        cur = X
    return P.finish()


def _pl(v):
    return np.asarray(v, np.float32).reshape(-1, 128).T


def kernel(x, norm_w, ffn_w_in, ffn_w_out, ret_w_in, ret_log1m_decay, ret_w_out,
           conv_w_in, conv_b_in, conv_w_dw, conv_b_dw, conv_ln_g, conv_ln_b,
           conv_w_out, conv_b_out, attn_w_in, attn_w_out):
    f = lambda a: np.ascontiguousarray(np.asarray(a, dtype=np.float32))
    x = f(x)
    B, T, _ = x.shape
    nc = build_program(T)
    rope, rcst = _ret_tables(T)
    ropeA, acst = _attn_tables(T)
    norm_w = f(norm_w)
    gT = f(norm_w.reshape(16, NCH, 128).transpose(2, 0, 1).reshape(128, 16 * NCH))
    dec = f(np.broadcast_to(f(ret_log1m_decay).reshape(2, 1, 8), (2, 128, 8)))
    cvp = f(np.concatenate([_pl(f(conv_b_in)[0]), _pl(f(conv_b_dw)[0]), _pl(f(conv_ln_g)[0]), _pl(f(conv_ln_b)[0]),
                            _pl(f(conv_b_out)[0]),
                            f(conv_w_dw)[0].reshape(31, NCH, 128).transpose(2, 0, 1).reshape(128, 248)], axis=1))
    shared = {
        "gT": gT, "cst": np.eye(128, dtype=np.float32),
        "ffn_w_in": f(ffn_w_in), "ffn_w_out": f(ffn_w_out),
        "ret_w_in": f(ret_w_in), "ret_w_out": f(ret_w_out), "dec": dec, "rope": rope, "rcst": rcst,
        "conv_w_in": f(conv_w_in)[0], "conv_w_out": f(conv_w_out)[0], "cvp": cvp,
        "attn_w_in": f(attn_w_in)[0], "attn_w_out": f(attn_w_out)[0], "ropeA": ropeA, "acst": acst,
    }
    in_maps = []
    for b in range(B):
        m = dict(shared)
        m["xT"] = np.ascontiguousarray(x[b].T)
        in_maps.append(m)
    res = run_bass_kernel_spmd(nc, in_maps, core_ids=list(range(B)))
    out = np.stack([np.ascontiguousarray(r["outT"].T) for r in res.results], axis=0)
    return out.astype(np.float32)
```

```python
import math
import numpy as np
import concourse.bass as bass
import concourse.mybir as mybir
from concourse.bass_utils import run_bass_kernel_spmd

F32 = mybir.dt.float32
BF16 = mybir.dt.bfloat16
ALU = mybir.AluOpType
AF = mybir.ActivationFunctionType
AX = mybir.AxisListType

D = 1024
NCH = D // 128
FFN_H = 2816
RMS_EPS = 1e-6
LN_EPS = 1e-5

PE, ACT, DVE, POOL, SP = "pe", "act", "dve", "pool", "sp"
ENGS = (PE, ACT, DVE, POOL, SP)


class Sched:
    def __init__(self, nc):
        self.nc = nc
        self.ops = {e: [] for e in ENGS}
        self.cnt = {e: 0 for e in ENGS}
        self.sems = {e: nc.alloc_semaphore(f"s_{e}") for e in ENGS}
        self.dsems = {}
        self.seen = {}
        self.res = {}
        self.floor = {}
        self.n_ops = 0
        self.bg = []
        self.bg_every = 2

    def bg_issue_one(self):
        name, q, key, fn, reads, writes = self.bg.pop(0)
        self.dma(q, key, fn, reads=reads, writes=writes)

    def ensure_bg(self, name):
        while any(b[0] == name for b in self.bg):
            self.bg_issue_one()

    def _res(self, r):
        st = self.res.get(r)
        if st is None:
            arena = r[0] if isinstance(r, tuple) else None
            st = {"w": None, "r": dict(self.floor.get(arena, {}))}
            self.res[r] = st
        return st

    def end_phase(self, arena):
        fl = dict(self.floor.get(arena, {}))
        dead = []
        for r, st in self.res.items():
            if isinstance(r, tuple) and r[0] == arena:
                if st["w"] is not None:
                    k, v = st["w"]
                    fl[k] = max(fl.get(k, 0), v)
                for k, v in st["r"].items():
                    fl[k] = max(fl.get(k, 0), v)
                dead.append(r)
        for r in dead:
            del self.res[r]
        self.floor[arena] = fl

    def _semh(self, key):
        if key in self.sems:
            return self.sems[key]
        return self.dsems[key][0]

    def _collect(self, eng, reads, writes, is_dma):
        deps = {}

        def add(tok, same_ok):
            if tok is None:
                return
            k, v = tok
            if same_ok and (not is_dma) and k == eng:
                return
            if v > deps.get(k, 0):
                deps[k] = v

        for r in reads:
            st = self._res(r)
            add(st["w"], False)
            if isinstance(r, tuple) and r[0] == "ps":
                for k, v in st["r"].items():
                    add((k, v), True)
        for w in writes:
            st = self._res(w)
            add(st["w"], True)
            for k, v in st["r"].items():
                add((k, v), True)
        waits = []
        for k, v in deps.items():
            if self.seen.get((eng, k), 0) < v:
                self.seen[(eng, k)] = v
                waits.append((k, v))
        return waits

    def _commit(self, tok, reads, writes):
        k, v = tok
        for r in reads:
            st = self._res(r)
            if v > st["r"].get(k, 0):
                st["r"][k] = v
        for w in writes:
            st = self._res(w)
            st["w"] = tok
            st["r"] = {}

    def op(self, eng, fn, reads=(), writes=(), inc=True):
        waits = self._collect(eng, reads, writes, False)
        if inc:
            self.cnt[eng] += 1
            tok = (eng, self.cnt[eng])
        else:
            tok = (eng, self.cnt[eng] + 1)
        self.ops[eng].append((waits, fn, (eng, 1) if inc else None))
        self._commit(tok, reads, writes)
        self.n_ops += 1
        if eng == POOL and self.bg and self.cnt[POOL] % self.bg_every == 0:
            self.bg_issue_one()
        return tok

    def dma(self, q, key, fn, reads=(), writes=()):
        if key not in self.dsems:
            self.dsems[key] = [self.nc.alloc_semaphore(f"d_{len(self.dsems)}"), 0]
        ds = self.dsems[key]
        waits = self._collect(q, reads, writes, True)
        if ds[1] > 0 and self.seen.get((q, key), 0) < ds[1]:
            self.seen[(q, key)] = ds[1]
            waits.append((key, ds[1]))
        ds[1] += 16
        tok = (key, ds[1])
        self.ops[q].append((waits, fn, (key, 16)))
        self._commit(tok, reads, writes)
        self.n_ops += 1
        return tok

    def final_wait(self, eng, toks):
        waits = []
        for k, v in toks:
            if self.seen.get((eng, k), 0) < v:
                self.seen[(eng, k)] = v
                waits.append((k, v))
        self.ops[eng].append((waits, None, None))

    def emit(self):
        nc = self.nc
        with nc.Block() as block:
            def run(e, name):
                for waits, fn, inc in self.ops[name]:
                    if fn is None:
                        for k, v in waits:
                            e.wait_ge(self._semh(k), v)
                        continue
                    for k, v in waits[:-1]:
                        e.wait_ge(self._semh(k), v)
                    ins = fn(e)
                    if waits:
                        ins._wait_ge(self._semh(waits[-1][0]), waits[-1][1])
                    if inc is not None:
                        ins.then_inc(self._semh(inc[0]), inc[1])

            @block.tensor
            def _(e):
                run(e, PE)

            @block.scalar
            def _(e):
                run(e, ACT)

            @block.vector
            def _(e):
                run(e, DVE)

            @block.gpsimd
            def _(e):
                run(e, POOL)

            @block.sync
            def _(e):
                run(e, SP)


class Arena:
    def __init__(self, nc, name, base, size):
        self.nc, self.name, self.base, self.size = nc, name, base, size
        self.off = 0
        self.uid = 0

    def reset(self):
        self.off = 0

    def tile(self, shape, dtype, tag):
        nbytes = int(np.prod(shape[1:])) * (2 if dtype == BF16 else 4)
        nbytes = (nbytes + 63) // 64 * 64
        assert self.off + nbytes <= self.size, (self.name, tag, self.off, nbytes, self.size)
        self.uid += 1
        t = self.nc.alloc_sbuf_tensor_at(f"{self.name}_{tag}_{self.uid}", list(shape), dtype,
                                         offset=self.base + self.off)
        self.off += nbytes
        return t


SB_BASE = 16512
C_SIZE = 12 * 1024
M_SIZE = 229344 - SB_BASE - C_SIZE


class Prog:
    def __init__(self, T):
        self.T = T
        nc = bass.Bass("TRN2", target_bir_lowering=False)
        self.nc = nc
        self.S = Sched(nc)
        self.C = Arena(nc, "C", SB_BASE, C_SIZE)
        self.W = self.A = Arena(nc, "M", SB_BASE + C_SIZE, M_SIZE)
        self.ps = [nc.alloc_psum_tensor(f"ps{i}", [128, 512], F32) for i in range(8)]
        self.psb = [t.bitcast(BF16) for t in self.ps]
        self.ps_rr = 0
        self.dram = {}
        self.out_toks = []

    def din(self, name, shape, dtype=F32):
        t = self.nc.dram_tensor(name, list(shape), dtype, kind="ExternalInput")
        self.dram[name] = t
        return t

    def dout(self, name, shape, dtype=F32):
        t = self.nc.dram_tensor(name, list(shape), dtype, kind="ExternalOutput")
        self.dram[name] = t
        return t

    def dscr(self, name, shape, dtype):
        t = self.nc.dram_tensor(name, list(shape), dtype)
        self.dram[name] = t
        return t

    def bank(self):
        i = self.ps_rr
        self.ps_rr = (self.ps_rr + 1) % 8
        return i

    def new_phase(self):
        S = self.S
        S.end_phase("W")
        S.end_phase("A")
        fl = dict(S.floor.get("W", {}))
        for k, v in S.floor.get("A", {}).items():
            fl[k] = max(fl.get(k, 0), v)
        S.floor["W"] = dict(fl)
        S.floor["A"] = dict(fl)
        self.A.reset()

    def consts(self, gT, cst):
        S, C = self.S, self.C
        identf = C.tile([128, 128], F32, "identf")
        S.dma(SP, "c_id", lambda e: e.dma_start(out=identf[:], in_=cst[:, 0:128]), writes=["c_identf"])
        self.ones = C.tile([128, 128], BF16, "ones")
        self.ident = C.tile([128, 128], BF16, "ident")
        self.gs = C.tile([128, 16 * NCH], F32, "gs")
        onesf = C.tile([128, 128], F32, "onesf")
        self.epsD = C.tile([128, 2], F32, "epsD")
        S.op(DVE, lambda e: e.memset(self.epsD[:, 0:1], float(D * RMS_EPS)), writes=["c_eps"])
        S.op(DVE, lambda e: e.memset(onesf[:], 1.0), writes=["c_onesf"])
        S.op(DVE, lambda e: e.tensor_copy(out=self.ones[:], in_=onesf[:]), reads=["c_onesf"], writes=["c_ones"])
        S.op(DVE, lambda e: e.tensor_copy(out=self.ident[:], in_=identf[:]), reads=["c_identf"], writes=["c_ident"])
        S.dma(SP, "c_g", lambda e: e.dma_start(out=self.gs[:], in_=gT[:, :]), writes=["c_g0"])
        S.op(DVE, lambda e: e.tensor_scalar(out=self.gs[:], in0=self.gs[:], scalar1=float(math.sqrt(D)),
                                            scalar2=None, op0=ALU.mult), reads=["c_g0"], writes=["c_gs"])

    def convert_w(self, name, w_ap, K, F):
        wb = self.dscr("wb_" + name, [K, F], BF16)
        for c in range(K // 128):
            self.S.bg.append((name, POOL, ("cv", len(self.S.bg) % 8),
                              (lambda e, c=c, wb=wb, w_ap=w_ap: e.dma_start(out=wb[c * 128:(c + 1) * 128, :], in_=w_ap[c * 128:(c + 1) * 128, :])),
                              [], [("WB", name, c)]))
        return (name, wb)

    def load_w(self, dst, wsrc, res, key, cols=None, rows=None):
        name, wb = wsrc
        self.S.ensure_bg(name)
        KC = dst.shape[1]
        c0 = 0 if rows is None else rows[0] // 128
        for c in range(KC):
            cs = c0 + c
            src = wb[cs * 128:(cs + 1) * 128, :] if cols is None else wb[cs * 128:(cs + 1) * 128, cols[0]:cols[1]]
            self.S.dma(ACT, (key, c % 4), lambda e, c=c, src=src: e.dma_start(out=dst[:, c, :], in_=src),
                       reads=[("WB", name, cs)], writes=[res + (c,)])

    def rstd_of(self, src, srcres, sq, sqres, r, rres, TT):
        S = self.S
        S.op(ACT, lambda e: e.activation(out=sq[:, :, :TT], in_=src[:, :, :TT], func=AF.Square),
             reads=srcres, writes=[sqres])
        b = self.bank()
        for c in range(NCH):
            S.op(PE, lambda e, c=c: e.matmul(self.ps[b][:, :TT], lhsT=self.ones[:], rhs=sq[:, c, :TT],
                                             start=(c == 0), stop=(c == NCH - 1)),
                 reads=["c_ones", sqres], writes=[("ps", b)], inc=(c == NCH - 1))
        S.op(ACT, lambda e: e.activation(out=r[:, :TT], in_=self.ps[b][:, :TT], func=AF.Sqrt, bias=self.epsD[:, 0:1]),
             reads=[("ps", b), "c_eps"], writes=[rres])
        S.op(DVE, lambda e: e.reciprocal(out=r[:, :TT], in_=r[:, :TT]), reads=[rres], writes=[rres])

    def load_x(self, x_src, t0, TT, x, xres, key):
        xv = x_src.rearrange("(c p) t -> p c t", p=128)
        self.S.dma(SP, key, lambda e: e.dma_start(out=x[:, :, :TT], in_=xv[:, :, t0:t0 + TT]),
                   reads=[("X", x_src.name, t0, c) for c in range(NCH)], writes=[xres])

    def prenorm(self, x, xres, h, hres, sq, sqres, r, rres, gi, TT):
        S = self.S
        self.rstd_of(x, [xres], sq, sqres, r, rres, TT)
        for c in range(NCH):
            g = self.gs[:, gi * NCH + c: gi * NCH + c + 1]
            S.op(DVE, lambda e, c=c, g=g: e.scalar_tensor_tensor(
                out=h[:, c, :TT], in0=x[:, c, :TT], scalar=g, in1=r[:, :TT], op0=ALU.mult, op1=ALU.mult),
                reads=[xres, rres, "c_gs"], writes=[hres + (c,)])

    def postnorm_store(self, ft, fkey, x, xres, sq, sqres, r2, r2res, tmp, tkey, gi, x_dst, t0, TT, final):
        S = self.S
        fres = [fkey + (c,) for c in range(NCH)]
        self.rstd_of(ft, fres, sq, sqres, r2, r2res, TT)
        for c in range(NCH):
            g = self.gs[:, gi * NCH + c: gi * NCH + c + 1]
            tm = tmp[c % len(tmp)]
            tres = tkey + (c % len(tmp),)
            S.op(POOL, lambda e, c=c, tm=tm: e.tensor_tensor(out=tm[:, :TT], in0=ft[:, c, :TT], in1=r2[:, :TT], op=ALU.mult),
                 reads=[fres[c], r2res], writes=[tres])
            S.op(DVE, lambda e, c=c, g=g, tm=tm: e.scalar_tensor_tensor(
                out=ft[:, c, :TT], in0=tm[:, :TT], scalar=g, in1=x[:, c, :TT], op0=ALU.mult, op1=ALU.add),
                reads=[tres, xres, "c_gs"], writes=[fres[c]])
            tk = S.dma(SP, ("xs", c), lambda e, c=c: e.dma_start(
                out=x_dst[c * 128:(c + 1) * 128, t0:t0 + TT], in_=ft[:, c, :TT]),
                reads=[fres[c]], writes=[("X", x_dst.name, t0, c)])
            if final:
                self.out_toks.append(tk)

    def ffn_layer(self, x_src, x_dst, w_in, w_out, gi_pre, gi_post, final=False):
        F1 = self.dram.get("ffn_f1")
        if F1 is None:
            F1 = self.dscr("ffn_f1", [D, self.T], F32)
        self._ffn_pass(0, x_src, x_dst, w_in, w_out, gi_pre, gi_post, F1, final)
        self._ffn_pass(1, x_src, x_dst, w_in, w_out, gi_pre, gi_post, F1, final)

    def _ffn_pass(self, hp, x_src, x_dst, w_in, w_out, gi_pre, gi_post, F1, final):
        S, T = self.S, self.T
        TT = 512
        NJ = FFN_H // 128
        NH = NJ // 2
        j0 = hp * NH
        self.new_phase()
        A = self.A
        wg = self.W.tile([128, NCH, NH * 128], BF16, "wg")
        wu = self.W.tile([128, NCH, NH * 128], BF16, "wu")
        wo = self.W.tile([128, NH, D], BF16, "wo")
        self.load_w(wg, w_in, ("W", "wg"), "wl", cols=(j0 * 128, (j0 + NH) * 128))
        self.load_w(wu, w_in, ("W", "wu"), "wl", cols=(FFN_H + j0 * 128, FFN_H + (j0 + NH) * 128))
        self.load_w(wo, w_out, ("W", "wo"), "wl", rows=(j0 * 128, (j0 + NH) * 128))
        xt = [A.tile([128, NCH, TT], F32, f"x{i}") for i in range(2)]
        ht = [A.tile([128, NCH, TT], BF16, f"h{i}") for i in range(2)]
        sq = A.tile([128, NCH, TT], BF16, "sq")
        rt = [A.tile([128, TT], F32, f"r{i}") for i in range(2)]
        st = [A.tile([128, TT], F32, f"s{i}") for i in range(3)]
        ut = A.tile([128, NH, TT], BF16, "u")
        ft = [A.tile([128, NCH, TT], F32, f"f{i}") for i in range(1)]
        f1t = A.tile([128, NCH, TT], F32, "f1") if hp == 1 else None
        sq2 = A.tile([128, NCH, TT], BF16, "sq2") if hp == 1 else None
        r2 = A.tile([128, TT], F32, "r2") if hp == 1 else None
        tmp = [A.tile([128, TT], F32, f"t{i}") for i in range(2)]
        wg_res = [("W", "wg", c) for c in range(NCH)]
        wu_res = [("W", "wu", c) for c in range(NCH)]
        wo_res = [("W", "wo", j) for j in range(NH)]
        F1v = F1.rearrange("(c p) t -> p c t", p=128)
        nt = T // TT

        def xr(tj):
            return ("A", "x", tj % 2)

        def load_tile(tj):
            self.load_x(x_src, tj * TT, TT, xt[tj % 2], xr(tj), ("xl", tj % 2))

        def load_f1(tj):
            tb = tj * TT
            S.dma(SP, "f1l", lambda e, tb=tb: e.dma_start(out=f1t[:], in_=F1v[:, :, tb:tb + TT]),
                  reads=[("F1", tb, c) for c in range(NCH)], writes=[("A", "f1")])

        def pre_a(tj):
            x = xt[tj % 2]
            S.op(ACT, lambda e, x=x: e.activation(out=sq[:], in_=x[:], func=AF.Square), reads=[xr(tj)], writes=[("A", "sq")])

        def pre_b(tj):
            sl = tj % 2
            x, h, r = xt[sl], ht[sl], rt[sl]
            hres, rres = ("A", "h", sl), ("A", "r", sl)
            b = self.bank()
            for c in range(NCH):
                S.op(PE, lambda e, c=c, b=b: e.matmul(self.ps[b][:, :], lhsT=self.ones[:], rhs=sq[:, c, :],
                                                      start=(c == 0), stop=(c == NCH - 1)),
                     reads=["c_ones", ("A", "sq")], writes=[("ps", b)], inc=(c == NCH - 1))
            S.op(ACT, lambda e, b=b, r=r: e.activation(out=r[:], in_=self.ps[b][:, :], func=AF.Sqrt, bias=self.epsD[:, 0:1]),
                 reads=[("ps", b), "c_eps"], writes=[rres])
            S.op(DVE, lambda e, r=r: e.reciprocal(out=r[:], in_=r[:]), reads=[rres], writes=[rres])
            for c in range(NCH):
                g = self.gs[:, gi_pre * NCH + c: gi_pre * NCH + c + 1]
                S.op(DVE, lambda e, c=c, g=g, x=x, h=h, r=r: e.scalar_tensor_tensor(
                    out=h[:, c, :], in0=x[:, c, :], scalar=g, in1=r[:], op0=ALU.mult, op1=ALU.mult),
                    reads=[xr(tj), rres, "c_gs"], writes=[hres + (c,)])

        def post_b(tj):
            f = ft[0]
            fk = ("A", "f", 0)
            self.postnorm_store(f, fk, xt[tj % 2], xr(tj), sq2, ("A", "sq2"), r2, ("A", "r2"),
                                tmp, ("A", "t"), gi_post, x_dst, tj * TT, TT, final)

        load_tile(0)
        if hp == 1:
            load_f1(0)
        pre_a(0)
        pre_b(0)
        pend_post = None
        for ti in range(nt):
            t0 = ti * TT
            sl = ti % 2
            h = ht[sl]
            hres = ("A", "h", sl)
            f = ft[ti % len(ft)]
            fk = ("A", "f", ti % len(ft))
            for j in range(NH):
                bg_, bu_ = self.bank(), self.bank()
                for c in range(NCH):
                    S.op(PE, lambda e, bg_=bg_, j=j, c=c, h=h: e.matmul(
                        self.ps[bg_][:, :], lhsT=wg[:, c, j * 128:(j + 1) * 128], rhs=h[:, c, :],
                        start=(c == 0), stop=(c == NCH - 1)),
                        reads=[wg_res[c], hres + (c,)], writes=[("ps", bg_)], inc=(c == NCH - 1))
                for c in range(NCH):
                    S.op(PE, lambda e, bu_=bu_, j=j, c=c, h=h: e.matmul(
                        self.ps[bu_][:, :], lhsT=wu[:, c, j * 128:(j + 1) * 128], rhs=h[:, c, :],
                        start=(c == 0), stop=(c == NCH - 1)),
                        reads=[wu_res[c], hres + (c,)], writes=[("ps", bu_)], inc=(c == NCH - 1))
                s_ = st[j % 3]
                sres = ("A", "s", j % 3)
                S.op(ACT, lambda e, bg_=bg_, s_=s_: e.activation(out=s_[:], in_=self.ps[bg_][:, :], func=AF.Silu),
                     reads=[("ps", bg_)], writes=[sres])
                S.op(DVE, lambda e, bu_=bu_, s_=s_, j=j: e.tensor_tensor(out=ut[:, j, :], in0=self.ps[bu_][:, :], in1=s_[:], op=ALU.mult),
                     reads=[("ps", bu_), sres], writes=[("A", "u", j)])
                if j == 1:
                    if pend_post is not None:
                        post_b(pend_post)
                        pend_post = None
                    if ti + 1 < nt:
                        load_tile(ti + 1)
                if j == NH - 4 and ti + 1 < nt:
                    pre_a(ti + 1)
            if ti + 1 < nt:
                pre_b(ti + 1)
            for c in range(NCH):
                b = self.bank()
                for j in range(NH):
                    S.op(PE, lambda e, b=b, c=c, j=j: e.matmul(
                        self.ps[b][:, :], lhsT=wo[:, j, c * 128:(c + 1) * 128], rhs=ut[:, j, :],
                        start=(j == 0), stop=(j == NH - 1)),
                        reads=[wo_res[j], ("A", "u", j)], writes=[("ps", b)], inc=(j == NH - 1))
                if hp == 0:
                    S.op(ACT, lambda e, b=b, c=c, f=f: e.activation(out=f[:, c, :], in_=self.ps[b][:, :], func=AF.Copy),
                         reads=[("ps", b)], writes=[fk + (c,)])
                    S.dma(SP, ("f1s", c % 4), lambda e, c=c, t0=t0, f=f: e.dma_start(out=F1[c * 128:(c + 1) * 128, t0:t0 + TT], in_=f[:, c, :]),
                          reads=[fk + (c,)], writes=[("F1", t0, c)])
                else:
                    S.op(DVE, lambda e, b=b, c=c, f=f: e.tensor_tensor(out=f[:, c, :], in0=self.ps[b][:, :], in1=f1t[:, c, :], op=ALU.add),
                         reads=[("ps", b), ("A", "f1")], writes=[fk + (c,)])
            if hp == 1:
                if ti + 1 < nt:
                    load_f1(ti + 1)
                pend_post = ti
        if pend_post is not None:
            post_b(pend_post)

    def conv_layer(self, x_src, x_dst, w_in, w_out, cvp, gi_pre, gi_post, dbg=0):
        U = self.dscr("conv_u", [D, self.T + 32], BF16)
        self._conv_p1(x_src, w_in, cvp, gi_pre, U, dbg)
        if dbg == 1:
            return
        self._conv_p2(x_src, x_dst, w_out, cvp, gi_post, U, dbg)

    def _conv_p1(self, x_src, w_in, cvp, gi_pre, U, dbg):
        S, T = self.S, self.T
        TT = 256
        KW, PAD = 31, 15
        import os
        sk = os.environ.get("SKIP", "")
        self.new_phase()
        A = self.A
        win = self.W.tile([128, NCH, 2 * D], BF16, "win")
        if "M" in sk:
            for c in range(NCH):
                S.op(DVE, lambda e, c=c: e.memset(win[:, c, :], 1e30), writes=[("W", "win", c)])
        self.load_w(win, w_in, ("W", "win"), "wl")
        cp = A.tile([128, 296], F32, "cp")
        S.dma(SP, "cpl", lambda e: e.dma_start(out=cp[:], in_=cvp[:, :]), writes=[("A", "cp")])
        zt = A.tile([128, 16], BF16, "z")
        S.op(DVE, lambda e: e.memset(zt[:], 0.0), writes=[("A", "z")])
        for c in range(NCH if "P" not in sk else 0):
            for side in range(2):
                col = 0 if side == 0 else T + 16
                S.dma(SP, ("uz", side), lambda e, c=c, col=col: e.dma_start(
                    out=U[c * 128:(c + 1) * 128, col:col + 16], in_=zt[:]), reads=[("A", "z")], writes=[("U", "pad", side, c)])
        xt = [A.tile([128, NCH, TT], F32, f"x{i}") for i in range(2)]
        ht = [A.tile([128, NCH, TT], BF16, f"h{i}") for i in range(2)]
        sq = A.tile([128, NCH, TT], BF16, "sq")
        rt = [A.tile([128, TT], F32, f"r{i}") for i in range(2)]
        st = [A.tile([128, TT], F32, f"s{i}") for i in range(4)]
        ut = [A.tile([128, NCH, TT], BF16, f"u{i}") for i in range(2)]
        win_res = [("W", "win", c) for c in range(NCH)]
        nt = T // TT
        for ti in range(nt):
            t0 = ti * TT
            sl = ti % 2
            x, h, r, u = xt[sl], ht[sl], rt[sl], ut[sl]
            xres, hres, rres = ("A", "x", sl), ("A", "h", sl), ("A", "r", sl)
            if ti == 0:
                self.load_x(x_src, 0, TT, xt[0], ("A", "x", 0), ("xl", 0))
            if ti + 1 < nt:
                self.load_x(x_src, t0 + TT, TT, xt[1 - sl], ("A", "x", 1 - sl), ("xl", 1 - sl))
            self.prenorm(x, xres, h, hres, sq, ("A", "sq"), r, rres, gi_pre, TT)
            for j in range(NCH):
                b = self.bank()
                for half in range(2):
                    col = half * D + j * 128
                    for c in range(NCH):
                        S.op(PE, lambda e, b=b, half=half, col=col, c=c, h=h: e.matmul(
                            self.ps[b][:, half * TT:(half + 1) * TT], lhsT=win[:, c, col:col + 128], rhs=h[:, c, :],
                            start=(c == 0), stop=(c == NCH - 1)),
                            reads=[win_res[c], hres + (c,)], writes=[("ps", b)], inc=(half == 1 and c == NCH - 1))
                sg = st[j % 4]
                sres = ("A", "s", j % 4)
                S.op(ACT, lambda e, b=b, sg=sg, j=j: e.activation(out=sg[:], in_=self.ps[b][:, TT:2 * TT], func=AF.Sigmoid,
                                                                 bias=cp[:, 8 + j:9 + j]),
                     reads=[("ps", b), ("A", "cp")], writes=[sres])
                S.op(DVE, lambda e, b=b, sg=sg, j=j, u=u: e.scalar_tensor_tensor(
                    out=u[:, j, :], in0=self.ps[b][:, 0:TT], scalar=cp[:, j:j + 1], in1=sg[:], op0=ALU.add, op1=ALU.mult),
                    reads=[("ps", b), sres, ("A", "cp")], writes=[("A", "u", sl, j)])
                if dbg == 2:
                    S.op(DVE, lambda e: e.memset(zt[:, 0:1], 0.0), writes=[("A", "u", sl, j)])
                S.dma(SP, "us", lambda e, j=j, u=u, t0=t0: e.dma_start(
                    out=U[j * 128:(j + 1) * 128, 16 + t0:16 + t0 + TT], in_=u[:, j, :]),
                    reads=[("A", "u", sl, j)] + ([("A", "s", (j + 1) % 4)] if "x" in sk else []), writes=[("U", ti, j)])
                if dbg == 11:
                    dt_ = st[(j + 2) % 4]
                    S.op(POOL, lambda e, j=j, u=u, dt_=dt_: e.tensor_copy(out=dt_[:], in_=u[:, j, :]), reads=[("A", "u", sl, j)], writes=[("A", "s", (j + 2) % 4)])
                    S.dma(SP, ("xs", j), lambda e, j=j, dt_=dt_, t0=t0: e.dma_start(out=x_dst[j * 128:(j + 1) * 128, t0:t0 + TT], in_=dt_[:]),
                          reads=[("A", "s", (j + 2) % 4)], writes=[("X", t0, j)])
    def _conv_p2(self, x_src, x_dst, w_out, cvp, gi_post, U, dbg):
        S, T = self.S, self.T
        TT = 256
        KW, PAD = 31, 15
        nt = T // TT
        import os
        sk = os.environ.get("SKIP", "")
        self.new_phase()
        A = self.A
        uh = [A.tile([128, NCH, TT + 32], BF16, f"uh{i}") for i in range(2)]
        dbt = A.tile([128, NCH, TT], F32, "dbt") if "L" in sk else None
        cp = A.tile([128, 296], F32, "cp")
        S.dma(SP, "cpl", lambda e: e.dma_start(out=cp[:], in_=cvp[:, :]), writes=[("A", "cp")])
        wout = self.W.tile([128, NCH, D], BF16, "wout")
        import os
        sk = os.environ.get("SKIP", "")
        if "w" not in sk:
            self.load_w(wout, w_out, ("W", "wout"), "wl")
        dg = self.W.tile([128, KW * NCH, 128], BF16, "dg")
        for k in range(KW * NCH if "d" not in sk else 0):
            S.op(DVE, lambda e, k=k: e.tensor_scalar(out=dg[:, k, :], in0=self.ident[:], scalar1=cp[:, 48 + k:49 + k],
                                                     scalar2=None, op0=ALU.mult),
                 reads=["c_ident", ("A", "cp")], writes=[("W", "dg", k)])
        epsl = A.tile([128, 1], F32, "epsl")
        S.op(DVE, lambda e: e.memset(epsl[:], float(LN_EPS)), writes=[("A", "epsl")])
        xt = [A.tile([128, NCH, TT], F32, f"x{i}") for i in range(2)]
        vt = A.tile([128, NCH, TT], F32, "v")
        vb = A.tile([128, NCH, TT], BF16, "vb")
        sq = A.tile([128, NCH, TT], BF16, "sq")
        mt = A.tile([128, TT], F32, "m")
        msq = A.tile([128, TT], F32, "msq")
        var = A.tile([128, TT], F32, "var")
        rs = A.tile([128, TT], F32, "rs")
        tmp = [A.tile([128, TT], F32, f"t{i}") for i in range(2)]
        zb = A.tile([128, NCH, TT], BF16, "zb")
        ft = A.tile([128, NCH, TT], F32, "f")
        r2 = A.tile([128, TT], F32, "r2")
        sq2 = A.tile([128, NCH, TT], BF16, "sq2")
        wout_res = [("W", "wout", c) for c in range(NCH)]
        for ti in range(nt):
            t0 = ti * TT
            sl = ti % 2
            x, xres = xt[sl], ("A", "x", sl)
            u, ures = uh[sl], ("A", "uh", sl)

            def loads(tj):
                s2 = tj % 2
                tb = tj * TT
                self.load_x(x_src, tb, TT, xt[s2], ("A", "x", s2), ("xl", s2))
                deps = [("U", tk, c) for tk in (tj - 1, tj, tj + 1) if 0 <= tk < nt for c in range(NCH)]
                deps += [("U", "pad", sd, c) for sd in range(2) for c in range(NCH)]
                Uv = U.rearrange("(c p) t -> p c t", p=128)
                S.dma(SP, ("uhl", s2), lambda e, u2=uh[s2], tb=tb: e.dma_start(out=u2[:], in_=Uv[:, :, tb:tb + TT + 32]),
                      reads=deps, writes=[("A", "uh", s2)])
            if ti == 0:
                loads(0)
            if ti + 1 < nt:
                loads(ti + 1)
            if dbg == 2:
                for c in range(NCH):
                    fx = dbt if dbt is not None else ft
                    S.op(DVE, lambda e, c=c, u=u, fx=fx: e.tensor_copy(out=fx[:, c, :], in_=u[:, c, 16:16 + TT]), reads=[ures], writes=[("A", "f", c)])
                    S.dma(SP, ("xs", c), lambda e, c=c, fx=fx: e.dma_start(out=x_dst[c * 128:(c + 1) * 128, t0:t0 + TT], in_=fx[:, c, :]),
                          reads=[("A", "f", c)], writes=[("X", t0, c)])
                return
            for c in range(NCH):
                b = self.bank()
                for k in range(KW):
                    S.op(PE, lambda e, b=b, c=c, k=k, u=u: e.matmul(
                        self.ps[b][:, :TT], lhsT=dg[:, k * NCH + c, :], rhs=u[:, c, k + 1:k + 1 + TT],
                        start=(k == 0), stop=(k == KW - 1)),
                        reads=[("W", "dg", k * NCH + c), ures], writes=[("ps", b)], inc=(k == KW - 1))
                S.op(ACT, lambda e, b=b, c=c: e.activation(out=vt[:, c, :], in_=self.ps[b][:, :TT], func=AF.Identity,
                                                           bias=cp[:, 16 + c:17 + c]),
                     reads=[("ps", b), ("A", "cp")], writes=[("A", "v", c)])
                S.op(POOL, lambda e, c=c: e.tensor_copy(out=vb[:, c, :], in_=vt[:, c, :]),
                     reads=[("A", "v", c)], writes=[("A", "vb", c)])
                S.op(ACT, lambda e, c=c: e.activation(out=sq[:, c, :], in_=vt[:, c, :], func=AF.Square),
                     reads=[("A", "v", c)], writes=[("A", "sq", c)])
            if dbg == 3:
                for c in range(NCH):
                    S.dma(SP, ("xs", c), lambda e, c=c: e.dma_start(out=x_dst[c * 128:(c + 1) * 128, t0:t0 + TT], in_=vt[:, c, :]),
                          reads=[("A", "v", c)], writes=[("X", t0, c)])
                return
            b = self.bank()
            for half, (src, key) in enumerate(((vb, "vb"), (sq, "sq"))):
                for c in range(NCH):
                    S.op(PE, lambda e, b=b, half=half, src=src, c=c: e.matmul(
                        self.ps[b][:, half * TT:(half + 1) * TT], lhsT=self.ones[:], rhs=src[:, c, :],
                        start=(c == 0), stop=(c == NCH - 1)),
                        reads=["c_ones", ("A", key, c)], writes=[("ps", b)], inc=(half == 1 and c == NCH - 1))
            S.op(DVE, lambda e, b=b: e.tensor_scalar(out=mt[:], in0=self.ps[b][:, 0:TT], scalar1=1.0 / D, scalar2=None,
                                                     op0=ALU.mult), reads=[("ps", b)], writes=[("A", "m")])
            S.op(POOL, lambda e: e.tensor_tensor(out=msq[:], in0=mt[:], in1=mt[:], op=ALU.mult),
                 reads=[("A", "m")], writes=[("A", "msq")])
            S.op(DVE, lambda e, b=b: e.scalar_tensor_tensor(out=var[:], in0=self.ps[b][:, TT:2 * TT], scalar=1.0 / D, in1=msq[:],
                                                            op0=ALU.mult, op1=ALU.subtract),
                 reads=[("ps", b), ("A", "msq")], writes=[("A", "var")])
            S.op(ACT, lambda e: e.activation(out=rs[:], in_=var[:], func=AF.Sqrt, bias=epsl[:, 0:1]),
                 reads=[("A", "var"), ("A", "epsl")], writes=[("A", "rs")])
            S.op(DVE, lambda e: e.reciprocal(out=rs[:], in_=rs[:]), reads=[("A", "rs")], writes=[("A", "rs")])
            for c in range(NCH):
                tm = tmp[c % 2]
                tres = ("A", "t", c % 2)
                S.op(POOL, lambda e, c=c, tm=tm: e.tensor_tensor(out=tm[:], in0=vt[:, c, :], in1=mt[:], op=ALU.subtract),
                     reads=[("A", "v", c), ("A", "m")], writes=[tres])
                S.op(DVE, lambda e, tm=tm: e.tensor_tensor(out=tm[:], in0=tm[:], in1=rs[:], op=ALU.mult),
                     reads=[tres, ("A", "rs")], writes=[tres])
                S.op(ACT, lambda e, c=c, tm=tm: e.activation(out=zb[:, c, :], in_=tm[:], func=AF.Silu,
                                                             scale=cp[:, 24 + c:25 + c], bias=cp[:, 32 + c:33 + c]),
                     reads=[tres, ("A", "cp")], writes=[("A", "zb", c)])
            if dbg == 4:
                for c in range(NCH):
                    S.op(DVE, lambda e, c=c: e.tensor_copy(out=ft[:, c, :], in_=zb[:, c, :]), reads=[("A", "zb", c)], writes=[("A", "f", c)])
                    S.dma(SP, ("xs", c), lambda e, c=c: e.dma_start(out=x_dst[c * 128:(c + 1) * 128, t0:t0 + TT], in_=ft[:, c, :]),
                          reads=[("A", "f", c)], writes=[("X", t0, c)])
                return
            for co in range(NCH):
                b = self.bank()
                for c in range(NCH):
                    S.op(PE, lambda e, b=b, co=co, c=c: e.matmul(
                        self.ps[b][:, :TT], lhsT=wout[:, c, co * 128:(co + 1) * 128], rhs=zb[:, c, :],
                        start=(c == 0), stop=(c == NCH - 1)),
                        reads=[wout_res[c], ("A", "zb", c)], writes=[("ps", b)], inc=(c == NCH - 1))
                S.op(ACT, lambda e, b=b, co=co: e.activation(out=ft[:, co, :], in_=self.ps[b][:, :TT], func=AF.Identity,
                                                             bias=cp[:, 40 + co:41 + co]),
                     reads=[("ps", b), ("A", "cp")], writes=[("A", "f", co)])
            self.postnorm_store(ft, ("A", "f"), x, xres, sq2, ("A", "sq2"), r2, ("A", "r2"),
                                tmp, ("A", "t"), gi_post, x_dst, t0, TT, False)

    def proj_post(self, src, K, w, x_src, x_dst, gi_post, final=False):
        S, T = self.S, self.T
        TT = 256
        KC = K // 128
        self.new_phase()
        A = self.A
        wt = self.W.tile([128, KC, D], BF16, "pw")
        self.load_w(wt, w, ("W", "pw"), "wl")
        yt = [A.tile([128, 2, KC, 128], BF16, f"py{i}") for i in range(2)]
        xt = [A.tile([128, NCH, TT], F32, f"px{i}") for i in range(2)]
        ft = A.tile([128, NCH, TT], F32, "pf")
        sq = A.tile([128, NCH, TT], BF16, "psq")
        r2 = A.tile([128, TT], F32, "pr2")
        tmp = [A.tile([128, TT], F32, f"pt{i}") for i in range(2)]
        for ti in range(T // TT):
            t0 = ti * TT
            sl = ti % 2
            x, xres = xt[sl], ("A", "px", sl)
            y, yres = yt[sl], ("A", "py", sl)

            def loads(tj):
                s2 = tj % 2
                tb = tj * TT
                self.load_x(x_src, tb, TT, xt[s2], ("A", "px", s2), ("xl", s2))
                for ch in range(2):
                    S.dma(SP, ("pyl", s2, ch), lambda e, y2=yt[s2], tb=tb, ch=ch: e.dma_start(
                        out=y2[:, ch, :, :].rearrange("p k t -> p (k t)"), in_=src[tb // 128 + ch, :, :]),
                        reads=[("SRC", src.name, tb // 128 + ch)], writes=[("A", "py", s2, ch)])
            if ti == 0:
                loads(0)
            if ti + 1 < T // TT:
                loads(ti + 1)
            for co in range(NCH):
                b = self.bank()
                for kc in range(KC):
                    S.op(PE, lambda e, b=b, co=co, kc=kc, y=y: e.matmul(
                        self.ps[b][:, :TT].rearrange("p (c t) -> p c t", c=2), lhsT=wt[:, kc, co * 128:(co + 1) * 128], rhs=y[:, :, kc, :],
                        start=(kc == 0), stop=(kc == KC - 1)),
                        reads=[("W", "pw", kc), yres + (0,), yres + (1,)], writes=[("ps", b)], inc=(kc == KC - 1))
                S.op(ACT, lambda e, b=b, co=co: e.activation(out=ft[:, co, :], in_=self.ps[b][:, :TT], func=AF.Copy),
                     reads=[("ps", b)], writes=[("A", "pf", co)])
            self.postnorm_store(ft, ("A", "pf"), x, xres, sq, ("A", "psq"), r2, ("A", "pr2"),
                                tmp, ("A", "pt"), gi_post, x_dst, t0, TT, final)

    def ret_layer(self, x_src, x_dst, w_in, w_out, dec, rope, rcst, gi_pre, gi_post, li):
        T = self.T
        QT = self.dscr(f"r{li}_qT", [T // 128, 128, D], BF16)
        KT = self.dscr(f"r{li}_kT", [T // 128, 128, D], BF16)
        V = self.dscr(f"r{li}_v", [T, 2048], BF16)
        G = self.dscr(f"r{li}_g", [T, 2048], BF16)
        CB = self.dscr(f"r{li}_cb", [T, 2048], BF16)
        YT = self.dscr(f"r{li}_yT", [T // 128, 128, 2048], BF16)
        dbg = getattr(self, "ret_dbg", 9)
        self._ret_inproj(x_src, w_in, rope, gi_pre, QT, KT, V, G)
        if dbg < 1:
            return
        self._ret_scan(dec, rcst, QT, KT, V, G, CB, YT, backward=True)
        if dbg < 2:
            return
        self._ret_scan(dec, rcst, QT, KT, V, G, CB, YT, backward=False)
        if dbg < 3:
            return
        self.proj_post(YT, 2048, w_out, x_src, x_dst, gi_post)

    def _ret_inproj(self, x_src, w_in, rope, gi_pre, QT, KT, V, G):
        S, T = self.S, self.T
        TT = 256
        self.new_phase()
        A = self.A
        win = self.W.tile([128, NCH, 6144], BF16, "rwin")
        self.load_w(win, w_in, ("W", "rwin"), "wl")
        xt = [A.tile([128, NCH, TT], F32, f"x{i}") for i in range(2)]
        ht = [A.tile([128, NCH, TT], BF16, f"h{i}") for i in range(2)]
        sq = A.tile([128, NCH, TT], BF16, "sq")
        rt = [A.tile([128, TT], F32, f"r{i}") for i in range(2)]
        rp = [A.tile([128, 4, TT], F32, f"rp{i}") for i in range(2)]
        t4 = [A.tile([128, TT], F32, f"t4{i}") for i in range(4)]
        ob = [A.tile([128, 2, 2, 128], BF16, f"ob{i}") for i in range(2)]
        vb = [A.tile([128, 512], BF16, f"vb{i}") for i in range(3)]
        win_res = [("W", "rwin", c) for c in range(NCH)]
        n_ob = 0
        n_vb = 0
        for ti in range(T // TT):
            t0 = ti * TT
            sl = ti % 2
            x, h, r, rpt = xt[sl], ht[sl], rt[sl], rp[sl]
            xres, hres, rres, rpres = ("A", "x", sl), ("A", "h", sl), ("A", "r", sl), ("A", "rp", sl)
            def loads(tj):
                s2 = tj % 2
                tb = tj * TT
                self.load_x(x_src, tb, TT, xt[s2], ("A", "x", s2), ("xl", s2))
                S.dma(SP, ("rpl", s2), lambda e, rp2=rp[s2], tb=tb: e.dma_start(out=rp2[:], in_=rope[:, :, tb:tb + TT]),
                      writes=[("A", "rp", s2)])
            if ti == 0:
                loads(0)
            if ti + 1 < T // TT:
                loads(ti + 1)
            self.prenorm(x, xres, h, hres, sq, ("A", "sq"), r, rres, gi_pre, TT)
            for qk in range(2):
                dst = QT if qk == 0 else KT
                for hd in range(4):
                    b = self.bank()
                    for half in range(2):
                        col = qk * 1024 + hd * 256 + half * 128
                        for c in range(NCH):
                            S.op(PE, lambda e, b=b, half=half, col=col, c=c, h=h: e.matmul(
                                self.ps[b][:, half * TT:(half + 1) * TT], lhsT=win[:, c, col:col + 128], rhs=h[:, c, :],
                                start=(c == 0), stop=(c == NCH - 1)),
                                reads=[win_res[c], hres + (c,)], writes=[("ps", b)], inc=(half == 1 and c == NCH - 1))
                    o = ob[n_ob % 2]
                    ores = ("A", "ob", n_ob % 2)
                    n_ob += 1
                    cs, sn = rpt[:, 2 * qk, :], rpt[:, 2 * qk + 1, :]
                    p1, p2 = self.ps[b][:, 0:TT], self.ps[b][:, TT:2 * TT]
                    for k4, (pa, tb) in enumerate(((p1, cs), (p2, sn), (p2, cs), (p1, sn))):
                        S.op(DVE, lambda e, k4=k4, pa=pa, tb=tb: e.tensor_tensor(out=t4[k4][:], in0=pa, in1=tb, op=ALU.mult),
                             reads=[("ps", b), rpres], writes=[("A", "t4", k4)])
                    S.op(POOL, lambda e, o=o: e.tensor_tensor(out=o[:, :, 0, :], in0=t4[0][:].rearrange("p (c t) -> p c t", c=2),
                                                              in1=t4[1][:].rearrange("p (c t) -> p c t", c=2), op=ALU.subtract),
                         reads=[("A", "t4", 0), ("A", "t4", 1)], writes=[ores])
                    S.op(POOL, lambda e, o=o: e.tensor_tensor(out=o[:, :, 1, :], in0=t4[2][:].rearrange("p (c t) -> p c t", c=2),
                                                              in1=t4[3][:].rearrange("p (c t) -> p c t", c=2), op=ALU.add),
                         reads=[("A", "t4", 2), ("A", "t4", 3)], writes=[ores])
                    for ch in range(2):
                        jc = t0 // 128 + ch
                        S.dma(SP, ("qks", ch), lambda e, o=o, ch=ch, jc=jc, dst=dst, hd=hd: e.dma_start(
                            out=dst[jc, :, hd * 256:(hd + 1) * 256], in_=o[:, ch, :, :].rearrange("p a t -> p (a t)")),
                            reads=[ores], writes=[("SRC", dst.name, jc, hd)])
            for blk in range(TT // 128):
                for vg in range(2):
                    dst = V if vg == 0 else G
                    for grp in range(4):
                        b = self.bank()
                        col = 2048 + vg * 2048 + grp * 512
                        for c in range(NCH):
                            S.op(PE, lambda e, b=b, col=col, c=c, h=h, blk=blk: e.matmul(
                                self.ps[b][:, :], lhsT=h[:, c, blk * 128:(blk + 1) * 128], rhs=win[:, c, col:col + 512],
                                start=(c == 0), stop=(c == NCH - 1)),
                                reads=[win_res[c], hres + (c,)], writes=[("ps", b)], inc=(c == NCH - 1))
                        v = vb[n_vb % 3]
                        vres = ("A", "vb", n_vb % 3)
                        n_vb += 1
                        S.op(ACT, lambda e, b=b, v=v, vg=vg: e.activation(out=v[:], in_=self.ps[b][:, :],
                                                                         func=(AF.Copy if vg == 0 else AF.Silu)),
                             reads=[("ps", b)], writes=[vres])
                        S.dma(SP, ("vgs", n_vb % 3), lambda e, v=v, dst=dst, grp=grp, t0=t0, blk=blk: e.dma_start(
                            out=dst[t0 + blk * 128:t0 + (blk + 1) * 128, grp * 512:(grp + 1) * 512], in_=v[:]),
                            reads=[vres], writes=[("SRC", dst.name, t0 // 128 + blk, grp)])

    def _ret_scan(self, dec, rcst, QT, KT, V, G, CB, YT, backward):
        S, T = self.S, self.T
        n = T // 128
        self.new_phase()
        A = self.A
        dct = A.tile([128, 8], F32, "dct")
        lg = A.tile([128, 8], F32, "lg")
        rc = A.tile([128, 772], F32, "rc")
        S.dma(SP, "dcl", lambda e: e.dma_start(out=dct[:], in_=dec[:, :]), writes=[("A", "dct")])
        S.dma(SP, "rcl", lambda e: e.dma_start(out=rc[:], in_=rcst[:, :]), writes=[("A", "rc")])
        S.op(ACT, lambda e: e.activation(out=lg[:], in_=dct[:], func=AF.Exp), reads=[("A", "dct")], writes=[("A", "lg")])
        S.op(DVE, lambda e: e.tensor_scalar(out=lg[:], in0=lg[:], scalar1=-1.0, scalar2=1.0, op0=ALU.mult, op1=ALU.add),
             reads=[("A", "lg")], writes=[("A", "lg")])
        S.op(ACT, lambda e: e.activation(out=lg[:], in_=lg[:], func=AF.Ln), reads=[("A", "lg")], writes=[("A", "lg")])
        d0 = 4 if backward else 0
        xi8 = A.tile([128, 8, 128], F32, "xi8")
        zeta = A.tile([128, 4], F32, "zeta")
        cdk = A.tile([128, 4], F32, "cdk")
        eps1 = A.tile([128, 1], F32, "eps1")
        S.op(DVE, lambda e: e.memset(eps1[:], float(RMS_EPS)), writes=[("A", "eps1")])
        rowc = 640 if backward else 512
        colc = 769 if backward else 768
        for hd in range(4):
            sc = lg[:, d0 + hd:d0 + hd + 1]
            for half in range(2):
                S.op(ACT, lambda e, hd=hd, half=half, sc=sc: e.activation(out=xi8[:, hd * 2 + half, :], in_=rc[:, rowc:rowc + 128],
                                                                          func=AF.Exp, scale=sc),
                     reads=[("A", "rc"), ("A", "lg")], writes=[("A", "xi8")])
            S.op(ACT, lambda e, hd=hd, sc=sc: e.activation(out=zeta[:, hd:hd + 1], in_=rc[:, colc:colc + 1], func=AF.Exp, scale=sc),
                 reads=[("A", "rc"), ("A", "lg")], writes=[("A", "zeta")])
            S.op(ACT, lambda e, hd=hd, sc=sc: e.activation(out=cdk[:, hd:hd + 1], in_=rc[:, 770:771], func=AF.Exp, scale=sc),
                 reads=[("A", "rc"), ("A", "lg")], writes=[("A", "cdk")])
        DT = A.tile([128, 4, 128], F32, "DT")
        dtmp = A.tile([128, 128], F32, "dtmp")
        if not backward:
            for hd in range(4):
                S.op(ACT, lambda e, hd=hd: e.activation(out=DT[:, hd, :], in_=rc[:, 0:128], func=AF.Exp, scale=lg[:, hd:hd + 1]),
                     reads=[("A", "rc"), ("A", "lg")], writes=[("A", "DT", hd)])
                S.op(DVE, lambda e, hd=hd: e.tensor_tensor(out=DT[:, hd, :], in0=DT[:, hd, :], in1=rc[:, 128:256], op=ALU.mult),
                     reads=[("A", "DT", hd), ("A", "rc")], writes=[("A", "DT", hd)])
                S.op(ACT, lambda e, hd=hd: e.activation(out=dtmp[:], in_=rc[:, 256:384], func=AF.Exp, scale=lg[:, 4 + hd:5 + hd]),
                     reads=[("A", "rc"), ("A", "lg")], writes=[("A", "dtmp")])
                S.op(DVE, lambda e: e.tensor_tensor(out=dtmp[:], in0=dtmp[:], in1=rc[:, 384:512], op=ALU.mult),
                     reads=[("A", "dtmp"), ("A", "rc")], writes=[("A", "dtmp")])
                S.op(DVE, lambda e, hd=hd: e.tensor_tensor(out=DT[:, hd, :], in0=DT[:, hd, :], in1=dtmp[:], op=ALU.add),
                     reads=[("A", "DT", hd), ("A", "dtmp")], writes=[("A", "DT", hd)])
        DTres = [("A", "DT", hd) for hd in range(4)]
        Sf = A.tile([128, 4, 2, 512], F32, "Sf")
        Sbs = [A.tile([128, 4, 2, 512], BF16, f"Sb{i}") for i in range(2)]
        for hd in range(4):
            S.op(POOL, lambda e, hd=hd: e.memset(Sf[:, hd, :, :], 0.0), writes=[("A", "Sf", hd, 0), ("A", "Sf", hd, 1)])
            S.op(POOL, lambda e, hd=hd: e.memset(Sbs[0][:, hd, :, :], 0.0), writes=[("A", "Sb", 0, hd, 0), ("A", "Sb", 0, hd, 1)])
        qt = [A.tile([128, 8, 128], BF16, f"qt{i}") for i in range(2)]
        kt = [A.tile([128, 8, 128], BF16, f"kt{i}") for i in range(2)]
        vt = [A.tile([128, 2048], BF16, f"vt{i}") for i in range(2)]
        qx = A.tile([128, 8, 128], BF16, "qx")
        kz = A.tile([128, 4, 256], BF16, "kz")
        if backward:
            cbo = [A.tile([128, 512], BF16, f"cbo{i}") for i in range(2)]
        else:
            gt = [A.tile([128, 2048], BF16, f"gt{i}") for i in range(2)]
            cbt = [A.tile([128, 2048], BF16, f"cbt{i}") for i in range(2)]
            Pm = A.tile([128, 512], BF16, "Pm")
            ot = A.tile([128, 4, 512], F32, "ot")
            junk = A.tile([128, 512], BF16, "junk")
            ss = A.tile([128, 4], F32, "ss")
            ybs = [A.tile([128, 2048], BF16, f"yb{i}") for i in range(2)]
            yTs = [A.tile([128, 16, 128], BF16, f"yT{i}") for i in range(2)]
        order = list(range(n - 1, -1, -1)) if backward else list(range(n))
        ncb = 0
        pend_y = None
        for it, jc in enumerate(order):
            sl = it % 2
            c0 = jc * 128
            if not backward:
                yb = ybs[sl]
                yT = yTs[sl]
            q, k, v = qt[sl], kt[sl], vt[sl]
            qres, kres, vres = ("A", "qt", sl), ("A", "kt", sl), ("A", "vt", sl)
            def loads(it2):
                s2 = it2 % 2
                j2 = order[it2]
                cb0 = j2 * 128
                S.dma(SP, ("ql", s2), lambda e, q2=qt[s2], j2=j2: e.dma_start(out=q2[:].rearrange("p c t -> p (c t)"), in_=QT[j2, :, :]),
                      reads=[("SRC", QT.name, j2, h4) for h4 in range(4)], writes=[("A", "qt", s2)])
                S.dma(SP, ("kl", s2), lambda e, k2=kt[s2], j2=j2: e.dma_start(out=k2[:].rearrange("p c t -> p (c t)"), in_=KT[j2, :, :]),
                      reads=[("SRC", KT.name, j2, h4) for h4 in range(4)], writes=[("A", "kt", s2)])
                S.dma(SP, ("vl", s2), lambda e, v2=vt[s2], cb0=cb0: e.dma_start(out=v2[:], in_=V[cb0:cb0 + 128, :]),
                      reads=[("SRC", V.name, j2, g4) for g4 in range(4)], writes=[("A", "vt", s2)])
                if not backward:
                    S.dma(SP, ("gl", s2), lambda e, g2=gt[s2], cb0=cb0: e.dma_start(out=g2[:], in_=G[cb0:cb0 + 128, :]),
                          reads=[("SRC", G.name, j2, g4) for g4 in range(4)], writes=[("A", "gt", s2)])
                    S.dma(SP, ("cbl", s2), lambda e, c2=cbt[s2], cb0=cb0: e.dma_start(out=c2[:], in_=CB[cb0:cb0 + 128, :]),
                          reads=[("SRC", CB.name, j2, g4) for g4 in range(4)], writes=[("A", "cbt", s2)])
            if it == 0:
                loads(0)
            if it + 1 < n:
                loads(it + 1)
            if not backward:
                g, cb = gt[sl], cbt[sl]
                gres, cbres = ("A", "gt", sl), ("A", "cbt", sl)
            S.op(DVE, lambda e, q=q: e.tensor_tensor(out=qx[:], in0=q[:], in1=xi8[:], op=ALU.mult),
                 reads=[qres, ("A", "xi8")], writes=[("A", "qx")])
            bt = self.bank()
            pbt = self.psb[bt]
            for c in range(8):
                S.op(PE, lambda e, c=c, k=k, pbt=pbt: e.transpose(out=pbt[:, c * 128:(c + 1) * 128], in_=k[:, c, :], identity=self.ident[:]),
                     reads=[kres, "c_ident"], writes=[("ps", bt)], inc=(c == 7))
            for hd in range(4):
                S.op(ACT, lambda e, hd=hd, pbt=pbt: e.activation(out=kz[:, hd, :], in_=pbt[:, hd * 256:(hd + 1) * 256], func=AF.Copy,
                                                                 scale=zeta[:, hd:hd + 1]),
                     reads=[("ps", bt), ("A", "zeta")], writes=[("A", "kz", hd)])
            if not backward:
                bs = self.bank()
                for hd in range(4):
                    for half in range(2):
                        S.op(PE, lambda e, hd=hd, half=half, k=k, q=q, bs=bs: e.matmul(
                            self.ps[bs][:, hd * 128:(hd + 1) * 128], lhsT=k[:, hd * 2 + half, :], rhs=q[:, hd * 2 + half, :],
                            start=(half == 0), stop=(half == 1)),
                            reads=[kres, qres], writes=[("ps", bs)], inc=(hd == 3 and half == 1))
                S.op(DVE, lambda e, bs=bs: e.tensor_tensor(out=Pm[:], in0=self.ps[bs][:], in1=DT[:].rearrange("p h i -> p (h i)"),
                                                           op=ALU.mult),
                     reads=[("ps", bs)] + DTres, writes=[("A", "Pm")])
            Sc, Sn = Sbs[it % 2], Sbs[(it + 1) % 2]
            for hd in range(4):
                for half in range(2):
                    b = self.bank()
                    S.op(PE, lambda e, hd=hd, half=half, b=b, v=v: e.matmul(
                        self.ps[b][:], lhsT=kz[:, hd, half * 128:(half + 1) * 128], rhs=v[:, hd * 512:(hd + 1) * 512],
                        start=True, stop=True),
                        reads=[("A", "kz", hd), vres], writes=[("ps", b)])
                    S.op(DVE, lambda e, hd=hd, half=half, b=b: e.scalar_tensor_tensor(
                        out=Sf[:, hd, half, :], in0=Sf[:, hd, half, :], scalar=cdk[:, hd:hd + 1], in1=self.ps[b][:],
                        op0=ALU.mult, op1=ALU.add),
                        reads=[("A", "Sf", hd, half), ("A", "cdk"), ("ps", b)], writes=[("A", "Sf", hd, half)])
                    ceng = ACT if (hd * 2 + half) % 2 == 0 else POOL
                    if ceng == ACT:
                        S.op(ACT, lambda e, hd=hd, half=half, Sn=Sn: e.activation(out=Sn[:, hd, half, :], in_=Sf[:, hd, half, :], func=AF.Copy),
                             reads=[("A", "Sf", hd, half)], writes=[("A", "Sb", (it + 1) % 2, hd, half)])
                    else:
                        S.op(POOL, lambda e, hd=hd, half=half, Sn=Sn: e.tensor_copy(out=Sn[:, hd, half, :], in_=Sf[:, hd, half, :]),
                             reads=[("A", "Sf", hd, half)], writes=[("A", "Sb", (it + 1) % 2, hd, half)])
            for hd in range(4):
                bo = self.bank()
                if not backward:
                    S.op(PE, lambda e, hd=hd, bo=bo, v=v: e.matmul(
                        self.ps[bo][:], lhsT=Pm[:, hd * 128:(hd + 1) * 128], rhs=v[:, hd * 512:(hd + 1) * 512], start=True, stop=False),
                        reads=[("A", "Pm"), vres], writes=[("ps", bo)], inc=False)
                for half in range(2):
                    S.op(PE, lambda e, hd=hd, half=half, bo=bo, Sc=Sc: e.matmul(
                        self.ps[bo][:], lhsT=qx[:, hd * 2 + half, :], rhs=Sc[:, hd, half, :],
                        start=(backward and half == 0), stop=(half == 1)),
                        reads=[("A", "qx"), ("A", "Sb", it % 2, hd, half)], writes=[("ps", bo)], inc=(half == 1))
                if backward:
                    co = cbo[ncb % 2]
                    cores = ("A", "cbo", ncb % 2)
                    ncb += 1
                    S.op(ACT, lambda e, bo=bo, co=co: e.activation(out=co[:], in_=self.ps[bo][:], func=AF.Copy),
                         reads=[("ps", bo)], writes=[cores])
                    S.dma(SP, ("cbs", ncb % 2), lambda e, co=co, c0=c0, hd=hd: e.dma_start(
                        out=CB[c0:c0 + 128, hd * 512:(hd + 1) * 512], in_=co[:]),
                        reads=[cores], writes=[("SRC", CB.name, jc, hd)])
                else:
                    S.op(DVE, lambda e, bo=bo, hd=hd, cb=cb: e.tensor_tensor(out=ot[:, hd, :], in0=self.ps[bo][:],
                                                                             in1=cb[:, hd * 512:(hd + 1) * 512], op=ALU.add),
                         reads=[("ps", bo), cbres], writes=[("A", "ot", hd)])
                    S.op(ACT, lambda e, hd=hd: e.activation(out=junk[:], in_=ot[:, hd, :], func=AF.Square, accum_out=ss[:, hd:hd + 1]),
                         reads=[("A", "ot", hd)], writes=[("A", "junk"), ("A", "ss", hd)])
                    S.op(ACT, lambda e, hd=hd: e.activation(out=ss[:, hd:hd + 1], in_=ss[:, hd:hd + 1], func=AF.Sqrt,
                                                            scale=1.0 / 512.0, bias=eps1[:, 0:1]),
                         reads=[("A", "ss", hd), ("A", "eps1")], writes=[("A", "ss", hd)])
                    S.op(DVE, lambda e, hd=hd: e.reciprocal(out=ss[:, hd:hd + 1], in_=ss[:, hd:hd + 1]),
                         reads=[("A", "ss", hd)], writes=[("A", "ss", hd)])
                    S.op(DVE, lambda e, hd=hd, g=g, yb=yb: e.scalar_tensor_tensor(
                        out=yb[:, hd * 512:(hd + 1) * 512], in0=ot[:, hd, :], scalar=ss[:, hd:hd + 1],
                        in1=g[:, hd * 512:(hd + 1) * 512], op0=ALU.mult, op1=ALU.mult),
                        reads=[("A", "ot", hd), ("A", "ss", hd), gres], writes=[("A", "yb", sl, hd)])
            if not backward:
                def emit_y(yb=yb, yT=yT, sl=sl, c0=c0, jc=jc):
                    for half8 in range(2):
                        by = self.bank()
                        pby = self.psb[by]
                        for bb in range(8):
                            blk = half8 * 8 + bb
                            S.op(PE, lambda e, blk=blk, bb=bb, pby=pby: e.transpose(
                                out=pby[:, bb * 128:(bb + 1) * 128], in_=yb[:, blk * 128:(blk + 1) * 128], identity=self.ident[:]),
                                reads=[("A", "yb", sl, blk // 4), "c_ident"], writes=[("ps", by)], inc=(bb == 7))
                        S.op(ACT, lambda e, half8=half8, pby=pby: e.activation(
                            out=yT[:, half8 * 8:(half8 + 1) * 8, :].rearrange("p a b -> p (a b)"), in_=pby[:, :], func=AF.Copy),
                            reads=[("ps", by)], writes=[("A", "yT", sl, half8)])
                    S.dma(SP, ("yts", sl), lambda e: e.dma_start(out=YT[jc, :, :], in_=yT[:].rearrange("p b t -> p (b t)")),
                          reads=[("A", "yT", sl, 0), ("A", "yT", sl, 1)], writes=[("SRC", YT.name, jc)])
            if not backward:
                if pend_y is not None:
                    pend_y()
                pend_y = emit_y
        if pend_y is not None:
            pend_y()

    DILS = (1, 4, 16)

    def attn_layer(self, x_src, x_dst, w_in, w_out, ropeA, acst, gi_pre, gi_post, dbg=9):
        T = self.T
        HD = [self.dscr(f"a_h{g}", [D, T], BF16) for g in range(3)]
        QT = [self.dscr(f"a_q{g}", [D, T], BF16) for g in range(3)]
        KT = [self.dscr(f"a_k{g}", [D, T], BF16) for g in range(3)]
        VE = [self.dscr(f"a_v{g}", [T + 128 * self.DILS[g], D], BF16) for g in range(3)]
        ND = [self.dscr(f"a_n{g}", [1, D, T], F32) for g in range(3)]
        DEN = [self.dscr(f"a_d{g}", [16, T], F32) for g in range(3)]
        self._attn_prep(x_src, gi_pre, HD)
        if dbg < 1:
            return
        for g in range(3):
            self._attn_inproj(g, HD[g], w_in, ropeA, acst, QT[g], KT[g], VE[g])
        if dbg < 2:
            return
        for g in range(3):
            self._attn_core(g, acst, QT[g], KT[g], VE[g], ND[g], DEN[g])
        if dbg < 3:
            return
        self._attn_out(ND, DEN, acst, w_out, x_src, x_dst, gi_post)

    def _attn_prep(self, x_src, gi_pre, HD):
        S, T = self.S, self.T
        TT = 256
        self.new_phase()
        A = self.A
        Hn = A.tile([128, NCH, T], BF16, "Hn")
        Hp = A.tile([128, NCH, T], BF16, "Hp")
        xt = [A.tile([128, NCH, TT], F32, f"x{i}") for i in range(2)]
        sq = A.tile([128, NCH, TT], BF16, "sq")
        rt = [A.tile([128, TT], F32, f"r{i}") for i in range(2)]
        for ti in range(T // TT):
            t0 = ti * TT
            sl = ti % 2
            x, r = xt[sl], rt[sl]
            xres, rres = ("A", "x", sl), ("A", "r", sl)
            if ti == 0:
                self.load_x(x_src, 0, TT, xt[0], ("A", "x", 0), ("xl", 0))
            if ti + 1 < T // TT:
                self.load_x(x_src, t0 + TT, TT, xt[1 - sl], ("A", "x", 1 - sl), ("xl", 1 - sl))
            self.rstd_of(x, [xres], sq, ("A", "sq"), r, rres, TT)
            for c in range(NCH):
                gsc = self.gs[:, gi_pre * NCH + c: gi_pre * NCH + c + 1]
                S.op(DVE, lambda e, c=c, gsc=gsc, x=x, r=r, t0=t0: e.scalar_tensor_tensor(
                    out=Hn[:, c, t0:t0 + TT], in0=x[:, c, :], scalar=gsc, in1=r[:], op0=ALU.mult, op1=ALU.mult),
                    reads=[xres, rres, "c_gs"], writes=[("A", "Hn", c)])
        for c in range(NCH):
            S.dma(SP, ("hs", c % 4), lambda e, c=c: e.dma_start(out=HD[0][c * 128:(c + 1) * 128, :], in_=Hn[:, c, :]),
                  reads=[("A", "Hn", c)], writes=[("SRC", HD[0].name, c)])
        for gi, dil in ((1, 4), (2, 16)):
            hp = Hp
            for c in range(NCH):
                eng = (POOL, DVE, ACT)[c % 3]
                if eng == ACT:
                    S.op(ACT, lambda e, c=c, hp=hp, dil=dil: e.activation(
                        out=hp[:, c, :].rearrange("p (r l) -> p r l", r=dil),
                        in_=Hn[:, c, :].rearrange("p (l r) -> p r l", r=dil), func=AF.Copy),
                        reads=[("A", "Hn", c)], writes=[("A", "Hp", c)])
                else:
                    S.op(eng, lambda e, c=c, hp=hp, dil=dil: e.tensor_copy(
                        out=hp[:, c, :].rearrange("p (r l) -> p r l", r=dil),
                        in_=Hn[:, c, :].rearrange("p (l r) -> p r l", r=dil)),
                        reads=[("A", "Hn", c)], writes=[("A", "Hp", c)])
                S.dma(SP, ("hs", c % 4), lambda e, c=c, hp=hp, gi=gi: e.dma_start(out=HD[gi][c * 128:(c + 1) * 128, :], in_=hp[:, c, :]),
                      reads=[("A", "Hp", c)], writes=[("SRC", HD[gi].name, c)])

    def _attn_inproj(self, g, Hg, w_in, ropeA, acst, QTg, KTg, VEg):
        S, T = self.S, self.T
        TT = 256
        dil = self.DILS[g]
        L = T // dil
        Lp = L + 128
        self.new_phase()
        A = self.A
        win = self.W.tile([128, NCH, 3072], BF16, "awin")
        self.load_w(win, w_in, ("W", "awin"), "wl", cols=(g * 3072, (g + 1) * 3072))
        win_res = [("W", "awin", c) for c in range(NCH)]
        pmf = A.tile([128, 128], F32, "pmf")
        pm = A.tile([128, 128], BF16, "pm")
        S.dma(SP, "pml", lambda e: e.dma_start(out=pmf[:], in_=acst[:, 0:128]), writes=[("A", "pmf")])
        S.op(DVE, lambda e: e.tensor_copy(out=pm[:], in_=pmf[:]), reads=[("A", "pmf")], writes=[("A", "pm")])
        zt = A.tile([64, D], BF16, "zt")
        S.op(POOL, lambda e: e.memset(zt[:], 0.0), writes=[("A", "zt")])
        for r in range(dil):
            for side in range(2):
                row = r * Lp + (0 if side == 0 else 64 + L)
                S.dma(SP, ("vz", side), lambda e, row=row: e.dma_start(out=VEg[row:row + 64, :], in_=zt[:]),
                      reads=[("A", "zt")], writes=[("SRC", VEg.name, "pad", r, side)])
        stop = getattr(self, "att_stop", 0)
        if stop == 1:
            return
        ht = [A.tile([128, NCH, TT], BF16, f"h{i}") for i in range(2)]
        rp = [A.tile([128, 2, TT], F32, f"rp{i}") for i in range(2)]
        raw = [A.tile([128, TT], BF16, f"raw{i}") for i in range(2)]
        t1 = [A.tile([128, TT], F32, f"t1{i}") for i in range(2)]
        t2 = [A.tile([128, TT], F32, f"t2{i}") for i in range(2)]
        ob = [A.tile([128, TT], BF16, f"ob{i}") for i in range(3)]
        vb = [A.tile([128, 512], BF16, f"vb{i}") for i in range(3)]
        Hv = Hg.rearrange("(c p) t -> p c t", p=128)
        n_o = 0
        n_v = 0
        for ti in range(T // TT):
            t0 = ti * TT
            sl = ti % 2
            h, rpt = ht[sl], rp[sl]
            hres, rpres = ("A", "h", sl), ("A", "rp", sl)
            def loads(tj):
                s2 = tj % 2
                tb = tj * TT
                S.dma(SP, ("hl", s2), lambda e, h2=ht[s2], tb=tb: e.dma_start(out=h2[:], in_=Hv[:, :, tb:tb + TT]),
                      reads=[("SRC", Hg.name, c) for c in range(NCH)], writes=[("A", "h", s2)])
                S.dma(SP, ("rpl", s2), lambda e, rp2=rp[s2], tb=tb: e.dma_start(out=rp2[:], in_=ropeA[g, :, :, tb:tb + TT]),
                      writes=[("A", "rp", s2)])
            if ti == 0:
                loads(0)
            if ti + 1 < T // TT:
                loads(ti + 1)
            if stop == 2:
                return
            pend_ep = None
            for qk in range(2):
                dst = QTg if qk == 0 else KTg
                for cc in range(NCH):
                    ba = self.bank()
                    col = qk * 1024 + cc * 128
                    for c in range(NCH):
                        S.op(PE, lambda e, ba=ba, col=col, c=c, h=h: e.matmul(
                            self.ps[ba][:, :TT], lhsT=win[:, c, col:col + 128], rhs=h[:, c, :],
                            start=(c == 0), stop=(c == NCH - 1)),
                            reads=[win_res[c], hres], writes=[("ps", ba)], inc=(c == NCH - 1))
                    k2 = n_o % 2
                    rw = raw[k2]
                    S.op(ACT, lambda e, ba=ba, rw=rw: e.activation(out=rw[:], in_=self.ps[ba][:, :TT], func=AF.Copy),
                         reads=[("ps", ba)], writes=[("A", "raw", k2)])
                    if pend_ep is not None:
                        pend_ep()

                    def epilogue(ba=ba, k2=k2, rw=rw, n_o=n_o, cc=cc, dst=dst, rpt=rpt, rpres=rpres, t0=t0, ti=ti):
                        a1, a2 = t1[k2], t2[k2]
                        o = ob[n_o % 3]
                        ores = ("A", "ob", n_o % 3)
                        bb = self.bank()
                        S.op(PE, lambda e: e.matmul(self.ps[bb][:, :TT], lhsT=pm[:], rhs=rw[:], start=True, stop=True),
                             reads=[("A", "pm"), ("A", "raw", k2)], writes=[("ps", bb)])
                        S.op(DVE, lambda e: e.tensor_tensor(out=a1[:], in0=self.ps[ba][:, :TT], in1=rpt[:, 0, :], op=ALU.mult),
                             reads=[("ps", ba), rpres, ("A", "raw", k2)], writes=[("A", "t1", k2)])
                        S.op(DVE, lambda e: e.tensor_tensor(out=a2[:], in0=self.ps[bb][:, :TT], in1=rpt[:, 1, :], op=ALU.mult),
                             reads=[("ps", bb), rpres], writes=[("A", "t2", k2)])
                        S.op(POOL, lambda e: e.tensor_tensor(out=o[:], in0=a1[:], in1=a2[:], op=ALU.add),
                             reads=[("A", "t1", k2), ("A", "t2", k2)], writes=[ores])
                        S.dma(SP, ("qks", n_o % 3), lambda e: e.dma_start(
                            out=dst[cc * 128:(cc + 1) * 128, t0:t0 + TT], in_=o[:]),
                            reads=[ores], writes=[("SRC", dst.name, cc, ti)])
                    pend_ep = epilogue
                    n_o += 1
            pend_ep()
            if stop == 3:
                return
            for blk in range(TT // 128):
                pos = t0 + blk * 128
                r, l0 = pos // L, pos % L
                row = r * Lp + 64 + l0
                for grp in range(2):
                    b = self.bank()
                    col = 2048 + grp * 512
                    for c in range(NCH):
                        S.op(PE, lambda e, b=b, col=col, c=c, h=h, blk=blk: e.matmul(
                            self.ps[b][:, :], lhsT=h[:, c, blk * 128:(blk + 1) * 128], rhs=win[:, c, col:col + 512],
                            start=(c == 0), stop=(c == NCH - 1)),
                            reads=[win_res[c], hres], writes=[("ps", b)], inc=(c == NCH - 1))
                    v = vb[n_v % 3]
                    vres = ("A", "vb", n_v % 3)
                    n_v += 1
                    S.op(ACT, lambda e, b=b, v=v: e.activation(out=v[:], in_=self.ps[b][:, :], func=AF.Copy),
                         reads=[("ps", b)], writes=[vres])
                    S.dma(SP, ("vgs", n_v % 3), lambda e, v=v, row=row, grp=grp: e.dma_start(
                        out=VEg[row:row + 128, grp * 512:(grp + 1) * 512], in_=v[:]),
                        reads=[vres], writes=[("SRC", VEg.name, pos // 128, grp)])

    def _attn_core(self, g, acst, QTg, KTg, VEg, NDg, DENg):
        S, T = self.S, self.T
        dil = self.DILS[g]
        L = T // dil
        Lp = L + 128
        nqb = L // 128
        nkb = nqb + 1
        self.new_phase()
        A = self.A
        mbf = A.tile([128, 1024], F32, "mbf")
        m01 = A.tile([128, 4, 2, 2, 128], BF16, "m01")
        S.dma(SP, "mbl", lambda e: e.dma_start(out=mbf[:], in_=acst[:, 128:1152]), writes=[("A", "mbf")])
        for hh in range(2):
            S.op(DVE, lambda e, hh=hh: e.tensor_scalar(
                out=m01[:, :, :, hh, :], in0=mbf[:].rearrange("p (v k q) -> p v k q", v=4, k=2), scalar1=-1.0, scalar2=None,
                op0=ALU.is_ge), reads=[("A", "mbf")], writes=[("A", "m01")])
        kt = [A.tile([128, dil * Lp], BF16, f"kt{i}") for i in range(2)]
        qt = [A.tile([128, 2, T], BF16, f"qt{i}") for i in range(2)]
        ve = [A.tile([128, dil * nkb, 128], BF16, f"ve{i}") for i in range(2)]
        for i in range(2):
            S.op(POOL, lambda e, i=i: e.memset(kt[i][:], 0.0), writes=[("A", "kt", i)])
            S.op(POOL, lambda e, i=i: e.memset(qt[i][:], 0.0), writes=[("A", "qt", i)])
        pmr = [A.tile([128, 512], BF16, f"pr{i}") for i in range(4)]
        pmt = [A.tile([128, 512], BF16, f"pm{i}") for i in range(4)]
        span = 128 * dil if dil > 1 else 512
        stg = [A.tile([64, 2, span], F32, f"stg{i}") for i in range(2)]
        sdn = [A.tile([1, 2, span], F32, f"sdn{i}") for i in range(2)]
        VEv = VEg.rearrange("(n p) f -> p n f", p=128)
        kdeps_all = [[("SRC", KTg.name, cc, ti) for ti in range(T // 256)] for cc in range(NCH)]
        qdeps_all = [[("SRC", QTg.name, cc, ti) for ti in range(T // 256)] for cc in range(NCH)]
        vdeps = [("SRC", VEg.name, pb, grp) for pb in range(T // 128) for grp in range(2)]
        vdeps += [("SRC", VEg.name, "pad", r, sd) for r in range(dil) for sd in range(2)]

        def loads(cc):
            sl = cc % 2
            k, q, v = kt[sl], qt[sl], ve[sl]
            kres, qres, vres = ("A", "kt", sl), ("A", "qt", sl), ("A", "ve", sl)
            S.dma(SP, ("kl", sl), lambda e, k=k, cc=cc: e.dma_start(
                out=k[:].rearrange("p (r l) -> p r l", r=dil)[:, :, 64:64 + L],
                in_=KTg[cc * 128:(cc + 1) * 128, :].rearrange("p (r l) -> p r l", r=dil)), reads=kdeps_all[cc], writes=[kres])
            for hh in range(2):
                r0 = cc * 128 + hh * 64
                S.dma(SP, ("ql", sl), lambda e, q=q, hh=hh, r0=r0: e.dma_start(out=q[hh * 64:(hh + 1) * 64, hh, :], in_=QTg[r0:r0 + 64, :]),
                      reads=qdeps_all[cc], writes=[qres])
            S.dma(SP, ("vl", sl), lambda e, v=v, cc=cc: e.dma_start(out=v[:], in_=VEv[:, :, cc * 128:(cc + 1) * 128]),
                  reads=vdeps, writes=[vres])

        nsp = T // span
        items = []
        for cc in range(NCH):
            for sp_i in range(nsp):
                if dil > 1:
                    blocks = [(r, sp_i) for r in range(dil)]
                else:
                    blocks = [(0, sp_i * 4 + b4) for b4 in range(4)]
                for bi, (r, i) in enumerate(blocks):
                    items.append((cc, sp_i, bi, r, i, bi == len(blocks) - 1))
        state = {"n_p": 0}

        def stage_a(item):
            cc, sp_i, bi, r, i, last = item
            sl = cc % 2
            k, q = kt[sl], qt[sl]
            kres, qres = ("A", "kt", sl), ("A", "qt", sl)
            var = 0
            if i == 0:
                var = 1
            if i == nqb - 1:
                var = 2 if var == 0 else 3
            bs = self.bank()
            qc = r * L + i * 128
            for kb in range(2):
                kc = r * Lp + (i + kb) * 128
                S.op(PE, lambda e, bs=bs, kb=kb, kc=kc, qc=qc, k=k, q=q: e.matmul(
                    self.ps[bs][:, kb * 256:(kb + 1) * 256].rearrange("p (h q) -> p h q", h=2),
                    lhsT=k[:, kc:kc + 128], rhs=q[:, :, qc:qc + 128], start=True, stop=True),
                    reads=[kres, qres], writes=[("ps", bs)], inc=(kb == 1))
            np_ = state["n_p"]
            state["n_p"] += 1
            pr = pmr[np_ % 4]
            prres = ("A", "pr", np_ % 4)
            pmx = pmt[np_ % 4]
            pres = ("A", "pm", np_ % 4)
            S.op(ACT, lambda e, bs=bs, pr=pr: e.activation(out=pr[:], in_=self.ps[bs][:], func=AF.Exp, scale=0.125),
                 reads=[("ps", bs)], writes=[prres])
            meng = POOL if np_ % 4 != 0 else DVE
            S.op(meng, lambda e, pr=pr, pmx=pmx, var=var: e.tensor_tensor(
                out=pmx[:], in0=pr[:], in1=m01[:, var, :, :, :].rearrange("p k h q -> p (k h q)"), op=ALU.mult),
                reads=[prres, ("A", "m01")], writes=[pres])
            return pmx, pres

        def stage_b(item, pmx, pres, st, stres):
            cc, sp_i, bi, r, i, last = item
            sl = cc % 2
            v = ve[sl]
            vres = ("A", "ve", sl)
            bo = self.bank()
            for hh in range(2):
                for kb in range(2):
                    kbi = r * nkb + i + kb
                    S.op(PE, lambda e, bo=bo, hh=hh, kb=kb, kbi=kbi, pmx=pmx, v=v: e.matmul(
                        self.ps[bo][0:64, hh * 128:(hh + 1) * 128], lhsT=v[:, kbi, hh * 64:(hh + 1) * 64],
                        rhs=pmx[:, kb * 256 + hh * 128:kb * 256 + (hh + 1) * 128],
                        start=(kb == 0), stop=(kb == 1)),
                        reads=[vres, pres], writes=[("ps", bo)], inc=False)
            for kb in range(2):
                S.op(PE, lambda e, bo=bo, kb=kb, pmx=pmx: e.matmul(
                    self.ps[bo][0:64, 256:512], lhsT=self.ones[:, 0:64], rhs=pmx[:, kb * 256:(kb + 1) * 256],
                    start=(kb == 0), stop=(kb == 1)),
                    reads=["c_ones", pres], writes=[("ps", bo)], inc=(kb == 1))
            sd = sdn[0] if st is stg[0] else sdn[1]
            if dil > 1:
                dstv = st[:].rearrange("p h (l r) -> p h l r", r=dil)[:, :, :, r]
                dstd = sd[:].rearrange("p h (l r) -> p h l r", r=dil)[:, :, :, r]
            else:
                dstv = st[:, :, bi * 128:(bi + 1) * 128]
                dstd = sd[:, :, bi * 128:(bi + 1) * 128]
            S.op(DVE, lambda e, bo=bo, dstv=dstv: e.tensor_copy(
                out=dstv, in_=self.ps[bo][0:64, 0:256].rearrange("p (h q) -> p h q", h=2)),
                reads=[("ps", bo)], writes=[stres])
            S.op(DVE, lambda e, bo=bo, dstd=dstd: e.tensor_copy(
                out=dstd, in_=self.ps[bo][0:1, 256:512].rearrange("p (h q) -> p h q", h=2)),
                reads=[("ps", bo)], writes=[stres + ("d",)])
            if last:
                for hh in range(2):
                    r0 = cc * 128 + hh * 64
                    S.dma(SP, ("nds", hh), lambda e, st=st, hh=hh, r0=r0, sp_i=sp_i: e.dma_start(
                        out=NDg[0, r0:r0 + 64, sp_i * span:(sp_i + 1) * span], in_=st[:, hh, :]),
                        reads=[stres], writes=[("SRC", NDg.name, cc, sp_i, hh)])
                    S.dma(SP, ("dns", hh), lambda e, sd=sd, hh=hh, sp_i=sp_i: e.dma_start(
                        out=DENg[cc * 2 + hh:cc * 2 + hh + 1, sp_i * span:(sp_i + 1) * span], in_=sd[0:1, hh, :]),
                        reads=[stres + ("d",)], writes=[("SRC", DENg.name, cc, sp_i, hh)])

        loads(0)
        queue = []
        n_s = 0
        k_in_chunk = 0
        PDEPTH = 3
        for idx, item in enumerate(items):
            cc, sp_i, bi, r, i, last = item
            if bi == 0 and sp_i == 0:
                k_in_chunk = 0
            a = stage_a(item)
            if bi == 0:
                cur_st = stg[n_s % 2]
                cur_stres = ("A", "stg", n_s % 2)
                n_s += 1
            queue.append((item, a[0], a[1], cur_st, cur_stres))
            if len(queue) >= PDEPTH:
                stage_b(*queue.pop(0))
            k_in_chunk += 1
            if k_in_chunk == PDEPTH and cc + 1 < NCH:
                loads(cc + 1)
        while queue:
            stage_b(*queue.pop(0))

    def _attn_out(self, ND, DEN, acst, w_out, x_src, x_dst, gi_post):
        S, T = self.S, self.T
        TT = 256
        self.new_phase()
        A = self.A
        wt = self.W.tile([128, NCH, D], BF16, "aw")
        self.load_w(wt, w_out, ("W", "aw"), "wl")
        sel = A.tile([16, NCH, 128], F32, "sel")
        S.dma(SP, "sell", lambda e: e.dma_start(out=sel[:].rearrange("p c m -> p (c m)"), in_=acst[0:16, 1152:2176]), writes=[("A", "sel")])
        nt = [[A.tile([128, NCH, TT], F32, f"n{g}{i}") for i in range(2)] for g in range(3)]
        dn = [[A.tile([16, TT], F32, f"d{g}{i}") for i in range(2)] for g in range(3)]
        ob = A.tile([128, NCH, TT], BF16, "ob")
        xt = [A.tile([128, NCH, TT], F32, f"x{i}") for i in range(2)]
        ft = A.tile([128, NCH, TT], F32, "f")
        sq = A.tile([128, NCH, TT], BF16, "sq")
        r2 = A.tile([128, TT], F32, "r2")
        tmp = [A.tile([128, TT], F32, f"t{i}") for i in range(2)]

        def loads(tj):
            s2 = tj % 2
            tb = tj * TT
            self.load_x(x_src, tb, TT, xt[s2], ("A", "x", s2), ("xl", s2))
            for g in range(3):
                dil = self.DILS[g]
                span = 128 * dil if dil > 1 else 512
                deps = [("SRC", ND[g].name, cc, tb // span, hh) for cc in range(NCH) for hh in range(2)]
                ddeps = [("SRC", DEN[g].name, cc, tb // span, hh) for cc in range(NCH) for hh in range(2)]
                src = ND[g][0].rearrange("(c p) t -> p c t", p=128)
                S.dma(SP, ("ndl", g, 0, s2), lambda e, tl=nt[g][s2], src=src, tb=tb: e.dma_start(out=tl[:], in_=src[:, :, tb:tb + TT]),
                      reads=deps, writes=[("A", "nd", g, 0, s2)])
                S.dma(SP, ("ndl", g, 1, s2), lambda e, tl=dn[g][s2], g=g, tb=tb: e.dma_start(out=tl[:], in_=DEN[g][:, tb:tb + TT]),
                      reads=ddeps, writes=[("A", "nd", g, 1, s2)])

        for ti in range(T // TT):
            t0 = ti * TT
            sl = ti % 2
            x, xres = xt[sl], ("A", "x", sl)
            if ti == 0:
                loads(0)
            if ti + 1 < T // TT:
                loads(ti + 1)
            n0, n1, n2 = nt[0][sl], nt[1][sl], nt[2][sl]
            d0, d1, d2 = dn[0][sl], dn[1][sl], dn[2][sl]
            S.op(POOL, lambda e, n0=n0, n1=n1: e.tensor_tensor(out=n0[:], in0=n0[:], in1=n1[:], op=ALU.add),
                 reads=[("A", "nd", 0, 0, sl), ("A", "nd", 1, 0, sl)], writes=[("A", "nd", 0, 0, sl)])
            S.op(POOL, lambda e, n0=n0, n2=n2: e.tensor_tensor(out=n0[:], in0=n0[:], in1=n2[:], op=ALU.add),
                 reads=[("A", "nd", 0, 0, sl), ("A", "nd", 2, 0, sl)], writes=[("A", "nd", 0, 0, sl)])
            S.op(DVE, lambda e, d0=d0, d1=d1: e.tensor_tensor(out=d0[:], in0=d0[:], in1=d1[:], op=ALU.add),
                 reads=[("A", "nd", 0, 1, sl), ("A", "nd", 1, 1, sl)], writes=[("A", "nd", 0, 1, sl)])
            S.op(DVE, lambda e, d0=d0, d2=d2: e.tensor_tensor(out=d0[:], in0=d0[:], in1=d2[:], op=ALU.add),
                 reads=[("A", "nd", 0, 1, sl), ("A", "nd", 2, 1, sl)], writes=[("A", "nd", 0, 1, sl)])
            S.op(DVE, lambda e, d0=d0: e.reciprocal(out=d0[:], in_=d0[:]), reads=[("A", "nd", 0, 1, sl)], writes=[("A", "nd", 0, 1, sl)])
            for c in range(NCH):
                bq = self.bank()
                S.op(PE, lambda e, bq=bq, c=c, d0=d0: e.matmul(self.ps[bq][:, :TT], lhsT=sel[:, c, :], rhs=d0[:], start=True, stop=True),
                     reads=[("A", "sel"), ("A", "nd", 0, 1, sl)], writes=[("ps", bq)])
                S.op(DVE, lambda e, bq=bq, c=c, n0=n0: e.tensor_tensor(out=ob[:, c, :], in0=self.ps[bq][:, :TT], in1=n0[:, c, :], op=ALU.mult),
                     reads=[("ps", bq), ("A", "nd", 0, 0, sl)], writes=[("A", "ob", c)])
            for co in range(NCH):
                b = self.bank()
                for kc in range(NCH):
                    S.op(PE, lambda e, b=b, co=co, kc=kc: e.matmul(
                        self.ps[b][:, :TT], lhsT=wt[:, kc, co * 128:(co + 1) * 128], rhs=ob[:, kc, :],
                        start=(kc == 0), stop=(kc == NCH - 1)),
                        reads=[("W", "aw", kc), ("A", "ob", kc)], writes=[("ps", b)], inc=(kc == NCH - 1))
                S.op(ACT, lambda e, b=b, co=co: e.activation(out=ft[:, co, :], in_=self.ps[b][:, :TT], func=AF.Copy),
                     reads=[("ps", b)], writes=[("A", "f", co)])
            self.postnorm_store(ft, ("A", "f"), x, xres, sq, ("A", "sq"), r2, ("A", "r2"),
                                tmp, ("A", "t"), gi_post, x_dst, t0, TT, False)

    def finish(self):
        self.S.final_wait(SP, self.out_toks)
        self.S.emit()
        return self.nc


SEQ = 4096
N_CORES = 8
DEPTH = 4


def _ret_tables(T):
    inv = (1.0 / (10000.0 ** np.linspace(0.0, 1.0, 128, dtype=np.float32))).astype(np.float32)
    ang = (np.arange(T, dtype=np.float32)[None, :] * inv[:, None]).astype(np.float32)
    cos, sin = np.cos(ang).astype(np.float32), np.sin(ang).astype(np.float32)
    rope = np.stack([cos, sin, cos / 16.0, sin / 16.0], axis=1).astype(np.float32)
    m = np.arange(128)[:, None]
    i = np.arange(128)[None, :]
    rc = np.zeros((128, 772), np.float32)
    rc[:, 0:128] = np.maximum(i - m, 0)
    rc[:, 128:256] = (i >= m)
    rc[:, 256:384] = np.maximum(m - i, 0)
    rc[:, 384:512] = (m > i)
    rc[:, 512:640] = i + 1
    rc[:, 640:768] = 128 - i
    rc[:, 768] = 127 - np.arange(128)
    rc[:, 769] = np.arange(128)
    rc[:, 770] = 128.0
    return np.ascontiguousarray(rope), rc


def _attn_tables(T):
    inv = (500000.0 ** (-np.arange(0, 16, 2, dtype=np.float32) / 16.0)).astype(np.float32)
    ropeA = np.zeros((3, 128, 2, T), np.float32)
    d = np.arange(128) % 64
    rot = d < 16
    for g, dil in enumerate(Prog.DILS):
        L = T // dil
        pos = np.arange(T)
        tok = (pos % L) * dil + pos // L
        ang = (tok.astype(np.float32)[None, :] * inv[:, None]).astype(np.float32)
        cos, sin = np.cos(ang).astype(np.float32), np.sin(ang).astype(np.float32)
        ropeA[g, :, 0, :] = 1.0
        ropeA[g, rot, 0, :] = cos[d[rot] % 8]
        ropeA[g, rot, 1, :] = sin[d[rot] % 8]
    ac = np.zeros((128, 2176), np.float32)
    for hsel in range(16):
        ac[hsel, 1152 + (hsel // 2) * 128 + (hsel % 2) * 64:1152 + (hsel // 2) * 128 + (hsel % 2) * 64 + 64] = 1.0
    for m in range(128):
        dd = m % 64
        if dd < 8:
            ac[m + 8, m] = -1.0
        elif dd < 16:
            ac[m - 8, m] = 1.0
    kk = np.arange(128)[:, None]
    qq = np.arange(128)[None, :]
    for var in range(4):
        v0 = (kk >= qq)
        v1 = (kk <= qq)
        if var in (1, 3):
            v0 = v0 & (kk >= 64)
        if var in (2, 3):
            v1 = v1 & (kk < 64)
        ac[:, 128 + var * 256:128 + var * 256 + 128] = np.where(v0, 0.0, -30000.0)
        ac[:, 128 + var * 256 + 128:128 + (var + 1) * 256] = np.where(v1, 0.0, -30000.0)
    return ropeA, ac


def build_program(T=SEQ, depth=DEPTH):
    P = Prog(T)
    xT = P.din("xT", [D, T])
    gT = P.din("gT", [128, 16 * NCH])
    cst = P.din("cst", [128, 128])
    ffn_w_in = P.din("ffn_w_in", [DEPTH, D, 2 * FFN_H])
    ffn_w_out = P.din("ffn_w_out", [DEPTH, FFN_H, D])
    ret_w_in = P.din("ret_w_in", [2, D, 6144])
    ret_w_out = P.din("ret_w_out", [2, 2048, D])
    dec = P.din("dec", [2, 128, 8])
    rope = P.din("rope", [128, 4, T])
    rcst = P.din("rcst", [128, 772])
    conv_w_in = P.din("conv_w_in", [D, 2 * D])
    conv_w_out = P.din("conv_w_out", [D, D])
    cvp = P.din("cvp", [128, 296])
    attn_w_in = P.din("attn_w_in", [D, 9216])
    attn_w_out = P.din("attn_w_out", [D, D])
    ropeA = P.din("ropeA", [3, 128, 2, T])
    acst = P.din("acst", [128, 2176])
    outT = P.dout("outT", [D, T])
    X = P.dscr("xres", [D, T], F32)
    P.consts(gT, cst)
    wsrc = []
    for i in range(depth):
        kind, j = i % 3, i // 3
        if kind == 0:
            wsrc.append((P.convert_w(f"ri{i}", ret_w_in[j], D, 6144), P.convert_w(f"ro{i}", ret_w_out[j], 2048, D)))
        elif kind == 1:
            wsrc.append((P.convert_w(f"ci{i}", conv_w_in, D, 2 * D), P.convert_w(f"co{i}", conv_w_out, D, D)))
        else:
            wsrc.append((P.convert_w(f"ai{i}", attn_w_in, D, 9216), P.convert_w(f"ao{i}", attn_w_out, D, D)))
        wsrc.append((P.convert_w(f"fi{i}", ffn_w_in[i], D, 2 * FFN_H), P.convert_w(f"fo{i}", ffn_w_out[i], FFN_H, D)))
    cur = xT
    for i in range(depth):
        kind, j = i % 3, i // 3
        wm, wf = wsrc[2 * i], wsrc[2 * i + 1]
        if kind == 0:
            P.ret_layer(cur, X, wm[0], wm[1], dec[j], rope, rcst, 4 * i + 0, 4 * i + 1, i)
        elif kind == 1:
            P.conv_layer(cur, X, wm[0], wm[1], cvp, 4 * i + 0, 4 * i + 1)
        else:
            P.attn_layer(cur, X, wm[0], wm[1], ropeA, acst, 4 * i + 0, 4 * i + 1)
        last = (i == depth - 1)
        P.ffn_layer(X, outT if last else X, wf[0], wf[1], 4 * i + 2, 4 * i + 3, final=last)
        cur = X
    return P.finish()


def _pl(v):
    return np.asarray(v, np.float32).reshape(-1, 128).T


def kernel(x, norm_w, ffn_w_in, ffn_w_out, ret_w_in, ret_log1m_decay, ret_w_out,
           conv_w_in, conv_b_in, conv_w_dw, conv_b_dw, conv_ln_g, conv_ln_b,
           conv_w_out, conv_b_out, attn_w_in, attn_w_out):
    f = lambda a: np.ascontiguousarray(np.asarray(a, dtype=np.float32))
    x = f(x)
    B, T, _ = x.shape
    nc = build_program(T)
    rope, rcst = _ret_tables(T)
    ropeA, acst = _attn_tables(T)
    norm_w = f(norm_w)
    gT = f(norm_w.reshape(16, NCH, 128).transpose(2, 0, 1).reshape(128, 16 * NCH))
    dec = f(np.broadcast_to(f(ret_log1m_decay).reshape(2, 1, 8), (2, 128, 8)))
    cvp = f(np.concatenate([_pl(f(conv_b_in)[0]), _pl(f(conv_b_dw)[0]), _pl(f(conv_ln_g)[0]), _pl(f(conv_ln_b)[0]),
                            _pl(f(conv_b_out)[0]),
                            f(conv_w_dw)[0].reshape(31, NCH, 128).transpose(2, 0, 1).reshape(128, 248)], axis=1))
    shared = {
        "gT": gT, "cst": np.eye(128, dtype=np.float32),
        "ffn_w_in": f(ffn_w_in), "ffn_w_out": f(ffn_w_out),
        "ret_w_in": f(ret_w_in), "ret_w_out": f(ret_w_out), "dec": dec, "rope": rope, "rcst": rcst,
        "conv_w_in": f(conv_w_in)[0], "conv_w_out": f(conv_w_out)[0], "cvp": cvp,
        "attn_w_in": f(attn_w_in)[0], "attn_w_out": f(attn_w_out)[0], "ropeA": ropeA, "acst": acst,
    }
    in_maps = []
    for b in range(B):
        m = dict(shared)
        m["xT"] = np.ascontiguousarray(x[b].T)
        in_maps.append(m)
    res = run_bass_kernel_spmd(nc, in_maps, core_ids=list(range(B)))
    out = np.stack([np.ascontiguousarray(r["outT"].T) for r in res.results], axis=0)
    return out.astype(np.float32)
```

```python
import math
import numpy as np
import concourse.bass as bass
import concourse.mybir as mybir
from concourse.bass_utils import run_bass_kernel_spmd

F32 = mybir.dt.float32
BF16 = mybir.dt.bfloat16
ALU = mybir.AluOpType
AF = mybir.ActivationFunctionType
AX = mybir.AxisListType

D = 1024
NCH = D // 128
FFN_H = 2816
RMS_EPS = 1e-6
LN_EPS = 1e-5

PE, ACT, DVE, POOL, SP = "pe", "act", "dve", "pool", "sp"
ENGS = (PE, ACT, DVE, POOL, SP)


class Sched:
    def __init__(self, nc):
        self.nc = nc
        self.ops = {e: [] for e in ENGS}
        self.cnt = {e: 0 for e in ENGS}
        self.sems = {e: nc.alloc_semaphore(f"s_{e}") for e in ENGS}
        self.dsems = {}
        self.seen = {}
        self.res = {}
        self.floor = {}
        self.n_ops = 0
        self.bg = []
        self.bg_every = 2

    def bg_issue_one(self):
        name, q, key, fn, reads, writes = self.bg.pop(0)
        self.dma(q, key, fn, reads=reads, writes=writes)

    def ensure_bg(self, name):
        while any(b[0] == name for b in self.bg):
            self.bg_issue_one()

    def _res(self, r):
        st = self.res.get(r)
        if st is None:
            arena = r[0] if isinstance(r, tuple) else None
            st = {"w": None, "r": dict(self.floor.get(arena, {}))}
            self.res[r] = st
        return st

    def end_phase(self, arena):
        fl = dict(self.floor.get(arena, {}))
        dead = []
        for r, st in self.res.items():
            if isinstance(r, tuple) and r[0] == arena:
                if st["w"] is not None:
                    k, v = st["w"]
                    fl[k] = max(fl.get(k, 0), v)
                for k, v in st["r"].items():
                    fl[k] = max(fl.get(k, 0), v)
                dead.append(r)
        for r in dead:
            del self.res[r]
        self.floor[arena] = fl

    def _semh(self, key):
        if key in self.sems:
            return self.sems[key]
        return self.dsems[key][0]

    def _collect(self, eng, reads, writes, is_dma):
        deps = {}

        def add(tok, same_ok):
            if tok is None:
                return
            k, v = tok
            if same_ok and (not is_dma) and k == eng:
                return
            if v > deps.get(k, 0):
                deps[k] = v

        for r in reads:
            st = self._res(r)
            add(st["w"], False)
            if isinstance(r, tuple) and r[0] == "ps":
                for k, v in st["r"].items():
                    add((k, v), True)
        for w in writes:
            st = self._res(w)
            add(st["w"], True)
            for k, v in st["r"].items():
                add((k, v), True)
        waits = []
        for k, v in deps.items():
            if self.seen.get((eng, k), 0) < v:
                self.seen[(eng, k)] = v
                waits.append((k, v))
        return waits

    def _commit(self, tok, reads, writes):
        k, v = tok
        for r in reads:
            st = self._res(r)
            if v > st["r"].get(k, 0):
                st["r"][k] = v
        for w in writes:
            st = self._res(w)
            st["w"] = tok
            st["r"] = {}

    def op(self, eng, fn, reads=(), writes=(), inc=True):
        waits = self._collect(eng, reads, writes, False)
        if inc:
            self.cnt[eng] += 1
            tok = (eng, self.cnt[eng])
        else:
            tok = (eng, self.cnt[eng] + 1)
        self.ops[eng].append((waits, fn, (eng, 1) if inc else None))
        self._commit(tok, reads, writes)
        self.n_ops += 1
        if eng == POOL and self.bg and self.cnt[POOL] % self.bg_every == 0:
            self.bg_issue_one()
        return tok

    def dma(self, q, key, fn, reads=(), writes=()):
        if key not in self.dsems:
            self.dsems[key] = [self.nc.alloc_semaphore(f"d_{len(self.dsems)}"), 0]
        ds = self.dsems[key]
        waits = self._collect(q, reads, writes, True)
        if ds[1] > 0 and self.seen.get((q, key), 0) < ds[1]:
            self.seen[(q, key)] = ds[1]
            waits.append((key, ds[1]))
        ds[1] += 16
        tok = (key, ds[1])
        self.ops[q].append((waits, fn, (key, 16)))
        self._commit(tok, reads, writes)
        self.n_ops += 1
        return tok

    def final_wait(self, eng, toks):
        waits = []
        for k, v in toks:
            if self.seen.get((eng, k), 0) < v:
                self.seen[(eng, k)] = v
                waits.append((k, v))
        self.ops[eng].append((waits, None, None))

    def emit(self):
        nc = self.nc
        with nc.Block() as block:
            def run(e, name):
                for waits, fn, inc in self.ops[name]:
                    if fn is None:
                        for k, v in waits:
                            e.wait_ge(self._semh(k), v)
                        continue
                    for k, v in waits[:-1]:
                        e.wait_ge(self._semh(k), v)
                    ins = fn(e)
                    if waits:
                        ins._wait_ge(self._semh(waits[-1][0]), waits[-1][1])
                    if inc is not None:
                        ins.then_inc(self._semh(inc[0]), inc[1])

            @block.tensor
            def _(e):
                run(e, PE)

            @block.scalar
            def _(e):
                run(e, ACT)

            @block.vector
            def _(e):
                run(e, DVE)

            @block.gpsimd
            def _(e):
                run(e, POOL)

            @block.sync
            def _(e):
                run(e, SP)


class Arena:
    def __init__(self, nc, name, base, size):
        self.nc, self.name, self.base, self.size = nc, name, base, size
        self.off = 0
        self.uid = 0

    def reset(self):
        self.off = 0

    def tile(self, shape, dtype, tag):
        nbytes = int(np.prod(shape[1:])) * (2 if dtype == BF16 else 4)
        nbytes = (nbytes + 63) // 64 * 64
        assert self.off + nbytes <= self.size, (self.name, tag, self.off, nbytes, self.size)
        self.uid += 1
        t = self.nc.alloc_sbuf_tensor_at(f"{self.name}_{tag}_{self.uid}", list(shape), dtype,
                                         offset=self.base + self.off)
        self.off += nbytes
        return t


SB_BASE = 16512
C_SIZE = 12 * 1024
M_SIZE = 229344 - SB_BASE - C_SIZE


class Prog:
    def __init__(self, T):
        self.T = T
        nc = bass.Bass("TRN2", target_bir_lowering=False)
        self.nc = nc
        self.S = Sched(nc)
        self.C = Arena(nc, "C", SB_BASE, C_SIZE)
        self.W = self.A = Arena(nc, "M", SB_BASE + C_SIZE, M_SIZE)
        self.ps = [nc.alloc_psum_tensor(f"ps{i}", [128, 512], F32) for i in range(8)]
        self.psb = [t.bitcast(BF16) for t in self.ps]
        self.ps_rr = 0
        self.dram = {}
        self.out_toks = []

    def din(self, name, shape, dtype=F32):
        t = self.nc.dram_tensor(name, list(shape), dtype, kind="ExternalInput")
        self.dram[name] = t
        return t

    def dout(self, name, shape, dtype=F32):
        t = self.nc.dram_tensor(name, list(shape), dtype, kind="ExternalOutput")
        self.dram[name] = t
        return t

    def dscr(self, name, shape, dtype):
        t = self.nc.dram_tensor(name, list(shape), dtype)
        self.dram[name] = t
        return t

    def bank(self):
        i = self.ps_rr
        self.ps_rr = (self.ps_rr + 1) % 8
        return i

    def new_phase(self):
        S = self.S
        S.end_phase("W")
        S.end_phase("A")
        fl = dict(S.floor.get("W", {}))
        for k, v in S.floor.get("A", {}).items():
            fl[k] = max(fl.get(k, 0), v)
        S.floor["W"] = dict(fl)
        S.floor["A"] = dict(fl)
        self.A.reset()

    def consts(self, gT, cst):
        S, C = self.S, self.C
        identf = C.tile([128, 128], F32, "identf")
        S.dma(SP, "c_id", lambda e: e.dma_start(out=identf[:], in_=cst[:, 0:128]), writes=["c_identf"])
        self.ones = C.tile([128, 128], BF16, "ones")
        self.ident = C.tile([128, 128], BF16, "ident")
        self.gs = C.tile([128, 16 * NCH], F32, "gs")
        onesf = C.tile([128, 128], F32, "onesf")
        self.epsD = C.tile([128, 2], F32, "epsD")
        S.op(DVE, lambda e: e.memset(self.epsD[:, 0:1], float(D * RMS_EPS)), writes=["c_eps"])
        S.op(DVE, lambda e: e.memset(onesf[:], 1.0), writes=["c_onesf"])
        S.op(DVE, lambda e: e.tensor_copy(out=self.ones[:], in_=onesf[:]), reads=["c_onesf"], writes=["c_ones"])
        S.op(DVE, lambda e: e.tensor_copy(out=self.ident[:], in_=identf[:]), reads=["c_identf"], writes=["c_ident"])
        S.dma(SP, "c_g", lambda e: e.dma_start(out=self.gs[:], in_=gT[:, :]), writes=["c_g0"])
        S.op(DVE, lambda e: e.tensor_scalar(out=self.gs[:], in0=self.gs[:], scalar1=float(math.sqrt(D)),
                                            scalar2=None, op0=ALU.mult), reads=["c_g0"], writes=["c_gs"])

    def convert_w(self, name, w_ap, K, F):
        wb = self.dscr("wb_" + name, [K, F], BF16)
        for c in range(K // 128):
            self.S.bg.append((name, POOL, ("cv", len(self.S.bg) % 8),
                              (lambda e, c=c, wb=wb, w_ap=w_ap: e.dma_start(out=wb[c * 128:(c + 1) * 128, :], in_=w_ap[c * 128:(c + 1) * 128, :])),
                              [], [("WB", name, c)]))
        return (name, wb)

    def load_w(self, dst, wsrc, res, key, cols=None, rows=None):
        name, wb = wsrc
        self.S.ensure_bg(name)
        KC = dst.shape[1]
        c0 = 0 if rows is None else rows[0] // 128
        for c in range(KC):
            cs = c0 + c
            src = wb[cs * 128:(cs + 1) * 128, :] if cols is None else wb[cs * 128:(cs + 1) * 128, cols[0]:cols[1]]
            self.S.dma(ACT, (key, c % 4), lambda e, c=c, src=src: e.dma_start(out=dst[:, c, :], in_=src),
                       reads=[("WB", name, cs)], writes=[res + (c,)])

    def rstd_of(self, src, srcres, sq, sqres, r, rres, TT):
        S = self.S
        S.op(ACT, lambda e: e.activation(out=sq[:, :, :TT], in_=src[:, :, :TT], func=AF.Square),
             reads=srcres, writes=[sqres])
        b = self.bank()
        for c in range(NCH):
            S.op(PE, lambda e, c=c: e.matmul(self.ps[b][:, :TT], lhsT=self.ones[:], rhs=sq[:, c, :TT],
                                             start=(c == 0), stop=(c == NCH - 1)),
                 reads=["c_ones", sqres], writes=[("ps", b)], inc=(c == NCH - 1))
        S.op(ACT, lambda e: e.activation(out=r[:, :TT], in_=self.ps[b][:, :TT], func=AF.Sqrt, bias=self.epsD[:, 0:1]),
             reads=[("ps", b), "c_eps"], writes=[rres])
        S.op(DVE, lambda e: e.reciprocal(out=r[:, :TT], in_=r[:, :TT]), reads=[rres], writes=[rres])

    def load_x(self, x_src, t0, TT, x, xres, key):
        xv = x_src.rearrange("(c p) t -> p c t", p=128)
        self.S.dma(SP, key, lambda e: e.dma_start(out=x[:, :, :TT], in_=xv[:, :, t0:t0 + TT]),
                   reads=[("X", x_src.name, t0, c) for c in range(NCH)], writes=[xres])

    def prenorm(self, x, xres, h, hres, sq, sqres, r, rres, gi, TT):
        S = self.S
        self.rstd_of(x, [xres], sq, sqres, r, rres, TT)
        for c in range(NCH):
            g = self.gs[:, gi * NCH + c: gi * NCH + c + 1]
            S.op(DVE, lambda e, c=c, g=g: e.scalar_tensor_tensor(
                out=h[:, c, :TT], in0=x[:, c, :TT], scalar=g, in1=r[:, :TT], op0=ALU.mult, op1=ALU.mult),
                reads=[xres, rres, "c_gs"], writes=[hres + (c,)])

    def postnorm_store(self, ft, fkey, x, xres, sq, sqres, r2, r2res, tmp, tkey, gi, x_dst, t0, TT, final):
        S = self.S
        fres = [fkey + (c,) for c in range(NCH)]
        self.rstd_of(ft, fres, sq, sqres, r2, r2res, TT)
        for c in range(NCH):
            g = self.gs[:, gi * NCH + c: gi * NCH + c + 1]
            tm = tmp[c % len(tmp)]
            tres = tkey + (c % len(tmp),)
            S.op(POOL, lambda e, c=c, tm=tm: e.tensor_tensor(out=tm[:, :TT], in0=ft[:, c, :TT], in1=r2[:, :TT], op=ALU.mult),
                 reads=[fres[c], r2res], writes=[tres])
            S.op(DVE, lambda e, c=c, g=g, tm=tm: e.scalar_tensor_tensor(
                out=ft[:, c, :TT], in0=tm[:, :TT], scalar=g, in1=x[:, c, :TT], op0=ALU.mult, op1=ALU.add),
                reads=[tres, xres, "c_gs"], writes=[fres[c]])
            tk = S.dma(SP, ("xs", c), lambda e, c=c: e.dma_start(
                out=x_dst[c * 128:(c + 1) * 128, t0:t0 + TT], in_=ft[:, c, :TT]),
                reads=[fres[c]], writes=[("X", x_dst.name, t0, c)])
            if final:
                self.out_toks.append(tk)

    def ffn_layer(self, x_src, x_dst, w_in, w_out, gi_pre, gi_post, final=False):
        F1 = self.dram.get("ffn_f1")
        if F1 is None:
            F1 = self.dscr("ffn_f1", [D, self.T], F32)
        self._ffn_pass(0, x_src, x_dst, w_in, w_out, gi_pre, gi_post, F1, final)
        self._ffn_pass(1, x_src, x_dst, w_in, w_out, gi_pre, gi_post, F1, final)

    def _ffn_pass(self, hp, x_src, x_dst, w_in, w_out, gi_pre, gi_post, F1, final):
        S, T = self.S, self.T
        TT = 512
        NJ = FFN_H // 128
        NH = NJ // 2
        j0 = hp * NH
        self.new_phase()
        A = self.A
        wg = self.W.tile([128, NCH, NH * 128], BF16, "wg")
        wu = self.W.tile([128, NCH, NH * 128], BF16, "wu")
        wo = self.W.tile([128, NH, D], BF16, "wo")
        self.load_w(wg, w_in, ("W", "wg"), "wl", cols=(j0 * 128, (j0 + NH) * 128))
        self.load_w(wu, w_in, ("W", "wu"), "wl", cols=(FFN_H + j0 * 128, FFN_H + (j0 + NH) * 128))
        self.load_w(wo, w_out, ("W", "wo"), "wl", rows=(j0 * 128, (j0 + NH) * 128))
        xt = [A.tile([128, NCH, TT], F32, f"x{i}") for i in range(2)]
        ht = [A.tile([128, NCH, TT], BF16, f"h{i}") for i in range(2)]
        sq = A.tile([128, NCH, TT], BF16, "sq")
        rt = [A.tile([128, TT], F32, f"r{i}") for i in range(2)]
        st = [A.tile([128, TT], F32, f"s{i}") for i in range(3)]
        ut = A.tile([128, NH, TT], BF16, "u")
        ft = [A.tile([128, NCH, TT], F32, f"f{i}") for i in range(1)]
        f1t = A.tile([128, NCH, TT], F32, "f1") if hp == 1 else None
        sq2 = A.tile([128, NCH, TT], BF16, "sq2") if hp == 1 else None
        r2 = A.tile([128, TT], F32, "r2") if hp == 1 else None
        tmp = [A.tile([128, TT], F32, f"t{i}") for i in range(2)]
        wg_res = [("W", "wg", c) for c in range(NCH)]
        wu_res = [("W", "wu", c) for c in range(NCH)]
        wo_res = [("W", "wo", j) for j in range(NH)]
        F1v = F1.rearrange("(c p) t -> p c t", p=128)
        nt = T // TT

        def xr(tj):
            return ("A", "x", tj % 2)

        def load_tile(tj):
            self.load_x(x_src, tj * TT, TT, xt[tj % 2], xr(tj), ("xl", tj % 2))

        def load_f1(tj):
            tb = tj * TT
            S.dma(SP, "f1l", lambda e, tb=tb: e.dma_start(out=f1t[:], in_=F1v[:, :, tb:tb + TT]),
                  reads=[("F1", tb, c) for c in range(NCH)], writes=[("A", "f1")])

        def pre_a(tj):
            x = xt[tj % 2]
            S.op(ACT, lambda e, x=x: e.activation(out=sq[:], in_=x[:], func=AF.Square), reads=[xr(tj)], writes=[("A", "sq")])

        def pre_b(tj):
            sl = tj % 2
            x, h, r = xt[sl], ht[sl], rt[sl]
            hres, rres = ("A", "h", sl), ("A", "r", sl)
            b = self.bank()
            for c in range(NCH):
                S.op(PE, lambda e, c=c, b=b: e.matmul(self.ps[b][:, :], lhsT=self.ones[:], rhs=sq[:, c, :],
                                                      start=(c == 0), stop=(c == NCH - 1)),
                     reads=["c_ones", ("A", "sq")], writes=[("ps", b)], inc=(c == NCH - 1))
            S.op(ACT, lambda e, b=b, r=r: e.activation(out=r[:], in_=self.ps[b][:, :], func=AF.Sqrt, bias=self.epsD[:, 0:1]),
                 reads=[("ps", b), "c_eps"], writes=[rres])
            S.op(DVE, lambda e, r=r: e.reciprocal(out=r[:], in_=r[:]), reads=[rres], writes=[rres])
            for c in range(NCH):
                g = self.gs[:, gi_pre * NCH + c: gi_pre * NCH + c + 1]
                S.op(DVE, lambda e, c=c, g=g, x=x, h=h, r=r: e.scalar_tensor_tensor(
                    out=h[:, c, :], in0=x[:, c, :], scalar=g, in1=r[:], op0=ALU.mult, op1=ALU.mult),
                    reads=[xr(tj), rres, "c_gs"], writes=[hres + (c,)])

        def post_b(tj):
            f = ft[0]
            fk = ("A", "f", 0)
            self.postnorm_store(f, fk, xt[tj % 2], xr(tj), sq2, ("A", "sq2"), r2, ("A", "r2"),
                                tmp, ("A", "t"), gi_post, x_dst, tj * TT, TT, final)

        load_tile(0)
        if hp == 1:
            load_f1(0)
        pre_a(0)
        pre_b(0)
        pend_post = None
        for ti in range(nt):
            t0 = ti * TT
            sl = ti % 2
            h = ht[sl]
            hres = ("A", "h", sl)
            f = ft[ti % len(ft)]
            fk = ("A", "f", ti % len(ft))
            for j in range(NH):
                bg_, bu_ = self.bank(), self.bank()
                for c in range(NCH):
                    S.op(PE, lambda e, bg_=bg_, j=j, c=c, h=h: e.matmul(
                        self.ps[bg_][:, :], lhsT=wg[:, c, j * 128:(j + 1) * 128], rhs=h[:, c, :],
                        start=(c == 0), stop=(c == NCH - 1)),
                        reads=[wg_res[c], hres + (c,)], writes=[("ps", bg_)], inc=(c == NCH - 1))
                for c in range(NCH):
                    S.op(PE, lambda e, bu_=bu_, j=j, c=c, h=h: e.matmul(
                        self.ps[bu_][:, :], lhsT=wu[:, c, j * 128:(j + 1) * 128], rhs=h[:, c, :],
                        start=(c == 0), stop=(c == NCH - 1)),
                        reads=[wu_res[c], hres + (c,)], writes=[("ps", bu_)], inc=(c == NCH - 1))
                s_ = st[j % 3]
                sres = ("A", "s", j % 3)
                S.op(ACT, lambda e, bg_=bg_, s_=s_: e.activation(out=s_[:], in_=self.ps[bg_][:, :], func=AF.Silu),
                     reads=[("ps", bg_)], writes=[sres])
                S.op(DVE, lambda e, bu_=bu_, s_=s_, j=j: e.tensor_tensor(out=ut[:, j, :], in0=self.ps[bu_][:, :], in1=s_[:], op=ALU.mult),
                     reads=[("ps", bu_), sres], writes=[("A", "u", j)])
                if j == 1:
                    if pend_post is not None:
                        post_b(pend_post)
                        pend_post = None
                    if ti + 1 < nt:
                        load_tile(ti + 1)
                if j == NH - 4 and ti + 1 < nt:
                    pre_a(ti + 1)
            if ti + 1 < nt:
                pre_b(ti + 1)
            for c in range(NCH):
                b = self.bank()
                for j in range(NH):
                    S.op(PE, lambda e, b=b, c=c, j=j: e.matmul(
                        self.ps[b][:, :], lhsT=wo[:, j, c * 128:(c + 1) * 128], rhs=ut[:, j, :],
                        start=(j == 0), stop=(j == NH - 1)),
                        reads=[wo_res[j], ("A", "u", j)], writes=[("ps", b)], inc=(j == NH - 1))
                if hp == 0:
                    S.op(ACT, lambda e, b=b, c=c, f=f: e.activation(out=f[:, c, :], in_=self.ps[b][:, :], func=AF.Copy),
                         reads=[("ps", b)], writes=[fk + (c,)])
                    S.dma(SP, ("f1s", c % 4), lambda e, c=c, t0=t0, f=f: e.dma_start(out=F1[c * 128:(c + 1) * 128, t0:t0 + TT], in_=f[:, c, :]),
                          reads=[fk + (c,)], writes=[("F1", t0, c)])
                else:
                    S.op(DVE, lambda e, b=b, c=c, f=f: e.tensor_tensor(out=f[:, c, :], in0=self.ps[b][:, :], in1=f1t[:, c, :], op=ALU.add),
                         reads=[("ps", b), ("A", "f1")], writes=[fk + (c,)])
            if hp == 1:
                if ti + 1 < nt:
                    load_f1(ti + 1)
                pend_post = ti
        if pend_post is not None:
            post_b(pend_post)

    def conv_layer(self, x_src, x_dst, w_in, w_out, cvp, gi_pre, gi_post, dbg=0):
        U = self.dscr("conv_u", [D, self.T + 32], BF16)
        self._conv_p1(x_src, w_in, cvp, gi_pre, U, dbg)
        if dbg == 1:
            return
        self._conv_p2(x_src, x_dst, w_out, cvp, gi_post, U, dbg)

    def _conv_p1(self, x_src, w_in, cvp, gi_pre, U, dbg):
        S, T = self.S, self.T
        TT = 256
        KW, PAD = 31, 15
        import os
        sk = os.environ.get("SKIP", "")
        self.new_phase()
        A = self.A
        win = self.W.tile([128, NCH, 2 * D], BF16, "win")
        if "M" in sk:
            for c in range(NCH):
                S.op(DVE, lambda e, c=c: e.memset(win[:, c, :], 1e30), writes=[("W", "win", c)])
        self.load_w(win, w_in, ("W", "win"), "wl")
        cp = A.tile([128, 296], F32, "cp")
        S.dma(SP, "cpl", lambda e: e.dma_start(out=cp[:], in_=cvp[:, :]), writes=[("A", "cp")])
        zt = A.tile([128, 16], BF16, "z")
        S.op(DVE, lambda e: e.memset(zt[:], 0.0), writes=[("A", "z")])
        for c in range(NCH if "P" not in sk else 0):
            for side in range(2):
                col = 0 if side == 0 else T + 16
                S.dma(SP, ("uz", side), lambda e, c=c, col=col: e.dma_start(
                    out=U[c * 128:(c + 1) * 128, col:col + 16], in_=zt[:]), reads=[("A", "z")], writes=[("U", "pad", side, c)])
        xt = [A.tile([128, NCH, TT], F32, f"x{i}") for i in range(2)]
        ht = [A.tile([128, NCH, TT], BF16, f"h{i}") for i in range(2)]
        sq = A.tile([128, NCH, TT], BF16, "sq")
        rt = [A.tile([128, TT], F32, f"r{i}") for i in range(2)]
        st = [A.tile([128, TT], F32, f"s{i}") for i in range(4)]
        ut = [A.tile([128, NCH, TT], BF16, f"u{i}") for i in range(2)]
        win_res = [("W", "win", c) for c in range(NCH)]
        nt = T // TT
        for ti in range(nt):
            t0 = ti * TT
            sl = ti % 2
            x, h, r, u = xt[sl], ht[sl], rt[sl], ut[sl]
            xres, hres, rres = ("A", "x", sl), ("A", "h", sl), ("A", "r", sl)
            if ti == 0:
                self.load_x(x_src, 0, TT, xt[0], ("A", "x", 0), ("xl", 0))
                self.prenorm(x, xres, h, hres, sq, ("A", "sq"), r, rres, gi_pre, TT)
            if ti + 1 < nt:
                self.load_x(x_src, t0 + TT, TT, xt[1 - sl], ("A", "x", 1 - sl), ("xl", 1 - sl))
            for j in range(NCH):
                if j == NCH // 2 and ti + 1 < nt:
                    s2 = 1 - sl
                    self.prenorm(xt[s2], ("A", "x", s2), ht[s2], ("A", "h", s2), sq, ("A", "sq"), rt[s2], ("A", "r", s2), gi_pre, TT)
                b = self.bank()
                for half in range(2):
                    col = half * D + j * 128
                    for c in range(NCH):
                        S.op(PE, lambda e, b=b, half=half, col=col, c=c, h=h: e.matmul(
                            self.ps[b][:, half * TT:(half + 1) * TT], lhsT=win[:, c, col:col + 128], rhs=h[:, c, :],
                            start=(c == 0), stop=(c == NCH - 1)),
                            reads=[win_res[c], hres + (c,)], writes=[("ps", b)], inc=(half == 1 and c == NCH - 1))
                sg = st[j % 4]
                sres = ("A", "s", j % 4)
                S.op(ACT, lambda e, b=b, sg=sg, j=j: e.activation(out=sg[:], in_=self.ps[b][:, TT:2 * TT], func=AF.Sigmoid,
                                                                 bias=cp[:, 8 + j:9 + j]),
                     reads=[("ps", b), ("A", "cp")], writes=[sres])
                S.op(DVE, lambda e, b=b, sg=sg, j=j, u=u: e.scalar_tensor_tensor(
                    out=u[:, j, :], in0=self.ps[b][:, 0:TT], scalar=cp[:, j:j + 1], in1=sg[:], op0=ALU.add, op1=ALU.mult),
                    reads=[("ps", b), sres, ("A", "cp")], writes=[("A", "u", sl, j)])
                if dbg == 2:
                    S.op(DVE, lambda e: e.memset(zt[:, 0:1], 0.0), writes=[("A", "u", sl, j)])
                S.dma(SP, "us", lambda e, j=j, u=u, t0=t0: e.dma_start(
                    out=U[j * 128:(j + 1) * 128, 16 + t0:16 + t0 + TT], in_=u[:, j, :]),
                    reads=[("A", "u", sl, j)] + ([("A", "s", (j + 1) % 4)] if "x" in sk else []), writes=[("U", ti, j)])
                if dbg == 11:
                    dt_ = st[(j + 2) % 4]
                    S.op(POOL, lambda e, j=j, u=u, dt_=dt_: e.tensor_copy(out=dt_[:], in_=u[:, j, :]), reads=[("A", "u", sl, j)], writes=[("A", "s", (j + 2) % 4)])
                    S.dma(SP, ("xs", j), lambda e, j=j, dt_=dt_, t0=t0: e.dma_start(out=x_dst[j * 128:(j + 1) * 128, t0:t0 + TT], in_=dt_[:]),
                          reads=[("A", "s", (j + 2) % 4)], writes=[("X", t0, j)])
    def _conv_p2(self, x_src, x_dst, w_out, cvp, gi_post, U, dbg):
        S, T = self.S, self.T
        TT = 256
        KW, PAD = 31, 15
        nt = T // TT
        import os
        sk = os.environ.get("SKIP", "")
        self.new_phase()
        A = self.A
        uh = [A.tile([128, NCH, TT + 32], BF16, f"uh{i}") for i in range(2)]
        dbt = A.tile([128, NCH, TT], F32, "dbt") if "L" in sk else None
        cp = A.tile([128, 296], F32, "cp")
        S.dma(SP, "cpl", lambda e: e.dma_start(out=cp[:], in_=cvp[:, :]), writes=[("A", "cp")])
        wout = self.W.tile([128, NCH, D], BF16, "wout")
        import os
        sk = os.environ.get("SKIP", "")
        if "w" not in sk:
            self.load_w(wout, w_out, ("W", "wout"), "wl")
        dg = self.W.tile([128, KW * NCH, 128], BF16, "dg")
        for k in range(KW * NCH if "d" not in sk else 0):
            S.op(DVE, lambda e, k=k: e.tensor_scalar(out=dg[:, k, :], in0=self.ident[:], scalar1=cp[:, 48 + k:49 + k],
                                                     scalar2=None, op0=ALU.mult),
                 reads=["c_ident", ("A", "cp")], writes=[("W", "dg", k)])
        epsl = A.tile([128, 1], F32, "epsl")
        S.op(DVE, lambda e: e.memset(epsl[:], float(LN_EPS)), writes=[("A", "epsl")])
        xt = [A.tile([128, NCH, TT], F32, f"x{i}") for i in range(2)]
        vt = A.tile([128, NCH, TT], F32, "v")
        vb = A.tile([128, NCH, TT], BF16, "vb")
        sq = A.tile([128, NCH, TT], BF16, "sq")
        mt = A.tile([128, TT], F32, "m")
        msq = A.tile([128, TT], F32, "msq")
        var = A.tile([128, TT], F32, "var")
        rs = A.tile([128, TT], F32, "rs")
        tmp = [A.tile([128, TT], F32, f"t{i}") for i in range(2)]
        zb = A.tile([128, NCH, TT], BF16, "zb")
        ft = A.tile([128, NCH, TT], F32, "f")
        r2 = A.tile([128, TT], F32, "r2")
        sq2 = A.tile([128, NCH, TT], BF16, "sq2")
        wout_res = [("W", "wout", c) for c in range(NCH)]
        for ti in range(nt):
            t0 = ti * TT
            sl = ti % 2
            x, xres = xt[sl], ("A", "x", sl)
            u, ures = uh[sl], ("A", "uh", sl)

            def loads(tj):
                s2 = tj % 2
                tb = tj * TT
                self.load_x(x_src, tb, TT, xt[s2], ("A", "x", s2), ("xl", s2))
                deps = [("U", tk, c) for tk in (tj - 1, tj, tj + 1) if 0 <= tk < nt for c in range(NCH)]
                deps += [("U", "pad", sd, c) for sd in range(2) for c in range(NCH)]
                Uv = U.rearrange("(c p) t -> p c t", p=128)
                S.dma(SP, ("uhl", s2), lambda e, u2=uh[s2], tb=tb: e.dma_start(out=u2[:], in_=Uv[:, :, tb:tb + TT + 32]),
                      reads=deps, writes=[("A", "uh", s2)])
            if ti == 0:
                loads(0)
            if ti + 1 < nt:
                loads(ti + 1)
            if dbg == 2:
                for c in range(NCH):
                    fx = dbt if dbt is not None else ft
                    S.op(DVE, lambda e, c=c, u=u, fx=fx: e.tensor_copy(out=fx[:, c, :], in_=u[:, c, 16:16 + TT]), reads=[ures], writes=[("A", "f", c)])
                    S.dma(SP, ("xs", c), lambda e, c=c, fx=fx: e.dma_start(out=x_dst[c * 128:(c + 1) * 128, t0:t0 + TT], in_=fx[:, c, :]),
                          reads=[("A", "f", c)], writes=[("X", t0, c)])
                return
            for c in range(NCH):
                b = self.bank()
                for k in range(KW):
                    S.op(PE, lambda e, b=b, c=c, k=k, u=u: e.matmul(
                        self.ps[b][:, :TT], lhsT=dg[:, k * NCH + c, :], rhs=u[:, c, k + 1:k + 1 + TT],
                        start=(k == 0), stop=(k == KW - 1)),
                        reads=[("W", "dg", k * NCH + c), ures], writes=[("ps", b)], inc=(k == KW - 1))
                S.op(ACT, lambda e, b=b, c=c: e.activation(out=vt[:, c, :], in_=self.ps[b][:, :TT], func=AF.Identity,
                                                           bias=cp[:, 16 + c:17 + c]),
                     reads=[("ps", b), ("A", "cp")], writes=[("A", "v", c)])
                S.op(POOL, lambda e, c=c: e.tensor_copy(out=vb[:, c, :], in_=vt[:, c, :]),
                     reads=[("A", "v", c)], writes=[("A", "vb", c)])
                S.op(ACT, lambda e, c=c: e.activation(out=sq[:, c, :], in_=vt[:, c, :], func=AF.Square),
                     reads=[("A", "v", c)], writes=[("A", "sq", c)])
            if dbg == 3:
                for c in range(NCH):
                    S.dma(SP, ("xs", c), lambda e, c=c: e.dma_start(out=x_dst[c * 128:(c + 1) * 128, t0:t0 + TT], in_=vt[:, c, :]),
                          reads=[("A", "v", c)], writes=[("X", t0, c)])
                return
            b = self.bank()
            for half, (src, key) in enumerate(((vb, "vb"), (sq, "sq"))):
                for c in range(NCH):
                    S.op(PE, lambda e, b=b, half=half, src=src, c=c: e.matmul(
                        self.ps[b][:, half * TT:(half + 1) * TT], lhsT=self.ones[:], rhs=src[:, c, :],
                        start=(c == 0), stop=(c == NCH - 1)),
                        reads=["c_ones", ("A", key, c)], writes=[("ps", b)], inc=(half == 1 and c == NCH - 1))
            S.op(DVE, lambda e, b=b: e.tensor_scalar(out=mt[:], in0=self.ps[b][:, 0:TT], scalar1=1.0 / D, scalar2=None,
                                                     op0=ALU.mult), reads=[("ps", b)], writes=[("A", "m")])
            S.op(POOL, lambda e: e.tensor_tensor(out=msq[:], in0=mt[:], in1=mt[:], op=ALU.mult),
                 reads=[("A", "m")], writes=[("A", "msq")])
            S.op(DVE, lambda e, b=b: e.scalar_tensor_tensor(out=var[:], in0=self.ps[b][:, TT:2 * TT], scalar=1.0 / D, in1=msq[:],
                                                            op0=ALU.mult, op1=ALU.subtract),
                 reads=[("ps", b), ("A", "msq")], writes=[("A", "var")])
            S.op(ACT, lambda e: e.activation(out=rs[:], in_=var[:], func=AF.Sqrt, bias=epsl[:, 0:1]),
                 reads=[("A", "var"), ("A", "epsl")], writes=[("A", "rs")])
            S.op(DVE, lambda e: e.reciprocal(out=rs[:], in_=rs[:]), reads=[("A", "rs")], writes=[("A", "rs")])
            for c in range(NCH):
                tm = tmp[c % 2]
                tres = ("A", "t", c % 2)
                S.op(POOL, lambda e, c=c, tm=tm: e.tensor_tensor(out=tm[:], in0=vt[:, c, :], in1=mt[:], op=ALU.subtract),
                     reads=[("A", "v", c), ("A", "m")], writes=[tres])
                S.op(DVE, lambda e, tm=tm: e.tensor_tensor(out=tm[:], in0=tm[:], in1=rs[:], op=ALU.mult),
                     reads=[tres, ("A", "rs")], writes=[tres])
                S.op(ACT, lambda e, c=c, tm=tm: e.activation(out=zb[:, c, :], in_=tm[:], func=AF.Silu,
                                                             scale=cp[:, 24 + c:25 + c], bias=cp[:, 32 + c:33 + c]),
                     reads=[tres, ("A", "cp")], writes=[("A", "zb", c)])
            if dbg == 4:
                for c in range(NCH):
                    S.op(DVE, lambda e, c=c: e.tensor_copy(out=ft[:, c, :], in_=zb[:, c, :]), reads=[("A", "zb", c)], writes=[("A", "f", c)])
                    S.dma(SP, ("xs", c), lambda e, c=c: e.dma_start(out=x_dst[c * 128:(c + 1) * 128, t0:t0 + TT], in_=ft[:, c, :]),
                          reads=[("A", "f", c)], writes=[("X", t0, c)])
                return
            for co in range(NCH):
                b = self.bank()
                for c in range(NCH):
                    S.op(PE, lambda e, b=b, co=co, c=c: e.matmul(
                        self.ps[b][:, :TT], lhsT=wout[:, c, co * 128:(co + 1) * 128], rhs=zb[:, c, :],
                        start=(c == 0), stop=(c == NCH - 1)),
                        reads=[wout_res[c], ("A", "zb", c)], writes=[("ps", b)], inc=(c == NCH - 1))
                S.op(ACT, lambda e, b=b, co=co: e.activation(out=ft[:, co, :], in_=self.ps[b][:, :TT], func=AF.Identity,
                                                             bias=cp[:, 40 + co:41 + co]),
                     reads=[("ps", b), ("A", "cp")], writes=[("A", "f", co)])
            self.postnorm_store(ft, ("A", "f"), x, xres, sq2, ("A", "sq2"), r2, ("A", "r2"),
                                tmp, ("A", "t"), gi_post, x_dst, t0, TT, False)

    def proj_post(self, src, K, w, x_src, x_dst, gi_post, final=False):
        S, T = self.S, self.T
        TT = 256
        KC = K // 128
        self.new_phase()
        A = self.A
        wt = self.W.tile([128, KC, D], BF16, "pw")
        self.load_w(wt, w, ("W", "pw"), "wl")
        yt = [A.tile([128, 2, KC, 128], BF16, f"py{i}") for i in range(2)]
        xt = [A.tile([128, NCH, TT], F32, f"px{i}") for i in range(2)]
        ft = A.tile([128, NCH, TT], F32, "pf")
        sq = A.tile([128, NCH, TT], BF16, "psq")
        r2 = A.tile([128, TT], F32, "pr2")
        tmp = [A.tile([128, TT], F32, f"pt{i}") for i in range(2)]
        for ti in range(T // TT):
            t0 = ti * TT
            sl = ti % 2
            x, xres = xt[sl], ("A", "px", sl)
            y, yres = yt[sl], ("A", "py", sl)

            def loads(tj):
                s2 = tj % 2
                tb = tj * TT
                self.load_x(x_src, tb, TT, xt[s2], ("A", "px", s2), ("xl", s2))
                for ch in range(2):
                    S.dma(SP, ("pyl", s2, ch), lambda e, y2=yt[s2], tb=tb, ch=ch: e.dma_start(
                        out=y2[:, ch, :, :].rearrange("p k t -> p (k t)"), in_=src[tb // 128 + ch, :, :]),
                        reads=[("SRC", src.name, tb // 128 + ch)], writes=[("A", "py", s2, ch)])
            if ti == 0:
                loads(0)
            if ti + 1 < T // TT:
                loads(ti + 1)
            for co in range(NCH):
                b = self.bank()
                for kc in range(KC):
                    S.op(PE, lambda e, b=b, co=co, kc=kc, y=y: e.matmul(
                        self.ps[b][:, :TT].rearrange("p (c t) -> p c t", c=2), lhsT=wt[:, kc, co * 128:(co + 1) * 128], rhs=y[:, :, kc, :],
                        start=(kc == 0), stop=(kc == KC - 1)),
                        reads=[("W", "pw", kc), yres + (0,), yres + (1,)], writes=[("ps", b)], inc=(kc == KC - 1))
                S.op(ACT, lambda e, b=b, co=co: e.activation(out=ft[:, co, :], in_=self.ps[b][:, :TT], func=AF.Copy),
                     reads=[("ps", b)], writes=[("A", "pf", co)])
            self.postnorm_store(ft, ("A", "pf"), x, xres, sq, ("A", "psq"), r2, ("A", "pr2"),
                                tmp, ("A", "pt"), gi_post, x_dst, t0, TT, final)

    def ret_layer(self, x_src, x_dst, w_in, w_out, dec, rope, rcst, gi_pre, gi_post, li):
        T = self.T
        QT = self.dscr(f"r{li}_qT", [T // 128, 128, D], BF16)
        KT = self.dscr(f"r{li}_kT", [T // 128, 128, D], BF16)
        V = self.dscr(f"r{li}_v", [T, 2048], BF16)
        G = self.dscr(f"r{li}_g", [T, 2048], BF16)
        CB = self.dscr(f"r{li}_cb", [T, 2048], BF16)
        YT = self.dscr(f"r{li}_yT", [T // 128, 128, 2048], BF16)
        dbg = getattr(self, "ret_dbg", 9)
        self._ret_inproj(x_src, w_in, rope, gi_pre, QT, KT, V, G)
        if dbg < 1:
            return
        self._ret_scan(dec, rcst, QT, KT, V, G, CB, YT, backward=True)
        if dbg < 2:
            return
        self._ret_scan(dec, rcst, QT, KT, V, G, CB, YT, backward=False)
        if dbg < 3:
            return
        self.proj_post(YT, 2048, w_out, x_src, x_dst, gi_post)

    def _ret_inproj(self, x_src, w_in, rope, gi_pre, QT, KT, V, G):
        S, T = self.S, self.T
        TT = 256
        self.new_phase()
        A = self.A
        win = self.W.tile([128, NCH, 6144], BF16, "rwin")
        self.load_w(win, w_in, ("W", "rwin"), "wl")
        xt = [A.tile([128, NCH, TT], F32, f"x{i}") for i in range(2)]
        ht = [A.tile([128, NCH, TT], BF16, f"h{i}") for i in range(2)]
        sq = A.tile([128, NCH, TT], BF16, "sq")
        rt = [A.tile([128, TT], F32, f"r{i}") for i in range(2)]
        rp = [A.tile([128, 4, TT], F32, f"rp{i}") for i in range(2)]
        t4 = [A.tile([128, TT], F32, f"t4{i}") for i in range(4)]
        ob = [A.tile([128, 2, 2, 128], BF16, f"ob{i}") for i in range(2)]
        vb = [A.tile([128, 512], BF16, f"vb{i}") for i in range(3)]
        win_res = [("W", "rwin", c) for c in range(NCH)]
        n_ob = 0
        n_vb = 0
        for ti in range(T // TT):
            t0 = ti * TT
            sl = ti % 2
            x, h, r, rpt = xt[sl], ht[sl], rt[sl], rp[sl]
            xres, hres, rres, rpres = ("A", "x", sl), ("A", "h", sl), ("A", "r", sl), ("A", "rp", sl)
            def loads(tj):
                s2 = tj % 2
                tb = tj * TT
                self.load_x(x_src, tb, TT, xt[s2], ("A", "x", s2), ("xl", s2))
                S.dma(SP, ("rpl", s2), lambda e, rp2=rp[s2], tb=tb: e.dma_start(out=rp2[:], in_=rope[:, :, tb:tb + TT]),
                      writes=[("A", "rp", s2)])
            if ti == 0:
                loads(0)
                self.prenorm(x, xres, h, hres, sq, ("A", "sq"), r, rres, gi_pre, TT)
            if ti + 1 < T // TT:
                loads(ti + 1)
            for qk in range(2):
                dst = QT if qk == 0 else KT
                for hd in range(4):
                    b = self.bank()
                    for half in range(2):
                        col = qk * 1024 + hd * 256 + half * 128
                        for c in range(NCH):
                            S.op(PE, lambda e, b=b, half=half, col=col, c=c, h=h: e.matmul(
                                self.ps[b][:, half * TT:(half + 1) * TT], lhsT=win[:, c, col:col + 128], rhs=h[:, c, :],
                                start=(c == 0), stop=(c == NCH - 1)),
                                reads=[win_res[c], hres + (c,)], writes=[("ps", b)], inc=(half == 1 and c == NCH - 1))
                    o = ob[n_ob % 2]
                    ores = ("A", "ob", n_ob % 2)
                    n_ob += 1
                    cs, sn = rpt[:, 2 * qk, :], rpt[:, 2 * qk + 1, :]
                    p1, p2 = self.ps[b][:, 0:TT], self.ps[b][:, TT:2 * TT]
                    for k4, (pa, tb) in enumerate(((p1, cs), (p2, sn), (p2, cs), (p1, sn))):
                        S.op(DVE, lambda e, k4=k4, pa=pa, tb=tb: e.tensor_tensor(out=t4[k4][:], in0=pa, in1=tb, op=ALU.mult),
                             reads=[("ps", b), rpres], writes=[("A", "t4", k4)])
                    S.op(POOL, lambda e, o=o: e.tensor_tensor(out=o[:, :, 0, :], in0=t4[0][:].rearrange("p (c t) -> p c t", c=2),
                                                              in1=t4[1][:].rearrange("p (c t) -> p c t", c=2), op=ALU.subtract),
                         reads=[("A", "t4", 0), ("A", "t4", 1)], writes=[ores])
                    S.op(POOL, lambda e, o=o: e.tensor_tensor(out=o[:, :, 1, :], in0=t4[2][:].rearrange("p (c t) -> p c t", c=2),
                                                              in1=t4[3][:].rearrange("p (c t) -> p c t", c=2), op=ALU.add),
                         reads=[("A", "t4", 2), ("A", "t4", 3)], writes=[ores])
                    for ch in range(2):
                        jc = t0 // 128 + ch
                        S.dma(SP, ("qks", ch), lambda e, o=o, ch=ch, jc=jc, dst=dst, hd=hd: e.dma_start(
                            out=dst[jc, :, hd * 256:(hd + 1) * 256], in_=o[:, ch, :, :].rearrange("p a t -> p (a t)")),
                            reads=[ores], writes=[("SRC", dst.name, jc, hd)])
            if ti + 1 < T // TT:
                s2 = 1 - sl
                self.prenorm(xt[s2], ("A", "x", s2), ht[s2], ("A", "h", s2), sq, ("A", "sq"), rt[s2], ("A", "r", s2), gi_pre, TT)
            for blk in range(TT // 128):
                for vg in range(2):
                    dst = V if vg == 0 else G
                    for grp in range(4):
                        b = self.bank()
                        col = 2048 + vg * 2048 + grp * 512
                        for c in range(NCH):
                            S.op(PE, lambda e, b=b, col=col, c=c, h=h, blk=blk: e.matmul(
                                self.ps[b][:, :], lhsT=h[:, c, blk * 128:(blk + 1) * 128], rhs=win[:, c, col:col + 512],
                                start=(c == 0), stop=(c == NCH - 1)),
                                reads=[win_res[c], hres + (c,)], writes=[("ps", b)], inc=(c == NCH - 1))
                        v = vb[n_vb % 3]
                        vres = ("A", "vb", n_vb % 3)
                        n_vb += 1
                        S.op(ACT, lambda e, b=b, v=v, vg=vg: e.activation(out=v[:], in_=self.ps[b][:, :],
                                                                         func=(AF.Copy if vg == 0 else AF.Silu)),
                             reads=[("ps", b)], writes=[vres])
                        S.dma(SP, ("vgs", n_vb % 3), lambda e, v=v, dst=dst, grp=grp, t0=t0, blk=blk: e.dma_start(
                            out=dst[t0 + blk * 128:t0 + (blk + 1) * 128, grp * 512:(grp + 1) * 512], in_=v[:]),
                            reads=[vres], writes=[("SRC", dst.name, t0 // 128 + blk, grp)])

    def _ret_scan(self, dec, rcst, QT, KT, V, G, CB, YT, backward):
        S, T = self.S, self.T
        n = T // 128
        self.new_phase()
        A = self.A
        dct = A.tile([128, 8], F32, "dct")
        lg = A.tile([128, 8], F32, "lg")
        rc = A.tile([128, 772], F32, "rc")
        S.dma(SP, "dcl", lambda e: e.dma_start(out=dct[:], in_=dec[:, :]), writes=[("A", "dct")])
        S.dma(SP, "rcl", lambda e: e.dma_start(out=rc[:], in_=rcst[:, :]), writes=[("A", "rc")])
        S.op(ACT, lambda e: e.activation(out=lg[:], in_=dct[:], func=AF.Exp), reads=[("A", "dct")], writes=[("A", "lg")])
        S.op(DVE, lambda e: e.tensor_scalar(out=lg[:], in0=lg[:], scalar1=-1.0, scalar2=1.0, op0=ALU.mult, op1=ALU.add),
             reads=[("A", "lg")], writes=[("A", "lg")])
        S.op(ACT, lambda e: e.activation(out=lg[:], in_=lg[:], func=AF.Ln), reads=[("A", "lg")], writes=[("A", "lg")])
        d0 = 4 if backward else 0
        xi8 = A.tile([128, 8, 128], F32, "xi8")
        zeta = A.tile([128, 4], F32, "zeta")
        cdk = A.tile([128, 4], F32, "cdk")
        eps1 = A.tile([128, 1], F32, "eps1")
        S.op(DVE, lambda e: e.memset(eps1[:], float(RMS_EPS)), writes=[("A", "eps1")])
        rowc = 640 if backward else 512
        colc = 769 if backward else 768
        for hd in range(4):
            sc = lg[:, d0 + hd:d0 + hd + 1]
            for half in range(2):
                S.op(ACT, lambda e, hd=hd, half=half, sc=sc: e.activation(out=xi8[:, hd * 2 + half, :], in_=rc[:, rowc:rowc + 128],
                                                                          func=AF.Exp, scale=sc),
                     reads=[("A", "rc"), ("A", "lg")], writes=[("A", "xi8")])
            S.op(ACT, lambda e, hd=hd, sc=sc: e.activation(out=zeta[:, hd:hd + 1], in_=rc[:, colc:colc + 1], func=AF.Exp, scale=sc),
                 reads=[("A", "rc"), ("A", "lg")], writes=[("A", "zeta")])
            S.op(ACT, lambda e, hd=hd, sc=sc: e.activation(out=cdk[:, hd:hd + 1], in_=rc[:, 770:771], func=AF.Exp, scale=sc),
                 reads=[("A", "rc"), ("A", "lg")], writes=[("A", "cdk")])
        DT = A.tile([128, 4, 128], F32, "DT")
        dtmp = A.tile([128, 128], F32, "dtmp")
        if not backward:
            for hd in range(4):
                S.op(ACT, lambda e, hd=hd: e.activation(out=DT[:, hd, :], in_=rc[:, 0:128], func=AF.Exp, scale=lg[:, hd:hd + 1]),
                     reads=[("A", "rc"), ("A", "lg")], writes=[("A", "DT", hd)])
                S.op(DVE, lambda e, hd=hd: e.tensor_tensor(out=DT[:, hd, :], in0=DT[:, hd, :], in1=rc[:, 128:256], op=ALU.mult),
                     reads=[("A", "DT", hd), ("A", "rc")], writes=[("A", "DT", hd)])
                S.op(ACT, lambda e, hd=hd: e.activation(out=dtmp[:], in_=rc[:, 256:384], func=AF.Exp, scale=lg[:, 4 + hd:5 + hd]),
                     reads=[("A", "rc"), ("A", "lg")], writes=[("A", "dtmp")])
                S.op(DVE, lambda e: e.tensor_tensor(out=dtmp[:], in0=dtmp[:], in1=rc[:, 384:512], op=ALU.mult),
                     reads=[("A", "dtmp"), ("A", "rc")], writes=[("A", "dtmp")])
                S.op(DVE, lambda e, hd=hd: e.tensor_tensor(out=DT[:, hd, :], in0=DT[:, hd, :], in1=dtmp[:], op=ALU.add),
                     reads=[("A", "DT", hd), ("A", "dtmp")], writes=[("A", "DT", hd)])
        DTres = [("A", "DT", hd) for hd in range(4)]
        Sf = A.tile([128, 4, 2, 512], F32, "Sf")
        Sbs = [A.tile([128, 4, 2, 512], BF16, f"Sb{i}") for i in range(2)]
        for hd in range(4):
            S.op(POOL, lambda e, hd=hd: e.memset(Sf[:, hd, :, :], 0.0), writes=[("A", "Sf", hd, 0), ("A", "Sf", hd, 1)])
            S.op(POOL, lambda e, hd=hd: e.memset(Sbs[0][:, hd, :, :], 0.0), writes=[("A", "Sb", 0, hd, 0), ("A", "Sb", 0, hd, 1)])
        qt = [A.tile([128, 8, 128], BF16, f"qt{i}") for i in range(2)]
        kt = [A.tile([128, 8, 128], BF16, f"kt{i}") for i in range(2)]
        vt = [A.tile([128, 2048], BF16, f"vt{i}") for i in range(2)]
        qx = A.tile([128, 8, 128], BF16, "qx")
        kz = A.tile([128, 4, 256], BF16, "kz")
        if backward:
            cbo = [A.tile([128, 512], BF16, f"cbo{i}") for i in range(2)]
        else:
            gt = [A.tile([128, 2048], BF16, f"gt{i}") for i in range(2)]
            cbt = [A.tile([128, 2048], BF16, f"cbt{i}") for i in range(2)]
            Pm = A.tile([128, 512], BF16, "Pm")
            ot = A.tile([128, 4, 512], F32, "ot")
            junk = A.tile([128, 512], BF16, "junk")
            ss = A.tile([128, 4], F32, "ss")
            ybs = [A.tile([128, 2048], BF16, f"yb{i}") for i in range(2)]
            yTs = [A.tile([128, 16, 128], BF16, f"yT{i}") for i in range(2)]
        order = list(range(n - 1, -1, -1)) if backward else list(range(n))
        ncb = 0
        pend_y = None
        for it, jc in enumerate(order):
            sl = it % 2
            c0 = jc * 128
            if not backward:
                yb = ybs[sl]
                yT = yTs[sl]
            q, k, v = qt[sl], kt[sl], vt[sl]
            qres, kres, vres = ("A", "qt", sl), ("A", "kt", sl), ("A", "vt", sl)
            def loads(it2):
                s2 = it2 % 2
                j2 = order[it2]
                cb0 = j2 * 128
                S.dma(SP, ("ql", s2), lambda e, q2=qt[s2], j2=j2: e.dma_start(out=q2[:].rearrange("p c t -> p (c t)"), in_=QT[j2, :, :]),
                      reads=[("SRC", QT.name, j2, h4) for h4 in range(4)], writes=[("A", "qt", s2)])
                S.dma(SP, ("kl", s2), lambda e, k2=kt[s2], j2=j2: e.dma_start(out=k2[:].rearrange("p c t -> p (c t)"), in_=KT[j2, :, :]),
                      reads=[("SRC", KT.name, j2, h4) for h4 in range(4)], writes=[("A", "kt", s2)])
                S.dma(SP, ("vl", s2), lambda e, v2=vt[s2], cb0=cb0: e.dma_start(out=v2[:], in_=V[cb0:cb0 + 128, :]),
                      reads=[("SRC", V.name, j2, g4) for g4 in range(4)], writes=[("A", "vt", s2)])
                if not backward:
                    S.dma(SP, ("gl", s2), lambda e, g2=gt[s2], cb0=cb0: e.dma_start(out=g2[:], in_=G[cb0:cb0 + 128, :]),
                          reads=[("SRC", G.name, j2, g4) for g4 in range(4)], writes=[("A", "gt", s2)])
                    S.dma(SP, ("cbl", s2), lambda e, c2=cbt[s2], cb0=cb0: e.dma_start(out=c2[:], in_=CB[cb0:cb0 + 128, :]),
                          reads=[("SRC", CB.name, j2, g4) for g4 in range(4)], writes=[("A", "cbt", s2)])
            if it == 0:
                loads(0)
            if it + 1 < n:
                loads(it + 1)
            if not backward:
                g, cb = gt[sl], cbt[sl]
                gres, cbres = ("A", "gt", sl), ("A", "cbt", sl)
            S.op(POOL, lambda e, q=q: e.tensor_tensor(out=qx[:], in0=q[:], in1=xi8[:], op=ALU.mult),
                 reads=[qres, ("A", "xi8")], writes=[("A", "qx")])
            bt = self.bank()
            pbt = self.psb[bt]
            for c in range(8):
                S.op(PE, lambda e, c=c, k=k, pbt=pbt: e.transpose(out=pbt[:, c * 128:(c + 1) * 128], in_=k[:, c, :], identity=self.ident[:]),
                     reads=[kres, "c_ident"], writes=[("ps", bt)], inc=(c == 7))
            for hd in range(4):
                S.op(ACT, lambda e, hd=hd, pbt=pbt: e.activation(out=kz[:, hd, :], in_=pbt[:, hd * 256:(hd + 1) * 256], func=AF.Copy,
                                                                 scale=zeta[:, hd:hd + 1]),
                     reads=[("ps", bt), ("A", "zeta")], writes=[("A", "kz", hd)])
            if not backward:
                bs = self.bank()
                for hd in range(4):
                    for half in range(2):
                        S.op(PE, lambda e, hd=hd, half=half, k=k, q=q, bs=bs: e.matmul(
                            self.ps[bs][:, hd * 128:(hd + 1) * 128], lhsT=k[:, hd * 2 + half, :], rhs=q[:, hd * 2 + half, :],
                            start=(half == 0), stop=(half == 1)),
                            reads=[kres, qres], writes=[("ps", bs)], inc=(hd == 3 and half == 1))
                S.op(DVE, lambda e, bs=bs: e.tensor_tensor(out=Pm[:], in0=self.ps[bs][:], in1=DT[:].rearrange("p h i -> p (h i)"),
                                                           op=ALU.mult),
                     reads=[("ps", bs)] + DTres, writes=[("A", "Pm")])
            Sc, Sn = Sbs[it % 2], Sbs[(it + 1) % 2]
            for hd in range(4):
                for half in range(2):
                    b = self.bank()
                    S.op(PE, lambda e, hd=hd, half=half, b=b, v=v: e.matmul(
                        self.ps[b][:], lhsT=kz[:, hd, half * 128:(half + 1) * 128], rhs=v[:, hd * 512:(hd + 1) * 512],
                        start=True, stop=True),
                        reads=[("A", "kz", hd), vres], writes=[("ps", b)])
                    S.op(DVE, lambda e, hd=hd, half=half, b=b: e.scalar_tensor_tensor(
                        out=Sf[:, hd, half, :], in0=Sf[:, hd, half, :], scalar=cdk[:, hd:hd + 1], in1=self.ps[b][:],
                        op0=ALU.mult, op1=ALU.add),
                        reads=[("A", "Sf", hd, half), ("A", "cdk"), ("ps", b)], writes=[("A", "Sf", hd, half)])
                    ceng = ACT if (hd * 2 + half) % 2 == 0 else POOL
                    if ceng == ACT:
                        S.op(ACT, lambda e, hd=hd, half=half, Sn=Sn: e.activation(out=Sn[:, hd, half, :], in_=Sf[:, hd, half, :], func=AF.Copy),
                             reads=[("A", "Sf", hd, half)], writes=[("A", "Sb", (it + 1) % 2, hd, half)])
                    else:
                        S.op(POOL, lambda e, hd=hd, half=half, Sn=Sn: e.tensor_copy(out=Sn[:, hd, half, :], in_=Sf[:, hd, half, :]),
                             reads=[("A", "Sf", hd, half)], writes=[("A", "Sb", (it + 1) % 2, hd, half)])
            for hd in range(4):
                bo = self.bank()
                if not backward:
                    S.op(PE, lambda e, hd=hd, bo=bo, v=v: e.matmul(
                        self.ps[bo][:], lhsT=Pm[:, hd * 128:(hd + 1) * 128], rhs=v[:, hd * 512:(hd + 1) * 512], start=True, stop=False),
                        reads=[("A", "Pm"), vres], writes=[("ps", bo)], inc=False)
                for half in range(2):
                    S.op(PE, lambda e, hd=hd, half=half, bo=bo, Sc=Sc: e.matmul(
                        self.ps[bo][:], lhsT=qx[:, hd * 2 + half, :], rhs=Sc[:, hd, half, :],
                        start=(backward and half == 0), stop=(half == 1)),
                        reads=[("A", "qx"), ("A", "Sb", it % 2, hd, half)], writes=[("ps", bo)], inc=(half == 1))
                if backward:
                    co = cbo[ncb % 2]
                    cores = ("A", "cbo", ncb % 2)
                    ncb += 1
                    S.op(ACT, lambda e, bo=bo, co=co: e.activation(out=co[:], in_=self.ps[bo][:], func=AF.Copy),
                         reads=[("ps", bo)], writes=[cores])
                    S.dma(SP, ("cbs", ncb % 2), lambda e, co=co, c0=c0, hd=hd: e.dma_start(
                        out=CB[c0:c0 + 128, hd * 512:(hd + 1) * 512], in_=co[:]),
                        reads=[cores], writes=[("SRC", CB.name, jc, hd)])
                else:
                    S.op(DVE, lambda e, bo=bo, hd=hd, cb=cb: e.tensor_tensor(out=ot[:, hd, :], in0=self.ps[bo][:],
                                                                             in1=cb[:, hd * 512:(hd + 1) * 512], op=ALU.add),
                         reads=[("ps", bo), cbres], writes=[("A", "ot", hd)])
                    S.op(ACT, lambda e, hd=hd: e.activation(out=junk[:], in_=ot[:, hd, :], func=AF.Square, accum_out=ss[:, hd:hd + 1]),
                         reads=[("A", "ot", hd)], writes=[("A", "junk"), ("A", "ss", hd)])
                    S.op(ACT, lambda e, hd=hd: e.activation(out=ss[:, hd:hd + 1], in_=ss[:, hd:hd + 1], func=AF.Sqrt,
                                                            scale=1.0 / 512.0, bias=eps1[:, 0:1]),
                         reads=[("A", "ss", hd), ("A", "eps1")], writes=[("A", "ss", hd)])
                    S.op(DVE, lambda e, hd=hd: e.reciprocal(out=ss[:, hd:hd + 1], in_=ss[:, hd:hd + 1]),
                         reads=[("A", "ss", hd)], writes=[("A", "ss", hd)])
                    S.op(DVE, lambda e, hd=hd, g=g, yb=yb: e.scalar_tensor_tensor(
                        out=yb[:, hd * 512:(hd + 1) * 512], in0=ot[:, hd, :], scalar=ss[:, hd:hd + 1],
                        in1=g[:, hd * 512:(hd + 1) * 512], op0=ALU.mult, op1=ALU.mult),
                        reads=[("A", "ot", hd), ("A", "ss", hd), gres], writes=[("A", "yb", sl, hd)])
            if not backward:
                def emit_y(yb=yb, yT=yT, sl=sl, c0=c0, jc=jc):
                    for half8 in range(2):
                        by = self.bank()
                        pby = self.psb[by]
                        for bb in range(8):
                            blk = half8 * 8 + bb
                            S.op(PE, lambda e, blk=blk, bb=bb, pby=pby: e.transpose(
                                out=pby[:, bb * 128:(bb + 1) * 128], in_=yb[:, blk * 128:(blk + 1) * 128], identity=self.ident[:]),
                                reads=[("A", "yb", sl, blk // 4), "c_ident"], writes=[("ps", by)], inc=(bb == 7))
                        S.op(ACT, lambda e, half8=half8, pby=pby: e.activation(
                            out=yT[:, half8 * 8:(half8 + 1) * 8, :].rearrange("p a b -> p (a b)"), in_=pby[:, :], func=AF.Copy),
                            reads=[("ps", by)], writes=[("A", "yT", sl, half8)])
                    S.dma(SP, ("yts", sl), lambda e: e.dma_start(out=YT[jc, :, :], in_=yT[:].rearrange("p b t -> p (b t)")),
                          reads=[("A", "yT", sl, 0), ("A", "yT", sl, 1)], writes=[("SRC", YT.name, jc)])
            if not backward:
                if pend_y is not None:
                    pend_y()
                pend_y = emit_y
        if pend_y is not None:
            pend_y()

    DILS = (1, 4, 16)

    def attn_layer(self, x_src, x_dst, w_in, w_out, ropeA, acst, gi_pre, gi_post, dbg=9):
        T = self.T
        HD = [self.dscr(f"a_h{g}", [D, T], BF16) for g in range(3)]
        QT = [self.dscr(f"a_q{g}", [D, T], BF16) for g in range(3)]
        KT = [self.dscr(f"a_k{g}", [D, T], BF16) for g in range(3)]
        VE = [self.dscr(f"a_v{g}", [T + 128 * self.DILS[g], D], BF16) for g in range(3)]
        ND = [self.dscr(f"a_n{g}", [1, D, T], F32) for g in range(3)]
        DEN = [self.dscr(f"a_d{g}", [16, T], F32) for g in range(3)]
        self._attn_prep(x_src, gi_pre, HD)
        if dbg < 1:
            return
        for g in range(3):
            self._attn_inproj(g, HD[g], w_in, ropeA, acst, QT[g], KT[g], VE[g])
        if dbg < 2:
            return
        for g in range(3):
            self._attn_core(g, acst, QT[g], KT[g], VE[g], ND[g], DEN[g])
        if dbg < 3:
            return
        self._attn_out(ND, DEN, acst, w_out, x_src, x_dst, gi_post)

    def _attn_prep(self, x_src, gi_pre, HD):
        S, T = self.S, self.T
        TT = 256
        self.new_phase()
        A = self.A
        Hn = A.tile([128, NCH, T], BF16, "Hn")
        Hp = A.tile([128, NCH, T], BF16, "Hp")
        xt = [A.tile([128, NCH, TT], F32, f"x{i}") for i in range(2)]
        sq = A.tile([128, NCH, TT], BF16, "sq")
        rt = [A.tile([128, TT], F32, f"r{i}") for i in range(2)]
        for ti in range(T // TT):
            t0 = ti * TT
            sl = ti % 2
            x, r = xt[sl], rt[sl]
            xres, rres = ("A", "x", sl), ("A", "r", sl)
            if ti == 0:
                self.load_x(x_src, 0, TT, xt[0], ("A", "x", 0), ("xl", 0))
            if ti + 1 < T // TT:
                self.load_x(x_src, t0 + TT, TT, xt[1 - sl], ("A", "x", 1 - sl), ("xl", 1 - sl))
            self.rstd_of(x, [xres], sq, ("A", "sq"), r, rres, TT)
            for c in range(NCH):
                gsc = self.gs[:, gi_pre * NCH + c: gi_pre * NCH + c + 1]
                S.op(DVE, lambda e, c=c, gsc=gsc, x=x, r=r, t0=t0: e.scalar_tensor_tensor(
                    out=Hn[:, c, t0:t0 + TT], in0=x[:, c, :], scalar=gsc, in1=r[:], op0=ALU.mult, op1=ALU.mult),
                    reads=[xres, rres, "c_gs"], writes=[("A", "Hn", c)])
        for c in range(NCH):
            S.dma(SP, ("hs", c % 4), lambda e, c=c: e.dma_start(out=HD[0][c * 128:(c + 1) * 128, :], in_=Hn[:, c, :]),
                  reads=[("A", "Hn", c)], writes=[("SRC", HD[0].name, c)])
        for gi, dil in ((1, 4), (2, 16)):
            hp = Hp
            for c in range(NCH):
                eng = (POOL, DVE, ACT)[c % 3]
                if eng == ACT:
                    S.op(ACT, lambda e, c=c, hp=hp, dil=dil: e.activation(
                        out=hp[:, c, :].rearrange("p (r l) -> p r l", r=dil),
                        in_=Hn[:, c, :].rearrange("p (l r) -> p r l", r=dil), func=AF.Copy),
                        reads=[("A", "Hn", c)], writes=[("A", "Hp", c)])
                else:
                    S.op(eng, lambda e, c=c, hp=hp, dil=dil: e.tensor_copy(
                        out=hp[:, c, :].rearrange("p (r l) -> p r l", r=dil),
                        in_=Hn[:, c, :].rearrange("p (l r) -> p r l", r=dil)),
                        reads=[("A", "Hn", c)], writes=[("A", "Hp", c)])
                S.dma(SP, ("hs", c % 4), lambda e, c=c, hp=hp, gi=gi: e.dma_start(out=HD[gi][c * 128:(c + 1) * 128, :], in_=hp[:, c, :]),
                      reads=[("A", "Hp", c)], writes=[("SRC", HD[gi].name, c)])

    def _attn_inproj(self, g, Hg, w_in, ropeA, acst, QTg, KTg, VEg):
        S, T = self.S, self.T
        TT = 256
        dil = self.DILS[g]
        L = T // dil
        Lp = L + 128
        self.new_phase()
        A = self.A
        win = self.W.tile([128, NCH, 3072], BF16, "awin")
        self.load_w(win, w_in, ("W", "awin"), "wl", cols=(g * 3072, (g + 1) * 3072))
        win_res = [("W", "awin", c) for c in range(NCH)]
        pmf = A.tile([128, 128], F32, "pmf")
        pm = A.tile([128, 128], BF16, "pm")
        S.dma(SP, "pml", lambda e: e.dma_start(out=pmf[:], in_=acst[:, 0:128]), writes=[("A", "pmf")])
        S.op(DVE, lambda e: e.tensor_copy(out=pm[:], in_=pmf[:]), reads=[("A", "pmf")], writes=[("A", "pm")])
        zt = A.tile([64, D], BF16, "zt")
        S.op(POOL, lambda e: e.memset(zt[:], 0.0), writes=[("A", "zt")])
        for r in range(dil):
            for side in range(2):
                row = r * Lp + (0 if side == 0 else 64 + L)
                S.dma(SP, ("vz", side), lambda e, row=row: e.dma_start(out=VEg[row:row + 64, :], in_=zt[:]),
                      reads=[("A", "zt")], writes=[("SRC", VEg.name, "pad", r, side)])
        stop = getattr(self, "att_stop", 0)
        if stop == 1:
            return
        ht = [A.tile([128, NCH, TT], BF16, f"h{i}") for i in range(2)]
        rp = [A.tile([128, 2, TT], F32, f"rp{i}") for i in range(2)]
        raw = [A.tile([128, TT], BF16, f"raw{i}") for i in range(2)]
        t1 = [A.tile([128, TT], F32, f"t1{i}") for i in range(2)]
        t2 = [A.tile([128, TT], F32, f"t2{i}") for i in range(2)]
        ob = [A.tile([128, TT], BF16, f"ob{i}") for i in range(3)]
        vb = [A.tile([128, 512], BF16, f"vb{i}") for i in range(3)]
        Hv = Hg.rearrange("(c p) t -> p c t", p=128)
        n_o = 0
        n_v = 0
        for ti in range(T // TT):
            t0 = ti * TT
            sl = ti % 2
            h, rpt = ht[sl], rp[sl]
            hres, rpres = ("A", "h", sl), ("A", "rp", sl)
            def loads(tj):
                s2 = tj % 2
                tb = tj * TT
                S.dma(SP, ("hl", s2), lambda e, h2=ht[s2], tb=tb: e.dma_start(out=h2[:], in_=Hv[:, :, tb:tb + TT]),
                      reads=[("SRC", Hg.name, c) for c in range(NCH)], writes=[("A", "h", s2)])
                S.dma(SP, ("rpl", s2), lambda e, rp2=rp[s2], tb=tb: e.dma_start(out=rp2[:], in_=ropeA[g, :, :, tb:tb + TT]),
                      writes=[("A", "rp", s2)])
            if ti == 0:
                loads(0)
            if ti + 1 < T // TT:
                loads(ti + 1)
            if stop == 2:
                return
            pend_ep = None
            for qk in range(2):
                dst = QTg if qk == 0 else KTg
                for cc in range(NCH):
                    ba = self.bank()
                    col = qk * 1024 + cc * 128
                    for c in range(NCH):
                        S.op(PE, lambda e, ba=ba, col=col, c=c, h=h: e.matmul(
                            self.ps[ba][:, :TT], lhsT=win[:, c, col:col + 128], rhs=h[:, c, :],
                            start=(c == 0), stop=(c == NCH - 1)),
                            reads=[win_res[c], hres], writes=[("ps", ba)], inc=(c == NCH - 1))
                    k2 = n_o % 2
                    rw = raw[k2]
                    S.op(ACT, lambda e, ba=ba, rw=rw: e.activation(out=rw[:], in_=self.ps[ba][:, :TT], func=AF.Copy),
                         reads=[("ps", ba)], writes=[("A", "raw", k2)])
                    if pend_ep is not None:
                        pend_ep()

                    def epilogue(ba=ba, k2=k2, rw=rw, n_o=n_o, cc=cc, dst=dst, rpt=rpt, rpres=rpres, t0=t0, ti=ti):
                        a1, a2 = t1[k2], t2[k2]
                        o = ob[n_o % 3]
                        ores = ("A", "ob", n_o % 3)
                        bb = self.bank()
                        S.op(PE, lambda e: e.matmul(self.ps[bb][:, :TT], lhsT=pm[:], rhs=rw[:], start=True, stop=True),
                             reads=[("A", "pm"), ("A", "raw", k2)], writes=[("ps", bb)])
                        S.op(DVE, lambda e: e.tensor_tensor(out=a1[:], in0=self.ps[ba][:, :TT], in1=rpt[:, 0, :], op=ALU.mult),
                             reads=[("ps", ba), rpres, ("A", "raw", k2)], writes=[("A", "t1", k2)])
                        S.op(DVE, lambda e: e.tensor_tensor(out=a2[:], in0=self.ps[bb][:, :TT], in1=rpt[:, 1, :], op=ALU.mult),
                             reads=[("ps", bb), rpres], writes=[("A", "t2", k2)])
                        S.op(POOL, lambda e: e.tensor_tensor(out=o[:], in0=a1[:], in1=a2[:], op=ALU.add),
                             reads=[("A", "t1", k2), ("A", "t2", k2)], writes=[ores])
                        S.dma(SP, ("qks", n_o % 3), lambda e: e.dma_start(
                            out=dst[cc * 128:(cc + 1) * 128, t0:t0 + TT], in_=o[:]),
                            reads=[ores], writes=[("SRC", dst.name, cc, ti)])
                    pend_ep = epilogue
                    n_o += 1
            pend_ep()
            if stop == 3:
                return
            for blk in range(TT // 128):
                pos = t0 + blk * 128
                r, l0 = pos // L, pos % L
                row = r * Lp + 64 + l0
                for grp in range(2):
                    b = self.bank()
                    col = 2048 + grp * 512
                    for c in range(NCH):
                        S.op(PE, lambda e, b=b, col=col, c=c, h=h, blk=blk: e.matmul(
                            self.ps[b][:, :], lhsT=h[:, c, blk * 128:(blk + 1) * 128], rhs=win[:, c, col:col + 512],
                            start=(c == 0), stop=(c == NCH - 1)),
                            reads=[win_res[c], hres], writes=[("ps", b)], inc=(c == NCH - 1))
                    v = vb[n_v % 3]
                    vres = ("A", "vb", n_v % 3)
                    n_v += 1
                    S.op(ACT, lambda e, b=b, v=v: e.activation(out=v[:], in_=self.ps[b][:, :], func=AF.Copy),
                         reads=[("ps", b)], writes=[vres])
                    S.dma(SP, ("vgs", n_v % 3), lambda e, v=v, row=row, grp=grp: e.dma_start(
                        out=VEg[row:row + 128, grp * 512:(grp + 1) * 512], in_=v[:]),
                        reads=[vres], writes=[("SRC", VEg.name, pos // 128, grp)])

    def _attn_core(self, g, acst, QTg, KTg, VEg, NDg, DENg):
        S, T = self.S, self.T
        dil = self.DILS[g]
        L = T // dil
        Lp = L + 128
        nqb = L // 128
        nkb = nqb + 1
        self.new_phase()
        A = self.A
        mbf = A.tile([128, 1024], F32, "mbf")
        m01 = A.tile([128, 4, 2, 2, 128], BF16, "m01")
        S.dma(SP, "mbl", lambda e: e.dma_start(out=mbf[:], in_=acst[:, 128:1152]), writes=[("A", "mbf")])
        for hh in range(2):
            S.op(DVE, lambda e, hh=hh: e.tensor_scalar(
                out=m01[:, :, :, hh, :], in0=mbf[:].rearrange("p (v k q) -> p v k q", v=4, k=2), scalar1=-1.0, scalar2=None,
                op0=ALU.is_ge), reads=[("A", "mbf")], writes=[("A", "m01")])
        kt = [A.tile([128, dil * Lp], BF16, f"kt{i}") for i in range(2)]
        qt = [A.tile([128, 2, T], BF16, f"qt{i}") for i in range(2)]
        ve = [A.tile([128, dil * nkb, 128], BF16, f"ve{i}") for i in range(2)]
        for i in range(2):
            S.op(POOL, lambda e, i=i: e.memset(kt[i][:], 0.0), writes=[("A", "kt", i)])
            S.op(POOL, lambda e, i=i: e.memset(qt[i][:], 0.0), writes=[("A", "qt", i)])
        pmr = [A.tile([128, 512], BF16, f"pr{i}") for i in range(4)]
        pmt = [A.tile([128, 512], BF16, f"pm{i}") for i in range(4)]
        span = 128 * dil if dil > 1 else 512
        stg = [A.tile([64, 2, span], F32, f"stg{i}") for i in range(2)]
        sdn = [A.tile([1, 2, span], F32, f"sdn{i}") for i in range(2)]
        VEv = VEg.rearrange("(n p) f -> p n f", p=128)
        kdeps_all = [[("SRC", KTg.name, cc, ti) for ti in range(T // 256)] for cc in range(NCH)]
        qdeps_all = [[("SRC", QTg.name, cc, ti) for ti in range(T // 256)] for cc in range(NCH)]
        vdeps = [("SRC", VEg.name, pb, grp) for pb in range(T // 128) for grp in range(2)]
        vdeps += [("SRC", VEg.name, "pad", r, sd) for r in range(dil) for sd in range(2)]

        def loads(cc):
            sl = cc % 2
            k, q, v = kt[sl], qt[sl], ve[sl]
            kres, qres, vres = ("A", "kt", sl), ("A", "qt", sl), ("A", "ve", sl)
            S.dma(SP, ("kl", sl), lambda e, k=k, cc=cc: e.dma_start(
                out=k[:].rearrange("p (r l) -> p r l", r=dil)[:, :, 64:64 + L],
                in_=KTg[cc * 128:(cc + 1) * 128, :].rearrange("p (r l) -> p r l", r=dil)), reads=kdeps_all[cc], writes=[kres])
            for hh in range(2):
                r0 = cc * 128 + hh * 64
                S.dma(SP, ("ql", sl), lambda e, q=q, hh=hh, r0=r0: e.dma_start(out=q[hh * 64:(hh + 1) * 64, hh, :], in_=QTg[r0:r0 + 64, :]),
                      reads=qdeps_all[cc], writes=[qres])
            S.dma(SP, ("vl", sl), lambda e, v=v, cc=cc: e.dma_start(out=v[:], in_=VEv[:, :, cc * 128:(cc + 1) * 128]),
                  reads=vdeps, writes=[vres])

        nsp = T // span
        items = []
        for cc in range(NCH):
            for sp_i in range(nsp):
                if dil > 1:
                    blocks = [(r, sp_i) for r in range(dil)]
                else:
                    blocks = [(0, sp_i * 4 + b4) for b4 in range(4)]
                for bi, (r, i) in enumerate(blocks):
                    items.append((cc, sp_i, bi, r, i, bi == len(blocks) - 1))
        state = {"n_p": 0}

        def stage_a(item):
            cc, sp_i, bi, r, i, last = item
            sl = cc % 2
            k, q = kt[sl], qt[sl]
            kres, qres = ("A", "kt", sl), ("A", "qt", sl)
            var = 0
            if i == 0:
                var = 1
            if i == nqb - 1:
                var = 2 if var == 0 else 3
            bs = self.bank()
            qc = r * L + i * 128
            for kb in range(2):
                kc = r * Lp + (i + kb) * 128
                S.op(PE, lambda e, bs=bs, kb=kb, kc=kc, qc=qc, k=k, q=q: e.matmul(
                    self.ps[bs][:, kb * 256:(kb + 1) * 256].rearrange("p (h q) -> p h q", h=2),
                    lhsT=k[:, kc:kc + 128], rhs=q[:, :, qc:qc + 128], start=True, stop=True),
                    reads=[kres, qres], writes=[("ps", bs)], inc=(kb == 1))
            np_ = state["n_p"]
            state["n_p"] += 1
            pr = pmr[np_ % 4]
            prres = ("A", "pr", np_ % 4)
            pmx = pmt[np_ % 4]
            pres = ("A", "pm", np_ % 4)
            S.op(ACT, lambda e, bs=bs, pr=pr: e.activation(out=pr[:], in_=self.ps[bs][:], func=AF.Exp, scale=0.125),
                 reads=[("ps", bs)], writes=[prres])
            meng = POOL if np_ % 4 != 0 else DVE
            S.op(meng, lambda e, pr=pr, pmx=pmx, var=var: e.tensor_tensor(
                out=pmx[:], in0=pr[:], in1=m01[:, var, :, :, :].rearrange("p k h q -> p (k h q)"), op=ALU.mult),
                reads=[prres, ("A", "m01")], writes=[pres])
            return pmx, pres

        def stage_b(item, pmx, pres, st, stres):
            cc, sp_i, bi, r, i, last = item
            sl = cc % 2
            v = ve[sl]
            vres = ("A", "ve", sl)
            bo = self.bank()
            for hh in range(2):
                for kb in range(2):
                    kbi = r * nkb + i + kb
                    S.op(PE, lambda e, bo=bo, hh=hh, kb=kb, kbi=kbi, pmx=pmx, v=v: e.matmul(
                        self.ps[bo][0:64, hh * 128:(hh + 1) * 128], lhsT=v[:, kbi, hh * 64:(hh + 1) * 64],
                        rhs=pmx[:, kb * 256 + hh * 128:kb * 256 + (hh + 1) * 128],
                        start=(kb == 0), stop=(kb == 1)),
                        reads=[vres, pres], writes=[("ps", bo)], inc=False)
            for kb in range(2):
                S.op(PE, lambda e, bo=bo, kb=kb, pmx=pmx: e.matmul(
                    self.ps[bo][0:64, 256:512], lhsT=self.ones[:, 0:64], rhs=pmx[:, kb * 256:(kb + 1) * 256],
                    start=(kb == 0), stop=(kb == 1)),
                    reads=["c_ones", pres], writes=[("ps", bo)], inc=(kb == 1))
            sd = sdn[0] if st is stg[0] else sdn[1]
            if dil > 1:
                dstv = st[:].rearrange("p h (l r) -> p h l r", r=dil)[:, :, :, r]
                dstd = sd[:].rearrange("p h (l r) -> p h l r", r=dil)[:, :, :, r]
            else:
                dstv = st[:, :, bi * 128:(bi + 1) * 128]
                dstd = sd[:, :, bi * 128:(bi + 1) * 128]
            S.op(DVE, lambda e, bo=bo, dstv=dstv: e.tensor_copy(
                out=dstv, in_=self.ps[bo][0:64, 0:256].rearrange("p (h q) -> p h q", h=2)),
                reads=[("ps", bo)], writes=[stres])
            S.op(DVE, lambda e, bo=bo, dstd=dstd: e.tensor_copy(
                out=dstd, in_=self.ps[bo][0:1, 256:512].rearrange("p (h q) -> p h q", h=2)),
                reads=[("ps", bo)], writes=[stres + ("d",)])
            if last:
                for hh in range(2):
                    r0 = cc * 128 + hh * 64
                    S.dma(SP, ("nds", hh), lambda e, st=st, hh=hh, r0=r0, sp_i=sp_i: e.dma_start(
                        out=NDg[0, r0:r0 + 64, sp_i * span:(sp_i + 1) * span], in_=st[:, hh, :]),
                        reads=[stres], writes=[("SRC", NDg.name, cc, sp_i, hh)])
                    S.dma(SP, ("dns", hh), lambda e, sd=sd, hh=hh, sp_i=sp_i: e.dma_start(
                        out=DENg[cc * 2 + hh:cc * 2 + hh + 1, sp_i * span:(sp_i + 1) * span], in_=sd[0:1, hh, :]),
                        reads=[stres + ("d",)], writes=[("SRC", DENg.name, cc, sp_i, hh)])

        loads(0)
        queue = []
        n_s = 0
        k_in_chunk = 0
        PDEPTH = 3
        for idx, item in enumerate(items):
            cc, sp_i, bi, r, i, last = item
            if bi == 0 and sp_i == 0:
                k_in_chunk = 0
            a = stage_a(item)
            if bi == 0:
                cur_st = stg[n_s % 2]
                cur_stres = ("A", "stg", n_s % 2)
                n_s += 1
            queue.append((item, a[0], a[1], cur_st, cur_stres))
            if len(queue) >= PDEPTH:
                stage_b(*queue.pop(0))
            k_in_chunk += 1
            if k_in_chunk == PDEPTH and cc + 1 < NCH:
                loads(cc + 1)
        while queue:
            stage_b(*queue.pop(0))

    def _attn_out(self, ND, DEN, acst, w_out, x_src, x_dst, gi_post):
        S, T = self.S, self.T
        TT = 256
        self.new_phase()
        A = self.A
        wt = self.W.tile([128, NCH, D], BF16, "aw")
        self.load_w(wt, w_out, ("W", "aw"), "wl")
        sel = A.tile([16, NCH, 128], F32, "sel")
        S.dma(SP, "sell", lambda e: e.dma_start(out=sel[:].rearrange("p c m -> p (c m)"), in_=acst[0:16, 1152:2176]), writes=[("A", "sel")])
        nt = [[A.tile([128, NCH, TT], F32, f"n{g}{i}") for i in range(2)] for g in range(3)]
        dn = [[A.tile([16, TT], F32, f"d{g}{i}") for i in range(2)] for g in range(3)]
        ob = A.tile([128, NCH, TT], BF16, "ob")
        xt = [A.tile([128, NCH, TT], F32, f"x{i}") for i in range(2)]
        ft = A.tile([128, NCH, TT], F32, "f")
        sq = A.tile([128, NCH, TT], BF16, "sq")
        r2 = A.tile([128, TT], F32, "r2")
        tmp = [A.tile([128, TT], F32, f"t{i}") for i in range(2)]

        def loads(tj):
            s2 = tj % 2
            tb = tj * TT
            self.load_x(x_src, tb, TT, xt[s2], ("A", "x", s2), ("xl", s2))
            for g in range(3):
                dil = self.DILS[g]
                span = 128 * dil if dil > 1 else 512
                deps = [("SRC", ND[g].name, cc, tb // span, hh) for cc in range(NCH) for hh in range(2)]
                ddeps = [("SRC", DEN[g].name, cc, tb // span, hh) for cc in range(NCH) for hh in range(2)]
                src = ND[g][0].rearrange("(c p) t -> p c t", p=128)
                S.dma(SP, ("ndl", g, 0, s2), lambda e, tl=nt[g][s2], src=src, tb=tb: e.dma_start(out=tl[:], in_=src[:, :, tb:tb + TT]),
                      reads=deps, writes=[("A", "nd", g, 0, s2)])
                S.dma(SP, ("ndl", g, 1, s2), lambda e, tl=dn[g][s2], g=g, tb=tb: e.dma_start(out=tl[:], in_=DEN[g][:, tb:tb + TT]),
                      reads=ddeps, writes=[("A", "nd", g, 1, s2)])

        for ti in range(T // TT):
            t0 = ti * TT
            sl = ti % 2
            x, xres = xt[sl], ("A", "x", sl)
            if ti == 0:
                loads(0)
            if ti + 1 < T // TT:
                loads(ti + 1)
            n0, n1, n2 = nt[0][sl], nt[1][sl], nt[2][sl]
            d0, d1, d2 = dn[0][sl], dn[1][sl], dn[2][sl]
            S.op(POOL, lambda e, n0=n0, n1=n1: e.tensor_tensor(out=n0[:], in0=n0[:], in1=n1[:], op=ALU.add),
                 reads=[("A", "nd", 0, 0, sl), ("A", "nd", 1, 0, sl)], writes=[("A", "nd", 0, 0, sl)])
            S.op(POOL, lambda e, n0=n0, n2=n2: e.tensor_tensor(out=n0[:], in0=n0[:], in1=n2[:], op=ALU.add),
                 reads=[("A", "nd", 0, 0, sl), ("A", "nd", 2, 0, sl)], writes=[("A", "nd", 0, 0, sl)])
            S.op(DVE, lambda e, d0=d0, d1=d1: e.tensor_tensor(out=d0[:], in0=d0[:], in1=d1[:], op=ALU.add),
                 reads=[("A", "nd", 0, 1, sl), ("A", "nd", 1, 1, sl)], writes=[("A", "nd", 0, 1, sl)])
            S.op(DVE, lambda e, d0=d0, d2=d2: e.tensor_tensor(out=d0[:], in0=d0[:], in1=d2[:], op=ALU.add),
                 reads=[("A", "nd", 0, 1, sl), ("A", "nd", 2, 1, sl)], writes=[("A", "nd", 0, 1, sl)])
            S.op(DVE, lambda e, d0=d0: e.reciprocal(out=d0[:], in_=d0[:]), reads=[("A", "nd", 0, 1, sl)], writes=[("A", "nd", 0, 1, sl)])
            for c in range(NCH):
                bq = self.bank()
                S.op(PE, lambda e, bq=bq, c=c, d0=d0: e.matmul(self.ps[bq][:, :TT], lhsT=sel[:, c, :], rhs=d0[:], start=True, stop=True),
                     reads=[("A", "sel"), ("A", "nd", 0, 1, sl)], writes=[("ps", bq)])
                S.op(DVE, lambda e, bq=bq, c=c, n0=n0: e.tensor_tensor(out=ob[:, c, :], in0=self.ps[bq][:, :TT], in1=n0[:, c, :], op=ALU.mult),
                     reads=[("ps", bq), ("A", "nd", 0, 0, sl)], writes=[("A", "ob", c)])
            for co in range(NCH):
                b = self.bank()
                for kc in range(NCH):
                    S.op(PE, lambda e, b=b, co=co, kc=kc: e.matmul(
                        self.ps[b][:, :TT], lhsT=wt[:, kc, co * 128:(co + 1) * 128], rhs=ob[:, kc, :],
                        start=(kc == 0), stop=(kc == NCH - 1)),
                        reads=[("W", "aw", kc), ("A", "ob", kc)], writes=[("ps", b)], inc=(kc == NCH - 1))
                S.op(ACT, lambda e, b=b, co=co: e.activation(out=ft[:, co, :], in_=self.ps[b][:, :TT], func=AF.Copy),
                     reads=[("ps", b)], writes=[("A", "f", co)])
            self.postnorm_store(ft, ("A", "f"), x, xres, sq, ("A", "sq"), r2, ("A", "r2"),
                                tmp, ("A", "t"), gi_post, x_dst, t0, TT, False)

    def finish(self):
        self.S.final_wait(SP, self.out_toks)
        self.S.emit()
        return self.nc


SEQ = 4096
N_CORES = 8
DEPTH = 4


def _ret_tables(T):
    inv = (1.0 / (10000.0 ** np.linspace(0.0, 1.0, 128, dtype=np.float32))).astype(np.float32)
    ang = (np.arange(T, dtype=np.float32)[None, :] * inv[:, None]).astype(np.float32)
    cos, sin = np.cos(ang).astype(np.float32), np.sin(ang).astype(np.float32)
    rope = np.stack([cos, sin, cos / 16.0, sin / 16.0], axis=1).astype(np.float32)
    m = np.arange(128)[:, None]
    i = np.arange(128)[None, :]
    rc = np.zeros((128, 772), np.float32)
    rc[:, 0:128] = np.maximum(i - m, 0)
    rc[:, 128:256] = (i >= m)
    rc[:, 256:384] = np.maximum(m - i, 0)
    rc[:, 384:512] = (m > i)
    rc[:, 512:640] = i + 1
    rc[:, 640:768] = 128 - i
    rc[:, 768] = 127 - np.arange(128)
    rc[:, 769] = np.arange(128)
    rc[:, 770] = 128.0
    return np.ascontiguousarray(rope), rc


def _attn_tables(T):
    inv = (500000.0 ** (-np.arange(0, 16, 2, dtype=np.float32) / 16.0)).astype(np.float32)
    ropeA = np.zeros((3, 128, 2, T), np.float32)
    d = np.arange(128) % 64
    rot = d < 16
    for g, dil in enumerate(Prog.DILS):
        L = T // dil
        pos = np.arange(T)
        tok = (pos % L) * dil + pos // L
        ang = (tok.astype(np.float32)[None, :] * inv[:, None]).astype(np.float32)
        cos, sin = np.cos(ang).astype(np.float32), np.sin(ang).astype(np.float32)
        ropeA[g, :, 0, :] = 1.0
        ropeA[g, rot, 0, :] = cos[d[rot] % 8]
        ropeA[g, rot, 1, :] = sin[d[rot] % 8]
    ac = np.zeros((128, 2176), np.float32)
    for hsel in range(16):
        ac[hsel, 1152 + (hsel // 2) * 128 + (hsel % 2) * 64:1152 + (hsel // 2) * 128 + (hsel % 2) * 64 + 64] = 1.0
    for m in range(128):
        dd = m % 64
        if dd < 8:
            ac[m + 8, m] = -1.0
        elif dd < 16:
            ac[m - 8, m] = 1.0
    kk = np.arange(128)[:, None]
    qq = np.arange(128)[None, :]
    for var in range(4):
        v0 = (kk >= qq)
        v1 = (kk <= qq)
        if var in (1, 3):
            v0 = v0 & (kk >= 64)
        if var in (2, 3):
            v1 = v1 & (kk < 64)
        ac[:, 128 + var * 256:128 + var * 256 + 128] = np.where(v0, 0.0, -30000.0)
        ac[:, 128 + var * 256 + 128:128 + (var + 1) * 256] = np.where(v1, 0.0, -30000.0)
    return ropeA, ac


def build_program(T=SEQ, depth=DEPTH):
    P = Prog(T)
    xT = P.din("xT", [D, T])
    gT = P.din("gT", [128, 16 * NCH])
    cst = P.din("cst", [128, 128])
    ffn_w_in = P.din("ffn_w_in", [DEPTH, D, 2 * FFN_H])
    ffn_w_out = P.din("ffn_w_out", [DEPTH, FFN_H, D])
    ret_w_in = P.din("ret_w_in", [2, D, 6144])
    ret_w_out = P.din("ret_w_out", [2, 2048, D])
    dec = P.din("dec", [2, 128, 8])
    rope = P.din("rope", [128, 4, T])
    rcst = P.din("rcst", [128, 772])
    conv_w_in = P.din("conv_w_in", [D, 2 * D])
    conv_w_out = P.din("conv_w_out", [D, D])
    cvp = P.din("cvp", [128, 296])
    attn_w_in = P.din("attn_w_in", [D, 9216])
    attn_w_out = P.din("attn_w_out", [D, D])
    ropeA = P.din("ropeA", [3, 128, 2, T])
    acst = P.din("acst", [128, 2176])
    outT = P.dout("outT", [D, T])
    X = P.dscr("xres", [D, T], F32)
    P.consts(gT, cst)
    wsrc = []
    for i in range(depth):
        kind, j = i % 3, i // 3
        if kind == 0:
            wsrc.append((P.convert_w(f"ri{i}", ret_w_in[j], D, 6144), P.convert_w(f"ro{i}", ret_w_out[j], 2048, D)))
        elif kind == 1:
            wsrc.append((P.convert_w(f"ci{i}", conv_w_in, D, 2 * D), P.convert_w(f"co{i}", conv_w_out, D, D)))
        else:
            wsrc.append((P.convert_w(f"ai{i}", attn_w_in, D, 9216), P.convert_w(f"ao{i}", attn_w_out, D, D)))
        wsrc.append((P.convert_w(f"fi{i}", ffn_w_in[i], D, 2 * FFN_H), P.convert_w(f"fo{i}", ffn_w_out[i], FFN_H, D)))
    cur = xT
    for i in range(depth):
        kind, j = i % 3, i // 3
        wm, wf = wsrc[2 * i], wsrc[2 * i + 1]
        if kind == 0:
            P.ret_layer(cur, X, wm[0], wm[1], dec[j], rope, rcst, 4 * i + 0, 4 * i + 1, i)
        elif kind == 1:
            P.conv_layer(cur, X, wm[0], wm[1], cvp, 4 * i + 0, 4 * i + 1)
        else:
            P.attn_layer(cur, X, wm[0], wm[1], ropeA, acst, 4 * i + 0, 4 * i + 1)
        last = (i == depth - 1)
        P.ffn_layer(X, outT if last else X, wf[0], wf[1], 4 * i + 2, 4 * i + 3, final=last)
        cur = X
    return P.finish()


def _pl(v):
    return np.asarray(v, np.float32).reshape(-1, 128).T


def kernel(x, norm_w, ffn_w_in, ffn_w_out, ret_w_in, ret_log1m_decay, ret_w_out,
           conv_w_in, conv_b_in, conv_w_dw, conv_b_dw, conv_ln_g, conv_ln_b,
           conv_w_out, conv_b_out, attn_w_in, attn_w_out):
    f = lambda a: np.ascontiguousarray(np.asarray(a, dtype=np.float32))
    x = f(x)
    B, T, _ = x.shape
    nc = build_program(T)
    rope, rcst = _ret_tables(T)
    ropeA, acst = _attn_tables(T)
    norm_w = f(norm_w)
    gT = f(norm_w.reshape(16, NCH, 128).transpose(2, 0, 1).reshape(128, 16 * NCH))
    dec = f(np.broadcast_to(f(ret_log1m_decay).reshape(2, 1, 8), (2, 128, 8)))
    cvp = f(np.concatenate([_pl(f(conv_b_in)[0]), _pl(f(conv_b_dw)[0]), _pl(f(conv_ln_g)[0]), _pl(f(conv_ln_b)[0]),
                            _pl(f(conv_b_out)[0]),
                            f(conv_w_dw)[0].reshape(31, NCH, 128).transpose(2, 0, 1).reshape(128, 248)], axis=1))
    shared = {
        "gT": gT, "cst": np.eye(128, dtype=np.float32),
        "ffn_w_in": f(ffn_w_in), "ffn_w_out": f(ffn_w_out),
        "ret_w_in": f(ret_w_in), "ret_w_out": f(ret_w_out), "dec": dec, "rope": rope, "rcst": rcst,
        "conv_w_in": f(conv_w_in)[0], "conv_w_out": f(conv_w_out)[0], "cvp": cvp,
        "attn_w_in": f(attn_w_in)[0], "attn_w_out": f(attn_w_out)[0], "ropeA": ropeA, "acst": acst,
    }
    in_maps = []
    for b in range(B):
        m = dict(shared)
        m["xT"] = np.ascontiguousarray(x[b].T)
        in_maps.append(m)
    res = run_bass_kernel_spmd(nc, in_maps, core_ids=list(range(B)))
    out = np.stack([np.ascontiguousarray(r["outT"].T) for r in res.results], axis=0)
    return out.astype(np.float32)
```
